# Optimizing a Trainium2 kernel written in Bass

```python
import math
import jax, jax.numpy as jnp
from jax import lax
import numpy as np

D_MODEL = 1024
BATCH = 4
SEQ = 8192
DEPTH = 2

N_A_LAYERS = DEPTH // 2
N_B_LAYERS = DEPTH - N_A_LAYERS

GDN_HEADS = 8
GDN_DK = 128
GDN_DV = 128
GDN_KDIM = GDN_HEADS * GDN_DK
GDN_VDIM = GDN_HEADS * GDN_DV
QKV_DIM = 2 * GDN_KDIM + GDN_VDIM
GDN_PROJ_DIM = QKV_DIM + GDN_VDIM + 2 * GDN_HEADS
CONV_WIDTH = 4
CHUNK = 64

SB_HEADS = 8
SB_DH = D_MODEL // SB_HEADS
SB_DIM = SB_HEADS * SB_DH
Q_BLOCK = 128

D_FF = -(-8 * D_MODEL // (3 * 256)) * 256

EPS = 1e-6

kernel_name = "yoco_gated_deltanet_stick_breaking"


def rms_norm(x, gain):
    xf = x.astype(jnp.float32)
    y = xf * lax.rsqrt(jnp.mean(xf * xf, axis=-1, keepdims=True) + EPS)
    return (y * gain.astype(jnp.float32)).astype(x.dtype)


def l2_norm(x):
    xf = x.astype(jnp.float32)
    return xf * lax.rsqrt(jnp.sum(xf * xf, axis=-1, keepdims=True) + EPS)


def causal_depthwise_conv(x, w):
    c = x.shape[-1]
    return lax.conv_general_dilated(
        x, w[:, None, :].astype(x.dtype), window_strides=(1,),
        padding=[(CONV_WIDTH - 1, 0)], dimension_numbers=('NWC', 'WIO', 'NWC'),
        feature_group_count=c)


def swiglu(x, w_gu, w_down):
    g, u = jnp.split(x @ w_gu, 2, axis=-1)
    return (jax.nn.silu(g) * u) @ w_down


def gated_delta_rule(q, k, v, beta, g):
    b_, h_, s_, dk = q.shape
    dv = v.shape[-1]
    n = s_ // CHUNK
    q = (q * dk ** -0.5).reshape(b_, h_, n, CHUNK, dk)
    k = k.reshape(b_, h_, n, CHUNK, dk)
    v = v.reshape(b_, h_, n, CHUNK, dv)
    beta = beta.reshape(b_, h_, n, CHUNK)
    g_cum = jnp.cumsum(g.reshape(b_, h_, n, CHUNK), axis=-1)

    idx = jnp.arange(CHUNK)
    incl = idx[:, None] >= idx[None, :]
    strict = idx[:, None] > idx[None, :]
    decay = jnp.exp(jnp.where(incl, g_cum[..., :, None] - g_cum[..., None, :], -jnp.inf))

    k_beta = k * beta[..., None]
    a_low = jnp.where(strict, jnp.einsum('bhncd,bhnsd->bhncs', k_beta, k) * decay, 0.0)
    eye = jnp.eye(CHUNK, dtype=q.dtype)
    rhs = jnp.concatenate([v * beta[..., None], k_beta * jnp.exp(g_cum)[..., None]], axis=-1)
    sol = lax.linalg.triangular_solve(eye + a_low, rhs, left_side=True, lower=True,
                                      unit_diagonal=True)
    u, w = sol[..., :dv], sol[..., dv:]

    attn_intra = jnp.einsum('bhncd,bhnsd->bhncs', q, k) * decay
    g_last = g_cum[..., -1]
    k_decay = k * jnp.exp(g_last[..., None] - g_cum)[..., None]
    q_decay = q * jnp.exp(g_cum)[..., None]

    def step(state, inp):
        qd, kd, ui, wi, ai, gl = inp
        v_new = ui - jnp.einsum('bhck,bhkv->bhcv', wi, state)
        o = (jnp.einsum('bhck,bhkv->bhcv', qd, state)
             + jnp.einsum('bhcs,bhsv->bhcv', ai, v_new))
        state = state * jnp.exp(gl)[..., None, None] + jnp.einsum('bhck,bhcv->bhkv', kd, v_new)
        return state, o

    xs = tuple(jnp.moveaxis(t, 2, 0) for t in (q_decay, k_decay, u, w, attn_intra, g_last))
    state0 = jnp.zeros((b_, h_, dk, dv), q.dtype)
    _, o = lax.scan(step, state0, xs)
    return jnp.moveaxis(o, 0, 2).reshape(b_, h_, s_, dv)


def gdn_mixer(hn, w_in, conv_w, a_log, dt_bias, o_gain, w_out):
    b_, s_, _ = hn.shape
    f32 = jnp.float32
    proj = hn @ w_in
    qkv, z, b_raw, a_raw = jnp.split(
        proj, [QKV_DIM, QKV_DIM + GDN_VDIM, QKV_DIM + GDN_VDIM + GDN_HEADS], axis=-1)
    qkv = jax.nn.silu(causal_depthwise_conv(qkv, conv_w))
    q, k, v = jnp.split(qkv, [GDN_KDIM, 2 * GDN_KDIM], axis=-1)

    def heads(t, d):
        return t.reshape(b_, s_, GDN_HEADS, d).transpose(0, 2, 1, 3)

    q = l2_norm(heads(q, GDN_DK))
    k = l2_norm(heads(k, GDN_DK))
    v = heads(v, GDN_DV).astype(f32)
    beta = jax.nn.sigmoid(b_raw.astype(f32)).transpose(0, 2, 1)
    g = (-jnp.exp(a_log.astype(f32))
         * jax.nn.softplus(a_raw.astype(f32) + dt_bias.astype(f32))).transpose(0, 2, 1)
    o = gated_delta_rule(q, k, v, beta, g).transpose(0, 2, 1, 3)
    zf = z.reshape(b_, s_, GDN_HEADS, GDN_DV).astype(f32)
    o = rms_norm(o, o_gain) * jax.nn.silu(zf)
    return o.reshape(b_, s_, GDN_VDIM).astype(hn.dtype) @ w_out


def shared_kv(h, kv_norm, w_kv, k_gain):
    b_, s_, _ = h.shape
    k, v = jnp.split(rms_norm(h, kv_norm) @ w_kv, 2, axis=-1)
    k = rms_norm(k.reshape(b_, s_, SB_HEADS, SB_DH), k_gain).transpose(0, 2, 1, 3)
    v = v.reshape(b_, s_, SB_HEADS, SB_DH).transpose(0, 2, 1, 3)
    return k, v


def stick_breaking_attention(q, k, v):
    _, _, s_, dh = q.shape
    f32 = jnp.float32
    scale = dh ** -0.5
    outs = []
    for blk in range(s_ // Q_BLOCK):
        t0 = blk * Q_BLOCK
        end = t0 + Q_BLOCK
        qb = q[:, :, t0:end].astype(f32)
        kb = k[:, :, :end].astype(f32)
        vb = v[:, :, :end].astype(f32)
        z = jnp.einsum('bhtd,bhsd->bhts', qb, kb) * scale
        t_idx = t0 + jnp.arange(Q_BLOCK)
        s_idx = jnp.arange(end)
        causal = s_idx[None, :] < t_idx[:, None]
        ls_neg = jax.nn.log_sigmoid(-z)
        log_fail = jnp.where(causal, ls_neg, 0.0)
        log_rest = lax.cumsum(log_fail, axis=3, reverse=True) - log_fail
        log_a = z + ls_neg + log_rest
        att = jnp.where(causal, jnp.exp(log_a), 0.0)
        outs.append(jnp.einsum('bhts,bhsd->bhtd', att, vb))
    return jnp.concatenate(outs, axis=2).astype(q.dtype)


def sb_mixer(hn, k_sh, v_sh, w_q, q_gain, w_out):
    b_, s_, _ = hn.shape
    q = rms_norm((hn @ w_q).reshape(b_, s_, SB_HEADS, SB_DH), q_gain).transpose(0, 2, 1, 3)
    o = stick_breaking_attention(q, k_sh, v_sh)
    return o.transpose(0, 2, 1, 3).reshape(b_, s_, SB_DIM).astype(hn.dtype) @ w_out


def setup_inputs(seed: int = 0) -> dict:
    key = jax.random.key(seed)
    ks = jax.random.split(key, 17)
    f32 = jnp.float32

    def dense(k, shape, fan_in, scale=1.0):
        return jax.random.normal(k, shape, f32) * (scale * fan_in ** -0.5)

    def gain(k, shape):
        return 1.0 + 0.02 * jax.random.normal(k, shape, f32)

    out_scale = (2 * DEPTH) ** -0.5
    x = jax.random.normal(ks[0], (BATCH, SEQ, D_MODEL), f32)
    attn_norm = gain(ks[1], (DEPTH, D_MODEL))
    ffn_norm = gain(ks[2], (DEPTH, D_MODEL))
    gdn_w_in = dense(ks[3], (N_A_LAYERS, D_MODEL, GDN_PROJ_DIM), D_MODEL)
    gdn_conv_w = jax.random.normal(ks[4], (N_A_LAYERS, CONV_WIDTH, QKV_DIM), f32) * CONV_WIDTH ** -0.5
    gdn_a_log = jnp.log(jax.random.uniform(ks[5], (N_A_LAYERS, GDN_HEADS), f32, 1.0, 16.0))
    dt = jnp.exp(jax.random.uniform(ks[6], (N_A_LAYERS, GDN_HEADS), f32,
                                    math.log(1e-3), math.log(1e-1)))
    gdn_dt_bias = dt + jnp.log(-jnp.expm1(-dt))
    gdn_o_gain = gain(ks[7], (N_A_LAYERS, GDN_DV))
    gdn_w_out = dense(ks[8], (N_A_LAYERS, GDN_VDIM, D_MODEL), GDN_VDIM, out_scale)
    kv_norm = gain(ks[9], (D_MODEL,))
    w_kv = dense(ks[10], (D_MODEL, 2 * SB_DIM), D_MODEL)
    k_gain = gain(ks[11], (SB_DH,))
    sb_w_q = dense(ks[12], (N_B_LAYERS, D_MODEL, SB_DIM), D_MODEL)
    sb_q_gain = gain(ks[13], (N_B_LAYERS, SB_DH))
    sb_w_out = dense(ks[14], (N_B_LAYERS, SB_DIM, D_MODEL), SB_DIM, out_scale)
    ffn_w_gu = dense(ks[15], (DEPTH, D_MODEL, 2 * D_FF), D_MODEL)
    ffn_w_down = dense(ks[16], (DEPTH, D_FF, D_MODEL), D_FF, out_scale)
    return {"x": x, "attn_norm": attn_norm, "ffn_norm": ffn_norm,
            "gdn_w_in": gdn_w_in, "gdn_conv_w": gdn_conv_w, "gdn_a_log": gdn_a_log,
            "gdn_dt_bias": gdn_dt_bias, "gdn_o_gain": gdn_o_gain, "gdn_w_out": gdn_w_out,
            "kv_norm": kv_norm, "w_kv": w_kv, "k_gain": k_gain,
            "sb_w_q": sb_w_q, "sb_q_gain": sb_q_gain, "sb_w_out": sb_w_out,
            "ffn_w_gu": ffn_w_gu, "ffn_w_down": ffn_w_down}


def reference(x, attn_norm, ffn_norm, gdn_w_in, gdn_conv_w, gdn_a_log, gdn_dt_bias,
              gdn_o_gain, gdn_w_out, kv_norm, w_kv, k_gain, sb_w_q, sb_q_gain, sb_w_out,
              ffn_w_gu, ffn_w_down):
    h = x
    k_sh = None
    v_sh = None
    for layer in range(DEPTH):
        hn = rms_norm(h, attn_norm[layer])
        if layer < N_A_LAYERS:
            i = layer
            h = h + gdn_mixer(hn, gdn_w_in[i], gdn_conv_w[i], gdn_a_log[i], gdn_dt_bias[i],
                              gdn_o_gain[i], gdn_w_out[i])
        else:
            if layer == N_A_LAYERS:
                k_sh, v_sh = shared_kv(h, kv_norm, w_kv, k_gain)
            j = layer - N_A_LAYERS
            h = h + sb_mixer(hn, k_sh, v_sh, sb_w_q[j], sb_q_gain[j], sb_w_out[j])
        h = h + swiglu(rms_norm(h, ffn_norm[layer]), ffn_w_gu[layer], ffn_w_down[layer])
    return h
```

```python
from contextlib import ExitStack
import numpy as np
import concourse.bass as bass
import concourse.mybir as mybir

F32 = mybir.dt.float32
BF16 = mybir.dt.bfloat16
ALU = mybir.AluOpType
AF = mybir.ActivationFunctionType
AX = mybir.AxisListType

SAME_ENGINE_SYNC = True
N_DMA_SEMS = 6


class Buf:
    def __init__(self, t, name):
        self.t = t
        self.name = name
        self.reads = {}
        self.writes = {}

    def __getitem__(self, idx):
        return View(self, self.t[idx])


class View:
    def __init__(self, buf, ap):
        self.buf = buf
        self.ap = ap


def _ap(x):
    return x.ap if isinstance(x, View) else x


class Eng:
    def __init__(self, name):
        self.name = name
        self.ops = []
        self.trace = []
        self.count = 0
        self.waited = {}
        self.sem = None
        self.dma_sems = []
        self.dma_vals = []
        self.dma_k = 0


class K:
    def __init__(self, nc):
        self.nc = nc
        self.es = ExitStack()
        self.engs = {n: Eng(n) for n in ("pe", "act", "dve", "pool", "sp")}
        self.sems = {}
        for n, e in self.engs.items():
            e.sem = self.es.enter_context(nc.semaphore("s_" + n))
            self.sems[n] = e.sem
        for n in ("sp", "pool", "act"):
            e = self.engs[n]
            for i in range(N_DMA_SEMS):
                s = self.es.enter_context(nc.semaphore("d_%s%d" % (n, i)))
                key = "d_%s%d" % (n, i)
                self.sems[key] = s
                e.dma_sems.append(key)
                e.dma_vals.append(0)
        self.nbuf = 0

    def sbuf(self, shape, dtype, name=None):
        self.nbuf += 1
        name = name or "sb%d" % self.nbuf
        t = self.es.enter_context(self.nc.sbuf_tensor(name, list(shape), dtype))
        return Buf(t, name)

    def psum(self, shape, dtype, name=None):
        self.nbuf += 1
        name = name or "ps%d" % self.nbuf
        t = self.es.enter_context(self.nc.psum_tensor(name, list(shape), dtype))
        return Buf(t, name)

    def dram(self, name, shape, dtype, kind="Internal"):
        t = self.nc.dram_tensor(name, list(shape), dtype, kind=kind)
        return Buf(t.ap(), name)

    def track(self, ap, name):
        return Buf(ap, name)

    def _wait(self, e, key, val, war=False):
        if key == e.name and (e.name == "pe" or war or not SAME_ENGINE_SYNC):
            return
        if e.waited.get(key, 0) >= val:
            return
        e.waited[key] = val
        sem = self.sems[key]
        e.ops.append(("wait", key, val))
        e.trace.append(("w", key, val))

    def _deps(self, e, reads, writes, nowaw=False):
        for v in reads:
            if isinstance(v, View):
                for k, val in v.buf.writes.items():
                    self._wait(e, k, val)
        for v in writes:
            if isinstance(v, View):
                for k, val in v.buf.reads.items():
                    self._wait(e, k, val, war=True)
                if not nowaw:
                    for k, val in v.buf.writes.items():
                        self._wait(e, k, val)

    def _mark(self, key, val, reads, writes):
        for v in reads:
            if isinstance(v, View):
                if v.buf.reads.get(key, 0) < val:
                    v.buf.reads[key] = val
        for v in writes:
            if isinstance(v, View):
                if v.buf.writes.get(key, 0) < val:
                    v.buf.writes[key] = val

    def op(self, eng, fn, reads, writes, nowaw=False):
        e = self.engs[eng]
        self._deps(e, reads, writes, nowaw)
        e.count += 1
        sem = e.sem
        e.ops.append(("op", fn, e.count))
        e.trace.append(("i", e.name, 1))
        self._mark(e.name, e.count, reads, writes)

    def dma(self, eng, out, in_, nowaw=True, **kw):
        e = self.engs[eng]
        self._deps(e, [in_], [out], nowaw)
        i = e.dma_k % N_DMA_SEMS
        e.dma_k += 1
        key = e.dma_sems[i]
        if e.dma_vals[i] > 0:
            self._wait(e, key, e.dma_vals[i])
        e.dma_vals[i] += 16
        val = e.dma_vals[i]
        sem = self.sems[key]
        o, s = _ap(out), _ap(in_)
        e.ops.append(("dma", o, s, key, kw))
        e.trace.append(("i", key, 16))
        self._mark(key, val, [in_], [out])
        return (key, val)

    def allgather(self, out, in_, groups):
        e = self.engs["pool"]
        self._deps(e, [in_], [out], False)
        i = e.dma_k % N_DMA_SEMS
        e.dma_k += 1
        key = e.dma_sems[i]
        if e.dma_vals[i] > 0:
            self._wait(e, key, e.dma_vals[i])
        e.dma_vals[i] += 16
        val = e.dma_vals[i]
        o, s_ = _ap(out), _ap(in_)
        e.ops.append(("cc", o, s_, key, groups))
        e.trace.append(("i", key, 16))
        self._mark(key, val, [in_], [out])

    def final_wait(self, eng, bufs):
        e = self.engs[eng]
        for b in bufs:
            for k, val in b.writes.items():
                self._wait(e, k, val)

    def matmul(self, out, lhsT, rhs, start=True, stop=True, **kw):
        o, l, r = _ap(out), _ap(lhsT), _ap(rhs)
        self.op("pe", lambda h: h.matmul(o, l, r, start=start, stop=stop, **kw), [lhsT, rhs], [out])

    def transpose(self, out, in_, ident):
        o, i, d = _ap(out), _ap(in_), _ap(ident)
        self.op("pe", lambda h: h.transpose(o, i, d), [in_, ident], [out])

    def act(self, out, in_, func, bias=None, scale=None, accum_out=None, eng="act"):
        o, i = _ap(out), _ap(in_)
        kw = {}
        rd = [in_]
        wr = [out]
        if bias is not None:
            kw["bias"] = _ap(bias)
            rd.append(bias)
        if scale is not None:
            kw["scale"] = _ap(scale)
            rd.append(scale)
        if accum_out is not None:
            kw["accum_out"] = _ap(accum_out)
            wr.append(accum_out)
        self.op("act", lambda h: h.activation(o, i, func, **kw), rd, wr)

    def tt(self, eng, out, in0, in1, op):
        o, a, b = _ap(out), _ap(in0), _ap(in1)
        self.op(eng, lambda h: h.tensor_tensor(o, a, b, op), [in0, in1], [out])

    def ts(self, eng, out, in0, s1, op0, s2=None, op1=None, accum_out=None):
        o, a = _ap(out), _ap(in0)
        rd = [in0]
        wr = [out]
        if isinstance(s1, View):
            rd.append(s1)
        if isinstance(s2, View):
            rd.append(s2)
        kw = {}
        if op1 is not None:
            kw["op1"] = op1
        if accum_out is not None:
            kw["accum_out"] = _ap(accum_out)
            wr.append(accum_out)
        x1, x2 = _ap(s1), _ap(s2)
        self.op(eng, lambda h: h.tensor_scalar(o, a, x1, x2, op0, **kw), rd, wr)

    def stt(self, out, in0, scalar, in1, op0, op1, eng="dve"):
        o, a, b = _ap(out), _ap(in0), _ap(in1)
        rd = [in0, in1]
        if isinstance(scalar, View):
            rd.append(scalar)
        s = _ap(scalar)
        self.op(eng, lambda h: h.scalar_tensor_tensor(o, a, s, b, op0, op1), rd, [out])

    def copy(self, eng, out, in_):
        o, i = _ap(out), _ap(in_)
        if eng == "act":
            self.op("act", lambda h: h.copy(o, i), [in_], [out])
        else:
            self.op(eng, lambda h: h.tensor_copy(o, i), [in_], [out])

    def memset(self, eng, out, val):
        o = _ap(out)
        self.op(eng, lambda h: h.memset(o, val), [], [out])

    def recip(self, out, in_):
        o, i = _ap(out), _ap(in_)
        self.op("dve", lambda h: h.reciprocal(o, i), [in_], [out])

    def affine_select(self, out, in_, pattern, cmp, fill, base, cm):
        o, i = _ap(out), _ap(in_)
        self.op("pool", lambda h: h.affine_select(o, i, pattern, cmp, fill, base=base, channel_multiplier=cm),
                [in_], [out])

    def check_deadlock(self):
        vals = {}
        pos = {n: 0 for n in self.engs}
        progress = True
        while progress:
            progress = False
            for n, e in self.engs.items():
                while pos[n] < len(e.trace):
                    kind, key, v = e.trace[pos[n]]
                    if kind == "w":
                        if vals.get(key, 0) >= v:
                            pos[n] += 1
                            progress = True
                        else:
                            break
                    else:
                        vals[key] = vals.get(key, 0) + v
                        pos[n] += 1
                        progress = True
        stuck = {n: (pos[n], len(e.trace), e.trace[pos[n]] if pos[n] < len(e.trace) else None)
                 for n, e in self.engs.items()}
        ok = all(pos[n] == len(e.trace) for n, e in self.engs.items())
        return ok, stuck, vals

    def emit(self):
        ok, stuck, vals = self.check_deadlock()
        if not ok:
            raise RuntimeError("DEADLOCK in sync graph: %s" % (stuck,))
        nc = self.nc
        import bisect
        sig = {n: set() for n in self.engs}
        for n, e in self.engs.items():
            for it in e.ops:
                if it[0] == "wait" and it[1] in sig:
                    sig[it[1]].add(it[2])
        sigl = {n: sorted(v) for n, v in sig.items()}
        sems = self.sems

        def run(n, h):
            e = self.engs[n]
            for it in e.ops:
                if it[0] == "wait":
                    key, val = it[1], it[2]
                    if key in sigl:
                        val = bisect.bisect_right(sigl[key], val)
                    h.wait_ge(sems[key], val)
                elif it[0] == "op":
                    ins = it[1](h)
                    if it[2] in sig[n]:
                        ins.then_inc(e.sem, 1)
                elif it[0] == "cc":
                    _, o, s_, key, groups = it
                    h.collective_compute("AllGather", ALU.bypass, replica_groups=groups, ins=[s_], outs=[o]).then_inc(sems[key], 16)
                else:
                    _, o, s_, key, kw = it
                    h.dma_start(out=o, in_=s_, **kw).then_inc(sems[key], 16)
        self.n_inc = {n: len(v) for n, v in sig.items()}
        with nc.Block() as block:
            @block.tensor
            def _(h):
                run("pe", h)

            @block.scalar
            def _(h):
                run("act", h)

            @block.vector
            def _(h):
                run("dve", h)

            @block.gpsimd
            def _(h):
                run("pool", h)

            @block.sync
            def _(h):
                run("sp", h)
        self.es.close()


D = 1024
H = 8
DH = 128
DFF = 2816
PROJ = 4112
EPS = 1e-6
NEG = -30000.0


class M:
    pass


def host_consts():
    r = np.arange(128)[:, None]
    c = np.arange(128)[None, :]
    mats = [r == c, np.ones((128, 128)), r <= c, r > c, c >= r]
    b = 1
    while b < 128:
        mats.append(((r // b) == (c // b) + 1) & ((r // b) % 2 == 1))
        b *= 2
    cf = np.stack([np.asarray(x, np.float32) for x in mats], axis=1)
    import ml_dtypes
    bmats = [r == c, np.ones((128, 128)), -(r >= c).astype(np.float32)]
    cb = np.stack([np.asarray(x, np.float32) for x in bmats], axis=1).astype(ml_dtypes.bfloat16)
    return np.ascontiguousarray(cf), np.ascontiguousarray(cb)


def setup_consts(k, m):
    cfd = k.dram("cf", [128, 12, 128], F32, kind="ExternalInput")
    cbd = k.dram("cb", [128, 3, 128], BF16, kind="ExternalInput")
    cf = k.sbuf([128, 12, 128], F32, "cfs")
    cb = k.sbuf([128, 3, 128], BF16, "cbs")
    k.dma("sp", cf[:], cfd[:])
    k.dma("sp", cb[:], cbd[:])
    m.cf, m.cb = cf, cb

    class V:
        def __init__(self, buf, j):
            self.buf, self.j = buf, j

        def __getitem__(self, idx):
            if idx == slice(None):
                return self.buf[:, self.j, :]
            a, b = idx
            return self.buf[a, self.j, b]
    m.ident = V(cf, 0)
    m.ones = V(cf, 1)
    m.triI = V(cf, 2)
    m.lmask = V(cf, 3)
    m.umask = V(cf, 4)
    m.lvl = [V(cf, 5 + i) for i in range(7)]
    m.identb = V(cb, 0)
    m.onesb = V(cb, 1)
    m.ntriS = V(cb, 2)


class Arena:
    def __init__(self, k, nbytes):
        self.k = k
        self.n32 = nbytes // 4
        self.base = k.es.enter_context(k.nc.sbuf_tensor("arena", [128, self.n32], F32))
        self.off = 0
        self.cnt = 0

    def reset(self):
        self.off = 0

    def alloc(self, shape, dtype, name=None):
        nel = int(np.prod(shape[1:]))
        nb = nel * (2 if dtype == BF16 else 4)
        n32 = (nb + 3) // 4
        n32 = (n32 + 7) // 8 * 8
        assert self.off + n32 <= self.n32, "arena overflow %d + %d > %d" % (self.off, n32, self.n32)
        ap = self.base[0:shape[0], self.off:self.off + n32]
        self.off += n32
        if dtype != F32:
            ap = ap.bitcast(dtype)
        ap = ap[:, 0:nel]
        if len(shape) == 3:
            ap = ap.rearrange("p (a b) -> p a b", a=shape[1])
        elif len(shape) == 4:
            ap = ap.rearrange("p (a b c) -> p a b c", a=shape[1], b=shape[2])
        self.cnt += 1
        return Buf(ap, name or "ar%d" % self.cnt)


def barrier(k):
    latest = {}
    for n, e in k.engs.items():
        if e.count:
            latest[n] = e.count
        for key, v in zip(e.dma_sems, e.dma_vals):
            if v:
                latest[key] = v
    for n, e in k.engs.items():
        for key, v in latest.items():
            if key == n:
                continue
            k._wait(e, key, v)


def new_prog():
    nc = bass.Bass("TRN2", target_bir_lowering=False)
    k = K(nc)
    m = M()
    m.ins = {}
    m.outs = {}
    setup_consts(k, m)
    m.pb = [k.psum([128, 512], F32, "pb%d" % i) for i in range(8)]
    m.pwi = 0

    def pw():
        b = m.pb[m.pwi % 8]
        m.pwi += 1
        return b
    m.pw = pw
    m.arena = Arena(k, 196 * 1024)
    return nc, k, m


def din(k, m, name, shape, dt=F32):
    b = k.dram(name, shape, dt, kind="ExternalInput")
    m.ins[name] = b
    return b


def dout(k, m, name, shape, dt=F32):
    b = k.dram(name, shape, dt, kind="ExternalOutput")
    m.outs[name] = b
    return b


def load_cols(k, m, rows):
    al = m.arena.alloc
    st = al([128, 128], F32)
    out = al([128, 128], F32)
    k.memset("dve", st[:], 0.0)
    r0 = 0
    for v, n in rows:
        k.dma("sp", st[r0:r0 + n, :], v, nowaw=False)
        r0 += n
    p = m.pw()
    k.transpose(p[:, 0:128], st[:], m.ident[:])
    k.copy("dve", out[:], p[:, 0:128])
    return out


def load_w(k, dst, src, kchunks):
    for kc in range(kchunks):
        k.dma("pool", dst[:, kc, :], src(kc), max_dma_last_dim=4096)


def rstd_of(k, junk, ss, src, n):
    k.act(junk[:, 0:n], src, AF.Square, accum_out=ss[:])
    k.ts("dve", ss[:], ss[:], 1.0 / n, ALU.mult, EPS, ALU.add)
    k.act(ss[:], ss[:], AF.Ln)
    k.act(ss[:], ss[:], AF.Exp, scale=-0.5)


def norm_T(k, m, xs, gains, outs):
    for half in range(2):
        p = m.pw()
        for c4 in range(4):
            c = half * 4 + c4
            k.transpose(p[:, c4 * 128:(c4 + 1) * 128], xs[:, c * 128:(c + 1) * 128], m.ident[:])
        for c4 in range(4):
            c = half * 4 + c4
            for g, o in zip(gains, outs):
                k.ts("dve", o[:, c, :], p[:, c4 * 128:(c4 + 1) * 128], g(c), ALU.mult)


D = 1024
H = 8
DFF = 2816
EPS = 1e-6


def build_dense_ffn(TOK):
    nc, k, m = new_prog()
    T = dict(x=din(k, m, "x", [TOK, D]), aT=din(k, m, "aT", [8, 128, TOK], BF16), w_o=din(k, m, "w_o", [D, D]),
             fnorm=din(k, m, "fnorm", [8, 128]), w_gu=din(k, m, "w_gu", [D, 2 * DFF]), w_d=din(k, m, "w_d", [DFF, D]),
             h=dout(k, m, "h", [TOK, D]))
    emit_dense_ffn(k, m, TOK, T)
    k.final_wait("sp", [T["h"]])
    k.emit()
    return nc


def emit_dense_ffn(k, m, TOK, T, xload=None):
    m.arena.reset()
    al = m.arena.alloc
    d_x, d_aT, d_wo, d_fn, d_wgu, d_wd, d_h = T["x"], T["aT"], T["w_o"], T["fnorm"], T["w_gu"], T["w_d"], T["h"]
    cols = load_cols(k, m, [(d_fn[:], 8)])
    w_o = al([128, 8, D], BF16, "w_o")
    w_gu = al([128, 8, 2 * DFF], BF16, "w_gu")
    w_d = al([128, 22, D], BF16, "w_d")
    load_w(k, w_o, lambda kc: d_wo[kc * 128:(kc + 1) * 128, :], 8)
    load_w(k, w_gu, lambda kc: d_wgu[kc * 128:(kc + 1) * 128, :], 8)
    load_w(k, w_d, lambda kc: d_wd[kc * 128:(kc + 1) * 128, :], 22)
    junk = al([128, 1024], BF16)
    ss = al([128, 1], F32)
    xt = [al([128, D], F32, "xt%d" % i) for i in range(1 if xload is not None else 2)]
    at = [al([128, 8, 128], BF16, "at%d" % i) for i in range(2)]
    h1 = [al([128, D], F32, "h1_%d" % i) for i in range(2)]
    h2 = [al([128, D], F32, "h2_%d" % i) for i in range(1 if xload is not None else 2)]
    xtmp = [al([128, D], F32, "xtmp%d" % i) for i in range(2)] if xload is not None else None
    xs = al([128, D], F32, "xs")
    hnT = al([128, 8, 128], BF16, "hnT")
    hidF = al([128, 22 * 128], BF16, "hidT")
    hidT = Buf(hidF.t.rearrange("p (a b) -> p a b", a=22), "hidT3")
    hidT.reads, hidT.writes = hidF.reads, hidF.writes
    sg = [al([128, 256], F32, "sg%d" % i) for i in range(2)]
    aTv = k.track(d_aT.t.rearrange("h p t -> p h t"), "aTv")
    for ti in range(TOK // 128):
        x_t, a_t, h1_, h2_ = xt[ti % len(xt)], at[ti % 2], h1[ti % 2], h2[ti % len(h2)]
        if xload is None:
            k.dma("sp", x_t[:], d_x[ti * 128:(ti + 1) * 128, :])
        else:
            xload(ti, x_t, xtmp)
        k.dma("sp", a_t[:], aTv[:, :, ti * 128:(ti + 1) * 128])
        for half in range(2):
            py = m.pw()
            for h in range(8):
                k.matmul(py[:], a_t[:, h, :], w_o[:, h, half * 512:(half + 1) * 512], start=(h == 0), stop=(h == 7))
            k.tt("dve", h1_[:, half * 512:(half + 1) * 512], py[:], x_t[:, half * 512:(half + 1) * 512], ALU.add)
        rstd_of(k, junk, ss, h1_[:], D)
        k.ts("dve", xs[:], h1_[:], ss[:], ALU.mult)
        norm_T(k, m, xs, [lambda c: cols[:, c:c + 1]], [hnT])
        for j2 in range(11):
            p = m.pw()
            for q in range(4):
                j = j2 * 2 + (q % 2)
                col = (0 if q < 2 else DFF) + j * 128
                for kc in range(8):
                    k.matmul(p[:, q * 128:(q + 1) * 128], w_gu[:, kc, col:col + 128], hnT[:, kc, :],
                             start=(kc == 0), stop=(kc == 7))
            s_ = sg[j2 % 2]
            k.act(s_[:], p[:, 0:256], AF.Silu)
            k.tt("dve", hidF[:, j2 * 256:(j2 + 1) * 256], p[:, 256:512], s_[:], ALU.mult)
        for half in range(2):
            py = m.pw()
            for j in range(22):
                k.matmul(py[:], hidT[:, j, :], w_d[:, j, half * 512:(half + 1) * 512], start=(j == 0), stop=(j == 21))
            k.tt("dve", h2_[:, half * 512:(half + 1) * 512], py[:], h1_[:, half * 512:(half + 1) * 512], ALU.add)
        k.dma("sp", d_h[ti * 128:(ti + 1) * 128, :], h2_[:])


def build_kvq(TOK):
    nc, k, m = new_prog()
    T = dict(h=din(k, m, "h", [TOK, D]), kvn=din(k, m, "kvn", [8, 128]), an=din(k, m, "an", [8, 128]),
             kg=din(k, m, "kg", [1, 128]), qg=din(k, m, "qg", [1, 128]), w_kv=din(k, m, "w_kv", [D, 2 * D]),
             w_q=din(k, m, "w_q", [D, D]), qT=dout(k, m, "qT", [8, 128, TOK], BF16),
             kT=dout(k, m, "kT", [8, 128, TOK], BF16), v=dout(k, m, "v", [TOK, D], BF16))
    outs = emit_kvq(k, m, TOK, T)
    k.final_wait("sp", outs)
    k.emit()
    return nc


def emit_kvq(k, m, TOK, T):
    m.arena.reset()
    al = m.arena.alloc
    d_h, d_kvn, d_an, d_kg, d_qg, d_wkv, d_wq = T["h"], T["kvn"], T["an"], T["kg"], T["qg"], T["w_kv"], T["w_q"]
    d_qT, d_kT, d_v = T["qT"], T["kT"], T["v"]
    cols = load_cols(k, m, [(d_kvn[:], 8), (d_an[:], 8), (d_kg[:], 1), (d_qg[:], 1)])
    w_kv = al([128, 8, 2 * D], BF16, "w_kv")
    w_q = al([128, 8, D], BF16, "w_q")
    load_w(k, w_kv, lambda kc: d_wkv[kc * 128:(kc + 1) * 128, :], 8)
    load_w(k, w_q, lambda kc: d_wq[kc * 128:(kc + 1) * 128, :], 8)
    junk = al([128, 1024], BF16)
    ss = al([128, 1], F32)
    ht = [al([128, D], F32, "ht%d" % i) for i in range(2)]
    xs = al([128, D], F32, "xs")
    xkT = al([128, 8, 128], BF16, "xkT")
    xqT = al([128, 8, 128], BF16, "xqT")
    vt = [al([128, D], BF16, "vt%d" % i) for i in range(2)]
    ss8 = {n: al([128, 8], F32, "ss8" + n) for n in "kq"}
    nrm = {n: al([128, D], F32, "nrm" + n) for n in "kq"}
    oT = {n: [al([128, 8 * 128], BF16, "oT%s%d" % (n, i)) for i in range(2)] for n in "kq"}
    qTv = k.track(d_qT.t.rearrange("h p t -> p h t"), "qTv")
    kTv = k.track(d_kT.t.rearrange("h p t -> p h t"), "kTv")
    qscale = 128.0 ** -0.5
    for ti in range(TOK // 128):
        h_t = ht[ti % 2]
        k.dma("sp", h_t[:], d_h[ti * 128:(ti + 1) * 128, :])
        rstd_of(k, junk, ss, h_t[:], D)
        k.ts("dve", xs[:], h_t[:], ss[:], ALU.mult)
        norm_T(k, m, xs, [lambda c: cols[:, c:c + 1], lambda c: cols[:, 8 + c:9 + c]], [xkT, xqT])
        v_t = vt[ti % 2]
        for n4 in (2, 3):
            py = m.pw()
            for kc in range(8):
                k.matmul(py[:], xkT[:, kc, :], w_kv[:, kc, n4 * 512:(n4 + 1) * 512], start=(kc == 0), stop=(kc == 7))
            k.copy("act", v_t[:, (n4 - 2) * 512:(n4 - 1) * 512], py[:])
        k.dma("sp", d_v[ti * 128:(ti + 1) * 128, :], v_t[:])
        for which, xT, w, gcol, dview in (("k", xkT, w_kv, 16, kTv), ("q", xqT, w_q, 17, qTv)):
            pys = []
            for n4 in range(2):
                py = m.pw()
                for kc in range(8):
                    k.matmul(py[:], xT[:, kc, :], w[:, kc, n4 * 512:(n4 + 1) * 512], start=(kc == 0), stop=(kc == 7))
                pys.append(py)
                for h4 in range(4):
                    hh = n4 * 4 + h4
                    k.act(junk[:, 0:128], py[:, h4 * 128:(h4 + 1) * 128], AF.Square, accum_out=ss8[which][:, hh:hh + 1])
            s8 = ss8[which]
            k.ts("dve", s8[:], s8[:], 1.0 / 128, ALU.mult, EPS, ALU.add)
            k.act(s8[:], s8[:], AF.Ln)
            k.act(s8[:], s8[:], AF.Exp, scale=-0.5)
            nr = nrm[which]
            for hh in range(8):
                k.ts("dve", nr[:, hh * 128:(hh + 1) * 128], pys[hh // 4][:, (hh % 4) * 128:(hh % 4 + 1) * 128],
                     s8[:, hh:hh + 1], ALU.mult)
            o_ = oT[which][ti % 2]
            for n4 in range(2):
                p = m.pw()
                for h4 in range(4):
                    hh = n4 * 4 + h4
                    k.transpose(p[:, h4 * 128:(h4 + 1) * 128], nr[:, hh * 128:(hh + 1) * 128], m.ident[:])
                if which == "k":
                    k.ts("dve", o_[:, n4 * 512:(n4 + 1) * 512], p[:], cols[:, gcol:gcol + 1], ALU.mult)
                else:
                    k.ts("dve", o_[:, n4 * 512:(n4 + 1) * 512], p[:], cols[:, gcol:gcol + 1], ALU.mult, qscale, ALU.mult)
            o3 = k.track(o_.t.rearrange("p (a b) -> p a b", a=8), "o3")
            o3.reads, o3.writes = o_.reads, o_.writes
            k.dma("sp", dview[:, :, ti * 128:(ti + 1) * 128], o3[:])
    return [d_v, qTv, kTv]


def host_masks():
    import ml_dtypes
    s = np.arange(128)[:, None, None]
    jj = np.arange(4)[None, :, None]
    t = np.arange(512)[None, None, :]
    mk = np.where(jj * 128 + s < t, 0.0, NEG).astype(np.float32)
    return np.ascontiguousarray(mk.astype(ml_dtypes.bfloat16))


NEG = -30000.0


def build_attn(S, NH):
    nc, k, m = new_prog()
    T = dict(qT=din(k, m, "qT", [NH, 128, S], BF16), kT=din(k, m, "kT", [NH, 128, S], BF16),
             v=din(k, m, "v", [S, NH * 128], BF16), masks=din(k, m, "masks", [128, 4, 512], BF16),
             oT=dout(k, m, "oT", [NH, 128, S], BF16))
    emit_attn(k, m, S, NH, T)
    k.final_wait("sp", [T["oT"]])
    k.emit()
    return nc


def emit_attn(k, m, S, NH, T, parity=False, sel=None):
    m.arena.reset()
    al = m.arena.alloc
    d_q, d_k, d_v, d_mk, d_o = T["qT"], T["kT"], T["v"], T["masks"], T["oT"]
    NB = S // 128
    NSB = S // 512
    NM = 8 if parity else 4
    masks = al([128, NM, 512], BF16, "masks_sb")
    k.dma("sp", masks[:], d_mk[:])
    qs = [al([128, S], BF16, "qs%d" % i) for i in range(2)]
    ks = [al([128, S], BF16, "ks%d" % i) for i in range(2)]
    vs = [al([128, NB, 128], BF16, "vs%d" % i) for i in range(2)]
    e_ = [al([128, 512], F32, "e%d" % i) for i in range(2)]
    sp_ = [al([128, 512], BF16, "sp%d" % i) for i in range(2)]
    att_ = [al([128, 512], BF16, "att%d" % i) for i in range(2)]
    fac_ = [al([128, 4], F32, "fac%d" % i) for i in range(2)]
    accs = [al([128, 512], F32, "acc%d" % i) for i in range(2)]
    tmp = al([128, 512], F32, "acctmp")
    ot_ = [al([128, 512], BF16, "ot%d" % i) for i in range(2)]
    qsel = [al([128, 512], BF16, "qsel%d" % i) for i in range(2)]
    qtmp = al([128, 512], BF16, "qtmp")
    vview = k.track(d_v.t.rearrange("(blk p) c -> p blk c", p=128), "vview")
    pds = [Buf(m.pb[7].t[:, j * 64:j * 64 + 64], "pd%d" % j) for j in range(8)]
    ring = m.pb[0:7]
    rs = {"i": 0}

    def pw7():
        b_ = ring[rs["i"] % 7]
        rs["i"] += 1
        return b_
    old_pw = m.pw
    m.pw = pw7
    tiles = []
    for h in range(NH):
        for i in range(NSB // 2 if parity else NSB):
            nkb = 4 * (2 * i + 2) if parity else 4 * (i + 1)
            for kb in range(nkb):
                tiles.append((h, i, kb, nkb))
    state = {"cur": None}

    def stage_a(t):
        h, i, kb, nkb = tiles[t]
        q_h, k_h, v_h = qs[h % 2], ks[h % 2], vs[h % 2]
        if i == 0 and kb == 0:
            k.dma("sp", q_h[:], d_q[h, :, :])
            k.dma("sp", k_h[:], d_k[h, :, :])
            step = 16 if NB >= 16 else NB
            for b0 in range(0, NB, step):
                k.dma("sp", v_h[:, b0:b0 + step, :], vview[:, b0:b0 + step, h * 128:(h + 1) * 128])
        if parity:
            qb = qsel[(h * (NSB // 2) + i) % 2]
            if kb == 0:
                k.ts("dve", qtmp[:], q_h[:, (2 * i) * 512:(2 * i + 1) * 512], sel[:, 0:1], ALU.mult)
                k.stt(qb[:], q_h[:, (2 * i + 1) * 512:(2 * i + 2) * 512], sel[:, 1:2], qtmp[:], ALU.mult, ALU.add)
            qv = qb[:]
        else:
            qv = q_h[:, i * 512:(i + 1) * 512]
        jj = kb - (nkb - NM)
        diag = jj >= 0
        kv_ = k_h[:, kb * 128:(kb + 1) * 128]
        e, sp = e_[t % 2], sp_[t % 2]
        pz = m.pw()
        k.matmul(pz[:], kv_, qv, start=True, stop=not diag)
        if diag:
            k.matmul(pz[:], m.identb[:], masks[:, jj, :], start=False, stop=True)
        k.act(e[:], pz[:], AF.Exp)
        k.act(sp[:], e[:], AF.Ln, bias=1.0)
        pd = pds[t % len(pds)]
        return (kv_, qv, diag, jj, sp, v_h, pd)

    def stage_pd(t, a):
        kb = tiles[t][2]
        sp, pd = a[4], a[6]
        if kb > 0:
            for sub in range(4):
                k.matmul(pd[:, sub:sub + 1], sp[:, sub * 128:(sub + 1) * 128], m.onesb[:, 0:1])

    def stage_b(t, a, mid):
        h, i, kb, nkb = tiles[t]
        kv_, qv, diag, jj, sp, v_h, pd = a
        att, fac = att_[t % 2], fac_[t % 2]
        cur = state["cur"]
        nxt = accs[0] if cur is not accs[0] else accs[1]
        pl = m.pw()
        k.matmul(pl[:], kv_, qv, start=True, stop=False)
        k.matmul(pl[:], m.ntriS[:], sp[:], start=False, stop=not diag)
        if diag:
            k.matmul(pl[:], m.identb[:], masks[:, jj, :], start=False, stop=True)
        mid()
        if kb > 0:
            k.act(fac[:], pd[:, 0:4], AF.Exp, scale=-1.0)
            for sub in range(4):
                k.ts("dve", tmp[:, sub * 128:(sub + 1) * 128], cur[:, sub * 128:(sub + 1) * 128],
                     fac[:, sub:sub + 1], ALU.mult)
        k.act(att[:], pl[:], AF.Exp)
        pP = m.pw()
        for sub in range(4):
            k.matmul(pP[:, sub * 128:(sub + 1) * 128], att[:, sub * 128:(sub + 1) * 128], v_h[:, kb, :])
        if kb == 0:
            k.copy("dve", nxt[:], pP[:])
        else:
            k.tt("dve", nxt[:], pP[:], tmp[:], ALU.add)
        state["cur"] = nxt
        if kb == nkb - 1:
            pT = m.pw()
            for sub in range(4):
                k.transpose(pT[:, sub * 128:(sub + 1) * 128], nxt[:, sub * 128:(sub + 1) * 128], m.ident[:])
            o_ = ot_[i % 2]
            k.copy("dve", o_[:], pT[:])
            k.dma("sp", d_o[h, :, i * 512:(i + 1) * 512], o_[:])

    a_next = stage_a(0)
    stage_pd(0, a_next)
    for t in range(len(tiles)):
        a_cur = a_next
        if t + 1 < len(tiles):
            a_next = stage_a(t + 1)
            stage_b(t, a_cur, lambda: stage_pd(t + 1, a_next))
        else:
            stage_b(t, a_cur, lambda: None)
    m.pw = old_pw


def build_gdn(S, NH=4, NGRP=1):
    nc, k, m = new_prog()
    W = NH * 128
    T = dict(x=din(k, m, "x", [S, D]), an=din(k, m, "an", [8, 128]), w_qkvz=din(k, m, "w_qkvz", [D, NGRP * 4 * W]),
             w_bg=din(k, m, "w_bg", [D, NGRP * 2 * NH]), conv=din(k, m, "conv", [4, NGRP * 3 * W]),
             alog=din(k, m, "alog", [1, NGRP * NH]), dtb=din(k, m, "dtb", [1, NGRP * NH]),
             ogain=din(k, m, "ogain", [1, 128]), ogT=dout(k, m, "ogT", [NGRP * NH, 128, S], BF16))
    ov = emit_gdn(k, m, S, T, NH, NGRP)
    print("gdn arena words used", m.arena.off, "of", m.arena.n32)
    k.final_wait("sp", [ov])
    k.emit()
    return nc


def emit_gdn(k, m, S, T, NH=4, NGRP=1):
    m.arena.reset()
    al = m.arena.alloc
    W = NH * 128
    d_x, d_an, d_w, d_wbg, d_conv, d_alog, d_dtb, d_og, d_o = (T["x"], T["an"], T["w_qkvz"], T["w_bg"], T["conv"],
                                                                  T["alog"], T["dtb"], T["ogain"], T["ogT"])
    NG = 3 * NH
    cols = load_cols(k, m, [(d_an[:], 8), (d_og[:], 1)])
    convc = load_cols(k, m, [(k.track(d_conv.t.rearrange("t (j p) -> (t j) p", p=128), "cwv")[:], 4 * NG * NGRP)])
    w = al([128, 8, NGRP * 4 * W], BF16, "w")
    wbg = al([128, 8, NGRP * 2 * NH], BF16, "wbg")
    load_w(k, w, lambda kc: d_w[kc * 128:(kc + 1) * 128, :], 8)
    load_w(k, wbg, lambda kc: d_wbg[kc * 128:(kc + 1) * 128, :], 8)
    alog = al([128, NGRP * NH], F32, "alog")
    dtb = al([128, NGRP * NH], F32, "dtb")
    k.dma("sp", alog[:], k.track(d_alog.t[0, :].partition_broadcast(128), "alv")[:])
    k.dma("sp", dtb[:], k.track(d_dtb.t[0, :].partition_broadcast(128), "dtv")[:])
    nexpA = al([128, NGRP * NH], F32, "nexpA")
    k.act(nexpA[:], alog[:], AF.Exp)
    k.ts("dve", nexpA[:], nexpA[:], -1.0, ALU.mult)
    ident4 = al([128, W], F32, "ident4")
    umask4 = al([128, W], F32, "umask4")
    lvl4 = [al([128, W], BF16, "lvl4_%d" % i) for i in range(7)]
    for h in range(NH):
        sl = slice(h * 128, (h + 1) * 128)
        k.copy("dve", ident4[:, sl], m.ident[:])
        k.copy("dve", umask4[:, sl], m.umask[:])
        for i in range(7):
            k.copy("dve", lvl4[i][:, sl], m.lvl[i][:])
    junk = al([128, 1024], BF16)
    ss = al([128, 1], F32)
    xt = [al([128, D], F32, "xt%d" % i) for i in range(2)]
    xs = al([128, D], F32, "xs")
    xnT = al([128, 8, 128], BF16, "xnT")
    cbs = [[al([128, NH, 131], BF16, "cb%d_%d" % (G, g)) for g in range(3)] for G in range(NGRP)]
    dgw = al([128, 4 * NG * NGRP, 128], BF16, "dgw")
    for c_ in range(4 * NG * NGRP):
        k.ts("dve", dgw[:, c_, :], m.identb[:], convc[:, c_:c_ + 1], ALU.mult)
    for G in range(NGRP):
        for g in range(3):
            k.memset("dve", cbs[G][g][:], 0.0)
    F2 = lambda n, dt=F32: al([128, W], dt, n)
    Ssts = [F2("S%d" % G) for G in range(NGRP)]
    Sbs = [F2("Sb%d" % G, BF16) for G in range(NGRP)]
    for G in range(NGRP):
        k.memset("dve", Ssts[G][:], 0.0)
        k.memset("dve", Sbs[G][:], 0.0)
    NSLOT = 2 if (NGRP >= 2 and NH <= 2) else 1

    def mk_slot(si):
        F = lambda n, dt=F32: al([128, W], dt, "%s_s%d" % (n, si))
        sl = {}
        sl["sil"] = [F("sil%d" % g_) for g_ in range(3)]
        for n in ("zs", "sq2", "ri", "kTf", "vtm", "ktm", "dg", "tmpF", "Fm", "FU", "Am", "u", "t1", "o_", "on", "St"):
            sl[n] = F(n)
        for n in ("qT", "kT", "Pa", "Pb", "Xa", "Xb", "Y", "attnT", "vb", "kb", "kd", "wT", "vnew"):
            sl[n] = F(n, BF16)
        sl["L"] = [F("L%d" % i, BF16) for i in range(7)]
        sl["ogt"] = [F("ogt%d" % i, BF16) for i in range(2)]
        sl["sm"] = {n: al([128, NH], F32, "%s_s%d" % (n, si)) for n in
                    ("beta", "g", "gc", "ngc", "glast", "egc", "ekd", "eglast", "sckb", "tmp8", "ss4")}
        return sl
    slots = [mk_slot(si) for si in range(NSLOT)]
    oview = k.track(d_o.t.rearrange("h p t -> p h t"), "oview")
    qscale = float(np.log(128.0 ** -0.5))
    HS = [slice(h * 128, (h + 1) * 128) for h in range(NH)]

    def mm4(p, lhs_of, rhs_of):
        for h in range(NH):
            k.matmul(p[:, HS[h]], lhs_of(h), rhs_of(h))

    for ci in range(S // 128):
        x_t = xt[ci % 2]
        k.dma("sp", x_t[:], d_x[ci * 128:(ci + 1) * 128, :])
        rstd_of(k, junk, ss, x_t[:], D)
        k.ts("dve", xs[:], x_t[:], ss[:], ALU.mult)
        norm_T(k, m, xs, [lambda c: cols[:, c:c + 1]], [xnT])
        def gbody(G, sl):
            cb, Sst, Sb = cbs[G], Ssts[G], Sbs[G]
            dtbG, nexpAG = dtb[:, G * NH:(G + 1) * NH], nexpA[:, G * NH:(G + 1) * NH]
            sil, sm, L, ogt = sl["sil"], sl["sm"], sl["L"], sl["ogt"]
            zs, sq2, ri, kTf, vtm, ktm, dg, tmpF, Fm, FU, Am = (sl[n] for n in ("zs", "sq2", "ri", "kTf", "vtm", "ktm", "dg", "tmpF", "Fm", "FU", "Am"))
            u, t1, o_, on, St = (sl[n] for n in ("u", "t1", "o_", "on", "St"))
            qT, kT, Pa, Pb, Xa, Xb, Y, attnT, vb, kb, kd, wT, vnew = (sl[n] for n in ("qT", "kT", "Pa", "Pb", "Xa", "Xb", "Y", "attnT", "vb", "kb", "kd", "wT", "vnew"))
            yield
            banks = []
            for g in range(4):
                p = m.pw()
                for h in range(NH):
                    fc = g * NH + h
                    for kc in range(8):
                        k.matmul(p[:, HS[h]], w[:, kc, G * 4 * W + fc * 128:G * 4 * W + (fc + 1) * 128], xnT[:, kc, :], start=(kc == 0), stop=(kc == 7))
                banks.append(p)
            k.act(zs[:], banks[3][:, 0:W], AF.Silu)
            for g in range(3):
                for h in range(NH):
                    k.copy("act", cb[g][:, h, 3:131], banks[g][:, HS[h]])
            for g in range(3):
                pc = m.pw()
                for h in range(NH):
                    j = g * NH + h
                    for t in range(4):
                        k.matmul(pc[:, HS[h]], dgw[:, t * NG * NGRP + G * NG + j, :], cb[g][:, h, t:t + 128],
                                 start=(t == 0), stop=(t == 3))
                for h in range(NH):
                    k.copy("act", cb[g][:, h, 0:3], cb[g][:, h, 128:131])
                k.act(sil[g][:], pc[:, 0:W], AF.Silu)
            yield
            pbg = m.pw()
            for kc in range(8):
                k.matmul(pbg[:, 0:2 * NH], xnT[:, kc, :], wbg[:, kc, G * 2 * NH:(G + 1) * 2 * NH], start=(kc == 0), stop=(kc == 7))
            k.act(sm["beta"][:], pbg[:, 0:NH], AF.Exp, scale=-1.0)
            k.ts("dve", sm["beta"][:], sm["beta"][:], 1.0, ALU.add)
            k.recip(sm["beta"][:], sm["beta"][:])
            k.tt("dve", sm["g"][:], pbg[:, NH:2 * NH], dtbG, ALU.add)
            k.act(sm["g"][:], sm["g"][:], AF.Exp)
            k.act(sm["g"][:], sm["g"][:], AF.Ln, bias=1.0)
            k.tt("dve", sm["g"][:], sm["g"][:], nexpAG, ALU.mult)
            pg = m.pw()
            k.matmul(pg[:, 0:NH], m.triI[:], sm["g"][:])
            k.matmul(pg[:, 8:8 + NH], m.ones[:], sm["g"][:])
            k.copy("dve", sm["gc"][:], pg[:, 0:NH])
            k.copy("dve", sm["glast"][:], pg[:, 8:8 + NH])
            k.ts("dve", sm["ngc"][:], sm["gc"][:], -1.0, ALU.mult)
            k.act(sm["egc"][:], sm["gc"][:], AF.Exp)
            k.act(sm["eglast"][:], sm["glast"][:], AF.Exp)
            k.tt("dve", sm["tmp8"][:], sm["glast"][:], sm["gc"][:], ALU.subtract)
            k.act(sm["ekd"][:], sm["tmp8"][:], AF.Exp)
            k.tt("dve", sm["sckb"][:], sm["beta"][:], sm["egc"][:], ALU.mult)
            yield
            for g in range(2):
                k.act(sq2[:], sil[g][:], AF.Square)
                ps = m.pw()
                k.matmul(ps[:, 0:W], m.ones[:], sq2[:])
                k.act(ri[:], ps[:, 0:W], AF.Ln, bias=EPS)
                if g == 0:
                    k.act(ri[:], ri[:], AF.Exp, scale=-0.5, bias=qscale)
                    k.tt("dve", qT[:], sil[0][:], ri[:], ALU.mult)
                else:
                    k.act(ri[:], ri[:], AF.Exp, scale=-0.5)
                    k.tt("dve", kTf[:], sil[1][:], ri[:], ALU.mult)
                    k.copy("act", kT[:], kTf[:])
            pt = m.pw()
            for h in range(NH):
                k.transpose(pt[:, HS[h]], kTf[:, HS[h]], m.ident[:])
            k.copy("dve", ktm[:], pt[:, 0:W])
            pt = m.pw()
            for h in range(NH):
                k.transpose(pt[:, HS[h]], sil[2][:, HS[h]], m.ident[:])
            k.copy("dve", vtm[:], pt[:, 0:W])
            yield
            for h in range(NH):
                k.ts("dve", dg[:, HS[h]], m.ident[:], sm["gc"][:, h:h + 1], ALU.mult)
            pR = m.pw()
            k.matmul(pR[:, 0:W], m.ones[:], dg[:])
            for h in range(NH):
                k.act(tmpF[:, HS[h]], pR[:, HS[h]], AF.Abs, bias=sm["ngc"][:, h:h + 1])
            k.act(Fm[:], tmpF[:], AF.Exp, scale=-1.0)
            k.tt("dve", FU[:], Fm[:], umask4[:], ALU.mult)
            pKK = m.pw()
            mm4(pKK, lambda h: kT[:, HS[h]], lambda h: kT[:, HS[h]])
            for h in range(NH):
                k.stt(Am[:, HS[h]], pKK[:, HS[h]], sm["beta"][:, h:h + 1], Fm[:, HS[h]], ALU.mult, ALU.mult)
            pQK = m.pw()
            mm4(pQK, lambda h: kT[:, HS[h]], lambda h: qT[:, HS[h]])
            k.tt("dve", attnT[:], pQK[:, 0:W], FU[:], ALU.mult)
            for i in range(7):
                k.tt("dve", L[i][:], Am[:], lvl4[i][:], ALU.mult)
            yield
            pY = m.pw()
            mm4(pY, lambda h: L[0][:, HS[h]], lambda h: m.identb[:])
            Pc, Pn, Xc, Xn = Pa, Pb, Xa, Xb
            k.stt(Pc[:], pY[:, 0:W], -1.0, ident4[:], ALU.mult, ALU.add)
            k.tt("dve", Xc[:], ident4[:], L[0][:], ALU.subtract)
            for i in range(1, 7):
                pY = m.pw()
                mm4(pY, lambda h: L[i][:, HS[h]], lambda h: Pc[:, HS[h]])
                k.copy("act", Y[:], pY[:, 0:W])
                pZ = m.pw()
                mm4(pZ, lambda h: Xc[:, HS[h]], lambda h: Y[:, HS[h]])
                if i < 6:
                    pZT = m.pw()
                    mm4(pZT, lambda h: Y[:, HS[h]], lambda h: Xc[:, HS[h]])
                k.stt(Pn[:], pZ[:, 0:W], -1.0, Pc[:], ALU.mult, ALU.add)
                Pc, Pn = Pn, Pc
                if i < 6:
                    k.stt(Xn[:], pZT[:, 0:W], -1.0, Xc[:], ALU.mult, ALU.add)
                    Xc, Xn = Xn, Xc
                yield
            yield
            for h in range(NH):
                k.ts("dve", vb[:, HS[h]], vtm[:, HS[h]], sm["beta"][:, h:h + 1], ALU.mult)
            for h in range(NH):
                k.ts("dve", kb[:, HS[h]], ktm[:, HS[h]], sm["sckb"][:, h:h + 1], ALU.mult)
            for h in range(NH):
                k.ts("dve", kd[:, HS[h]], ktm[:, HS[h]], sm["ekd"][:, h:h + 1], ALU.mult)
            pu = m.pw()
            mm4(pu, lambda h: Pc[:, HS[h]], lambda h: vb[:, HS[h]])
            k.copy("dve", u[:], pu[:, 0:W])
            pw_ = m.pw()
            mm4(pw_, lambda h: kb[:, HS[h]], lambda h: Pc[:, HS[h]])
            k.copy("dve", wT[:], pw_[:, 0:W])
            yield
            pws = m.pw()
            mm4(pws, lambda h: wT[:, HS[h]], lambda h: Sb[:, HS[h]])
            k.stt(vnew[:], pws[:, 0:W], -1.0, u[:], ALU.mult, ALU.add)
            po1 = m.pw()
            mm4(po1, lambda h: qT[:, HS[h]], lambda h: Sb[:, HS[h]])
            po2 = m.pw()
            mm4(po2, lambda h: attnT[:, HS[h]], lambda h: vnew[:, HS[h]])
            for h in range(NH):
                k.ts("dve", t1[:, HS[h]], po1[:, HS[h]], sm["egc"][:, h:h + 1], ALU.mult)
            k.tt("dve", o_[:], po2[:, 0:W], t1[:], ALU.add)
            pS = m.pw()
            mm4(pS, lambda h: kd[:, HS[h]], lambda h: vnew[:, HS[h]])
            for h in range(NH):
                k.ts("dve", St[:, HS[h]], Sst[:, HS[h]], sm["eglast"][:, h:h + 1], ALU.mult)
            k.tt("dve", Sst[:], pS[:, 0:W], St[:], ALU.add)
            k.copy("act", Sb[:], Sst[:])
            yield
            for h in range(NH):
                k.act(junk[:, 0:128], o_[:, HS[h]], AF.Square, accum_out=sm["ss4"][:, h:h + 1])
            k.ts("dve", sm["ss4"][:], sm["ss4"][:], 1.0 / 128, ALU.mult, EPS, ALU.add)
            k.act(sm["ss4"][:], sm["ss4"][:], AF.Ln)
            k.act(sm["ss4"][:], sm["ss4"][:], AF.Exp, scale=-0.5)
            for h in range(NH):
                k.ts("dve", on[:, HS[h]], o_[:, HS[h]], sm["ss4"][:, h:h + 1], ALU.mult)
            pT = m.pw()
            for h in range(NH):
                k.transpose(pT[:, HS[h]], on[:, HS[h]], m.ident[:])
            og = ogt[(ci * NGRP + G) % 2]
            k.stt(og[:], pT[:, 0:W], cols[:, 8:9], zs[:], ALU.mult, ALU.mult)
            og3 = k.track(og.t.rearrange("p (a b) -> p a b", a=NH), "og3")
            og3.reads, og3.writes = og.reads, og.writes
            k.dma("sp", oview[:, G * NH:(G + 1) * NH, ci * 128:(ci + 1) * 128], og3[:])

        for G0 in range(0, NGRP, NSLOT):
            gens = [gbody(G0 + si, slots[si]) for si in range(NSLOT)]
            alive = list(gens)
            while alive:
                for gen_ in list(alive):
                    try:
                        next(gen_)
                    except StopIteration:
                        alive.remove(gen_)
    return oview


GDN_HPG = 4


def build_fused(S):
    nc, k, m = new_prog()
    NPOS = S // 1024
    TOKO = NPOS * 512
    I = lambda n, shp, dt=F32: din(k, m, n, shp, dt)
    x = I("x", [S, D])
    Tg = dict(x=x, an=I("an0", [8, 128]), w_qkvz=I("w_qkvz", [D, 4096]), w_bg=I("w_bg", [D, 16]),
              conv=I("conv", [4, 3072]), alog=I("alog", [1, 8]), dtb=I("dtb", [1, 8]), ogain=I("ogain", [1, 128]))
    ogT = k.dram("ogT_scr", [8, 128, S], BF16)
    Tg["ogT"] = ogT
    h2 = k.dram("h2_scr", [S, D], F32)
    Tf0 = dict(x=x, aT=ogT, w_o=I("gdn_w_out", [D, D]), fnorm=I("fn0", [8, 128]), w_gu=I("w_gu0", [D, 2 * DFF]),
               w_d=I("w_d0", [DFF, D]), h=h2)
    qT = k.dram("qT_scr", [8, 128, S], BF16)
    kT = k.dram("kT_scr", [8, 128, S], BF16)
    v = k.dram("v_scr", [S, D], BF16)
    Tk = dict(h=h2, kvn=I("kvn", [8, 128]), an=I("an1", [8, 128]), kg=I("kg", [1, 128]), qg=I("qg", [1, 128]),
              w_kv=I("w_kv", [D, 2 * D]), w_q=I("w_q", [D, D]), qT=qT, kT=kT, v=v)
    oTs = k.dram("oT_scr", [8, 128, TOKO], BF16)
    Ta = dict(qT=qT, kT=kT, v=v, masks=I("masks", [128, 8, 512], BF16), oT=oTs)
    d_sel = I("sel", [128, 2])
    out = dout(k, m, "out", [TOKO, D])
    Tf1 = dict(x=None, aT=oTs, w_o=I("sb_w_out", [D, D]), fnorm=I("fn1", [8, 128]), w_gu=I("w_gu1", [D, 2 * DFF]),
               w_d=I("w_d1", [DFF, D]), h=out)
    sel = k.sbuf([128, 2], F32, "sel_sb")
    k.dma("sp", sel[:], d_sel[:])

    emit_gdn(k, m, S, Tg, GDN_HPG, 8 // GDN_HPG)
    barrier(k)
    emit_dense_ffn(k, m, S, Tf0)
    barrier(k)
    emit_kvq(k, m, S, Tk)
    barrier(k)
    emit_attn(k, m, S, 8, Ta, parity=True, sel=sel)
    barrier(k)

    def xload(ti, x_t, xtmp):
        j, r = ti // 4, ti % 4
        ra = (2 * j) * 512 + r * 128
        rb = (2 * j + 1) * 512 + r * 128
        k.dma("sp", xtmp[0][:], h2[ra:ra + 128, :])
        k.dma("sp", xtmp[1][:], h2[rb:rb + 128, :])
        k.ts("dve", xtmp[0][:], xtmp[0][:], sel[:, 0:1], ALU.mult)
        k.stt(x_t[:], xtmp[1][:], sel[:, 1:2], xtmp[0][:], ALU.mult, ALU.add)
    emit_dense_ffn(k, m, TOKO, Tf1, xload=xload)
    k.final_wait("sp", [out])
    k.emit()
    return nc


def host_parity_masks(p):
    c = host_masks()
    z = np.zeros_like(c)
    f = np.full_like(c, NEG)
    return _c(np.concatenate([z, c], axis=1) if p == 1 else np.concatenate([c, f], axis=1))


def kernel_fused(inp, x):
    B, S, _ = x.shape
    cores = list(range(2 * B))
    cf, cb = host_consts()
    w_in = inp["gdn_w_in"][0]
    cw = inp["gdn_conv_w"][0]

    def grp(a, width, hpg=GDN_HPG):
        nblk = a.shape[1] // (8 * width)
        parts = []
        for G in range(8 // hpg):
            for blk in range(nblk):
                parts.append(a[:, blk * 8 * width + G * hpg * width: blk * 8 * width + (G + 1) * hpg * width])
        return _c(np.concatenate(parts, axis=1))
    common = dict(cf=cf, cb=cb, an0=_c(inp["attn_norm"][0].reshape(8, 128)), w_qkvz=grp(w_in[:, 0:4096], 128),
                  w_bg=grp(w_in[:, 4096:4112], 1), conv=grp(cw, 128), alog=_c(inp["gdn_a_log"].reshape(1, 8)),
                  dtb=_c(inp["gdn_dt_bias"].reshape(1, 8)), ogain=_c(inp["gdn_o_gain"].reshape(1, 128)),
                  gdn_w_out=_c(inp["gdn_w_out"][0]), fn0=_c(inp["ffn_norm"][0].reshape(8, 128)),
                  w_gu0=_c(inp["ffn_w_gu"][0]), w_d0=_c(inp["ffn_w_down"][0]),
                  kvn=_c(inp["kv_norm"].reshape(8, 128)), an1=_c(inp["attn_norm"][1].reshape(8, 128)),
                  kg=_c(inp["k_gain"].reshape(1, 128)), qg=_c(inp["sb_q_gain"].reshape(1, 128)),
                  w_kv=_c(inp["w_kv"]), w_q=_c(inp["sb_w_q"][0]), sb_w_out=_c(inp["sb_w_out"][0]),
                  fn1=_c(inp["ffn_norm"][1].reshape(8, 128)), w_gu1=_c(inp["ffn_w_gu"][1]), w_d1=_c(inp["ffn_w_down"][1]))
    feeds = []
    for c in cores:
        b, p = c // 2, c % 2
        selv = np.empty((128, 2), np.float32)
        selv[:, 0] = 1.0 - p
        selv[:, 1] = float(p)
        feeds.append(dict(common, x=_c(x[b]), masks=host_parity_masks(p), sel=selv))
    res = run_bass_kernel_spmd(_prog("fused", build_fused, S), feeds, core_ids=cores).results
    out = np.empty((B, S, D), np.float32)
    for c in cores:
        b, p = c // 2, c % 2
        o = res[c]["out"]
        for j in range(S // 1024):
            out[b, (2 * j + p) * 512:(2 * j + p + 1) * 512] = o[j * 512:(j + 1) * 512]
    return out

from concourse.bass_utils import run_bass_kernel_spmd

_PROGS = {}


def _prog(name, fn, *args):
    key = (name,) + args
    if key not in _PROGS:
        _PROGS[key] = fn(*args)
    return _PROGS[key]


def _c(a):
    return np.ascontiguousarray(a)


def kernel_unfused(**inputs):
    inp = {n: np.asarray(v) for n, v in inputs.items()}
    x = inp["x"].astype(np.float32, copy=False)
    B, S, _ = x.shape
    NC = 8
    HALF = S // 2
    cores = list(range(NC))
    cf, cb = host_consts()
    cst = {"cf": cf, "cb": cb}

    w_in = inp["gdn_w_in"][0]
    cw = inp["gdn_conv_w"][0]
    feeds = []
    for c in cores:
        b, hg = c // 2, c % 2
        hs = slice(hg * 512, (hg + 1) * 512)
        h4 = slice(hg * 4, (hg + 1) * 4)
        feeds.append(dict(cst, x=_c(x[b]), an=_c(inp["attn_norm"][0].reshape(8, 128)),
                          w_qkvz=_c(np.concatenate([w_in[:, 0:1024][:, hs], w_in[:, 1024:2048][:, hs],
                                                    w_in[:, 2048:3072][:, hs], w_in[:, 3072:4096][:, hs]], axis=1)),
                          w_bg=_c(np.concatenate([w_in[:, 4096:4104][:, h4], w_in[:, 4104:4112][:, h4]], axis=1)),
                          conv=_c(np.concatenate([cw[:, 0:1024][:, hs], cw[:, 1024:2048][:, hs], cw[:, 2048:3072][:, hs]], axis=1)),
                          alog=_c(inp["gdn_a_log"][:, h4]), dtb=_c(inp["gdn_dt_bias"][:, h4]),
                          ogain=_c(inp["gdn_o_gain"].reshape(1, 128))))
    r1 = run_bass_kernel_spmd(_prog("gdn", build_gdn, S, 4), feeds, core_ids=cores).results
    ogT = [np.concatenate([r1[2 * b]["ogT"], r1[2 * b + 1]["ogT"]], axis=0) for b in range(B)]

    nc_ffn = _prog("ffn", build_dense_ffn, HALF)
    feeds = []
    for c in cores:
        b, hf = c // 2, c % 2
        ts = slice(hf * HALF, (hf + 1) * HALF)
        feeds.append(dict(cst, x=_c(x[b, ts]), aT=_c(ogT[b][:, :, ts]), w_o=_c(inp["gdn_w_out"][0]),
                          fnorm=_c(inp["ffn_norm"][0].reshape(8, 128)), w_gu=_c(inp["ffn_w_gu"][0]),
                          w_d=_c(inp["ffn_w_down"][0])))
    r2 = run_bass_kernel_spmd(nc_ffn, feeds, core_ids=cores).results
    h2 = [r["h"] for r in r2]

    feeds = []
    for c in cores:
        feeds.append(dict(cst, h=_c(h2[c]), kvn=_c(inp["kv_norm"].reshape(8, 128)),
                          an=_c(inp["attn_norm"][1].reshape(8, 128)), kg=_c(inp["k_gain"].reshape(1, 128)),
                          qg=_c(inp["sb_q_gain"].reshape(1, 128)), w_kv=_c(inp["w_kv"]), w_q=_c(inp["sb_w_q"][0])))
    r3 = run_bass_kernel_spmd(_prog("kvq", build_kvq, HALF), feeds, core_ids=cores).results

    mk = host_masks()
    feeds = []
    for c in cores:
        b, hg = c // 2, c % 2
        h4 = slice(hg * 4, (hg + 1) * 4)
        qT = np.concatenate([r3[2 * b]["qT"][h4], r3[2 * b + 1]["qT"][h4]], axis=2)
        kT = np.concatenate([r3[2 * b]["kT"][h4], r3[2 * b + 1]["kT"][h4]], axis=2)
        v = np.concatenate([r3[2 * b]["v"], r3[2 * b + 1]["v"]], axis=0)[:, hg * 512:(hg + 1) * 512]
        feeds.append(dict(cst, qT=_c(qT), kT=_c(kT), v=_c(v), masks=mk))
    r4 = run_bass_kernel_spmd(_prog("attn", build_attn, S, 4), feeds, core_ids=cores).results
    oT = [np.concatenate([r4[2 * b]["oT"], r4[2 * b + 1]["oT"]], axis=0) for b in range(B)]

    feeds = []
    for c in cores:
        b, hf = c // 2, c % 2
        ts = slice(hf * HALF, (hf + 1) * HALF)
        feeds.append(dict(cst, x=_c(h2[c]), aT=_c(oT[b][:, :, ts]), w_o=_c(inp["sb_w_out"][0]),
                          fnorm=_c(inp["ffn_norm"][1].reshape(8, 128)), w_gu=_c(inp["ffn_w_gu"][1]),
                          w_d=_c(inp["ffn_w_down"][1])))
    r5 = run_bass_kernel_spmd(_prog("ffn_b", build_dense_ffn, HALF), feeds, core_ids=cores).results
    out = np.empty((B, S, D), np.float32)
    for c in cores:
        b, hf = c // 2, c % 2
        out[b, hf * HALF:(hf + 1) * HALF] = r5[c]["h"]
    return out


def kernel(**inputs):
    inp = {n: np.asarray(v) for n, v in inputs.items()}
    x = inp["x"].astype(np.float32, copy=False)
    return kernel_fused(inp, x)
```

```python
from contextlib import ExitStack
import numpy as np
import concourse.bass as bass
import concourse.mybir as mybir

F32 = mybir.dt.float32
BF16 = mybir.dt.bfloat16
ALU = mybir.AluOpType
AF = mybir.ActivationFunctionType
AX = mybir.AxisListType

SAME_ENGINE_SYNC = True
N_DMA_SEMS = 6


class Buf:
    def __init__(self, t, name):
        self.t = t
        self.name = name
        self.reads = {}
        self.writes = {}

    def __getitem__(self, idx):
        return View(self, self.t[idx])


class View:
    def __init__(self, buf, ap):
        self.buf = buf
        self.ap = ap


def _ap(x):
    return x.ap if isinstance(x, View) else x


class Eng:
    def __init__(self, name):
        self.name = name
        self.ops = []
        self.trace = []
        self.count = 0
        self.waited = {}
        self.sem = None
        self.dma_sems = []
        self.dma_vals = []
        self.dma_k = 0


class K:
    def __init__(self, nc):
        self.nc = nc
        self.es = ExitStack()
        self.engs = {n: Eng(n) for n in ("pe", "act", "dve", "pool", "sp")}
        self.sems = {}
        for n, e in self.engs.items():
            e.sem = self.es.enter_context(nc.semaphore("s_" + n))
            self.sems[n] = e.sem
        for n in ("sp", "pool", "act"):
            e = self.engs[n]
            for i in range(N_DMA_SEMS):
                s = self.es.enter_context(nc.semaphore("d_%s%d" % (n, i)))
                key = "d_%s%d" % (n, i)
                self.sems[key] = s
                e.dma_sems.append(key)
                e.dma_vals.append(0)
        self.nbuf = 0

    def sbuf(self, shape, dtype, name=None):
        self.nbuf += 1
        name = name or "sb%d" % self.nbuf
        t = self.es.enter_context(self.nc.sbuf_tensor(name, list(shape), dtype))
        return Buf(t, name)

    def psum(self, shape, dtype, name=None):
        self.nbuf += 1
        name = name or "ps%d" % self.nbuf
        t = self.es.enter_context(self.nc.psum_tensor(name, list(shape), dtype))
        return Buf(t, name)

    def dram(self, name, shape, dtype, kind="Internal"):
        t = self.nc.dram_tensor(name, list(shape), dtype, kind=kind)
        return Buf(t.ap(), name)

    def track(self, ap, name):
        return Buf(ap, name)

    def _wait(self, e, key, val, war=False):
        if key == e.name and (e.name == "pe" or war or not SAME_ENGINE_SYNC):
            return
        if e.waited.get(key, 0) >= val:
            return
        e.waited[key] = val
        sem = self.sems[key]
        e.ops.append(("wait", key, val))
        e.trace.append(("w", key, val))

    def _deps(self, e, reads, writes, nowaw=False):
        for v in reads:
            if isinstance(v, View):
                for k, val in v.buf.writes.items():
                    self._wait(e, k, val)
        for v in writes:
            if isinstance(v, View):
                for k, val in v.buf.reads.items():
                    self._wait(e, k, val, war=True)
                if not nowaw:
                    for k, val in v.buf.writes.items():
                        self._wait(e, k, val)

    def _mark(self, key, val, reads, writes):
        for v in reads:
            if isinstance(v, View):
                if v.buf.reads.get(key, 0) < val:
                    v.buf.reads[key] = val
        for v in writes:
            if isinstance(v, View):
                if v.buf.writes.get(key, 0) < val:
                    v.buf.writes[key] = val

    def op(self, eng, fn, reads, writes, nowaw=False):
        e = self.engs[eng]
        self._deps(e, reads, writes, nowaw)
        e.count += 1
        sem = e.sem
        e.ops.append(("op", fn, e.count))
        e.trace.append(("i", e.name, 1))
        self._mark(e.name, e.count, reads, writes)

    def dma(self, eng, out, in_, nowaw=True, **kw):
        e = self.engs[eng]
        self._deps(e, [in_], [out], nowaw)
        i = e.dma_k % N_DMA_SEMS
        e.dma_k += 1
        key = e.dma_sems[i]
        if e.dma_vals[i] > 0:
            self._wait(e, key, e.dma_vals[i])
        e.dma_vals[i] += 16
        val = e.dma_vals[i]
        sem = self.sems[key]
        o, s = _ap(out), _ap(in_)
        e.ops.append(("dma", o, s, key, kw))
        e.trace.append(("i", key, 16))
        self._mark(key, val, [in_], [out])
        return (key, val)

    def allgather(self, out, in_, groups):
        e = self.engs["pool"]
        self._deps(e, [in_], [out], False)
        i = e.dma_k % N_DMA_SEMS
        e.dma_k += 1
        key = e.dma_sems[i]
        if e.dma_vals[i] > 0:
            self._wait(e, key, e.dma_vals[i])
        e.dma_vals[i] += 16
        val = e.dma_vals[i]
        o, s_ = _ap(out), _ap(in_)
        e.ops.append(("cc", o, s_, key, groups))
        e.trace.append(("i", key, 16))
        self._mark(key, val, [in_], [out])

    def final_wait(self, eng, bufs):
        e = self.engs[eng]
        for b in bufs:
            for k, val in b.writes.items():
                self._wait(e, k, val)

    def matmul(self, out, lhsT, rhs, start=True, stop=True, **kw):
        o, l, r = _ap(out), _ap(lhsT), _ap(rhs)
        self.op("pe", lambda h: h.matmul(o, l, r, start=start, stop=stop, **kw), [lhsT, rhs], [out])

    def transpose(self, out, in_, ident):
        o, i, d = _ap(out), _ap(in_), _ap(ident)
        self.op("pe", lambda h: h.transpose(o, i, d), [in_, ident], [out])

    def act(self, out, in_, func, bias=None, scale=None, accum_out=None, eng="act"):
        o, i = _ap(out), _ap(in_)
        kw = {}
        rd = [in_]
        wr = [out]
        if bias is not None:
            kw["bias"] = _ap(bias)
            rd.append(bias)
        if scale is not None:
            kw["scale"] = _ap(scale)
            rd.append(scale)
        if accum_out is not None:
            kw["accum_out"] = _ap(accum_out)
            wr.append(accum_out)
        self.op("act", lambda h: h.activation(o, i, func, **kw), rd, wr)

    def tt(self, eng, out, in0, in1, op):
        o, a, b = _ap(out), _ap(in0), _ap(in1)
        self.op(eng, lambda h: h.tensor_tensor(o, a, b, op), [in0, in1], [out])

    def ts(self, eng, out, in0, s1, op0, s2=None, op1=None, accum_out=None):
        o, a = _ap(out), _ap(in0)
        rd = [in0]
        wr = [out]
        if isinstance(s1, View):
            rd.append(s1)
        if isinstance(s2, View):
            rd.append(s2)
        kw = {}
        if op1 is not None:
            kw["op1"] = op1
        if accum_out is not None:
            kw["accum_out"] = _ap(accum_out)
            wr.append(accum_out)
        x1, x2 = _ap(s1), _ap(s2)
        self.op(eng, lambda h: h.tensor_scalar(o, a, x1, x2, op0, **kw), rd, wr)

    def tt_bc(self, out, in0, sc, ng, op):
        o3 = _ap(out).rearrange("p (s d) -> p s d", s=ng)
        a3 = _ap(in0).rearrange("p (s d) -> p s d", s=ng)
        b3 = _ap(sc).unsqueeze(2).broadcast_to([128, ng, 128])
        self.op("dve", lambda h: h.tensor_tensor(o3, a3, b3, op), [in0, sc], [out])

    def stt(self, out, in0, scalar, in1, op0, op1, eng="dve"):
        o, a, b = _ap(out), _ap(in0), _ap(in1)
        rd = [in0, in1]
        if isinstance(scalar, View):
            rd.append(scalar)
        s = _ap(scalar)
        self.op(eng, lambda h: h.scalar_tensor_tensor(o, a, s, b, op0, op1), rd, [out])

    def copy(self, eng, out, in_):
        o, i = _ap(out), _ap(in_)
        if eng == "act":
            self.op("act", lambda h: h.copy(o, i), [in_], [out])
        else:
            self.op(eng, lambda h: h.tensor_copy(o, i), [in_], [out])

    def memset(self, eng, out, val):
        o = _ap(out)
        self.op(eng, lambda h: h.memset(o, val), [], [out])

    def recip(self, out, in_):
        o, i = _ap(out), _ap(in_)
        self.op("dve", lambda h: h.reciprocal(o, i), [in_], [out])

    def affine_select(self, out, in_, pattern, cmp, fill, base, cm):
        o, i = _ap(out), _ap(in_)
        self.op("pool", lambda h: h.affine_select(o, i, pattern, cmp, fill, base=base, channel_multiplier=cm),
                [in_], [out])

    def check_deadlock(self):
        vals = {}
        pos = {n: 0 for n in self.engs}
        progress = True
        while progress:
            progress = False
            for n, e in self.engs.items():
                while pos[n] < len(e.trace):
                    kind, key, v = e.trace[pos[n]]
                    if kind == "w":
                        if vals.get(key, 0) >= v:
                            pos[n] += 1
                            progress = True
                        else:
                            break
                    else:
                        vals[key] = vals.get(key, 0) + v
                        pos[n] += 1
                        progress = True
        stuck = {n: (pos[n], len(e.trace), e.trace[pos[n]] if pos[n] < len(e.trace) else None)
                 for n, e in self.engs.items()}
        ok = all(pos[n] == len(e.trace) for n, e in self.engs.items())
        return ok, stuck, vals

    def emit(self):
        ok, stuck, vals = self.check_deadlock()
        if not ok:
            raise RuntimeError("DEADLOCK in sync graph: %s" % (stuck,))
        nc = self.nc
        import bisect
        sig = {n: set() for n in self.engs}
        for n, e in self.engs.items():
            for it in e.ops:
                if it[0] == "wait" and it[1] in sig:
                    sig[it[1]].add(it[2])
        sigl = {n: sorted(v) for n, v in sig.items()}
        sems = self.sems

        def run(n, h):
            e = self.engs[n]
            for it in e.ops:
                if it[0] == "wait":
                    key, val = it[1], it[2]
                    if key in sigl:
                        val = bisect.bisect_right(sigl[key], val)
                    h.wait_ge(sems[key], val)
                elif it[0] == "op":
                    ins = it[1](h)
                    if it[2] in sig[n]:
                        ins.then_inc(e.sem, 1)
                elif it[0] == "cc":
                    _, o, s_, key, groups = it
                    h.collective_compute("AllGather", ALU.bypass, replica_groups=groups, ins=[s_], outs=[o]).then_inc(sems[key], 16)
                else:
                    _, o, s_, key, kw = it
                    h.dma_start(out=o, in_=s_, **kw).then_inc(sems[key], 16)
        self.n_inc = {n: len(v) for n, v in sig.items()}
        with nc.Block() as block:
            @block.tensor
            def _(h):
                run("pe", h)

            @block.scalar
            def _(h):
                run("act", h)

            @block.vector
            def _(h):
                run("dve", h)

            @block.gpsimd
            def _(h):
                run("pool", h)

            @block.sync
            def _(h):
                run("sp", h)
        self.es.close()


D = 1024
H = 8
DH = 128
DFF = 2816
PROJ = 4112
EPS = 1e-6
NEG = -30000.0


class M:
    pass


def host_consts():
    r = np.arange(128)[:, None]
    c = np.arange(128)[None, :]
    mats = [r == c, np.ones((128, 128)), r <= c, r > c, c >= r]
    b = 1
    while b < 128:
        mats.append(((r // b) == (c // b) + 1) & ((r // b) % 2 == 1))
        b *= 2
    cf = np.stack([np.asarray(x, np.float32) for x in mats], axis=1)
    import ml_dtypes
    bmats = [r == c, np.ones((128, 128)), -(r >= c).astype(np.float32)]
    cb = np.stack([np.asarray(x, np.float32) for x in bmats], axis=1).astype(ml_dtypes.bfloat16)
    return np.ascontiguousarray(cf), np.ascontiguousarray(cb)


def setup_consts(k, m):
    cfd = k.dram("cf", [128, 12, 128], F32, kind="ExternalInput")
    cbd = k.dram("cb", [128, 3, 128], BF16, kind="ExternalInput")
    cf = k.sbuf([128, 12, 128], F32, "cfs")
    cb = k.sbuf([128, 3, 128], BF16, "cbs")
    k.dma("sp", cf[:], cfd[:])
    k.dma("sp", cb[:], cbd[:])
    m.cf, m.cb = cf, cb

    class V:
        def __init__(self, buf, j):
            self.buf, self.j = buf, j

        def __getitem__(self, idx):
            if idx == slice(None):
                return self.buf[:, self.j, :]
            a, b = idx
            return self.buf[a, self.j, b]
    m.ident = V(cf, 0)
    m.ones = V(cf, 1)
    m.triI = V(cf, 2)
    m.lmask = V(cf, 3)
    m.umask = V(cf, 4)
    m.lvl = [V(cf, 5 + i) for i in range(7)]
    m.identb = V(cb, 0)
    m.onesb = V(cb, 1)
    m.ntriS = V(cb, 2)


class Arena:
    def __init__(self, k, nbytes):
        self.k = k
        self.n32 = nbytes // 4
        self.base = k.es.enter_context(k.nc.sbuf_tensor("arena", [128, self.n32], F32))
        self.off = 0
        self.cnt = 0

    def reset(self):
        self.off = 0

    def alloc(self, shape, dtype, name=None):
        nel = int(np.prod(shape[1:]))
        nb = nel * (2 if dtype == BF16 else 4)
        n32 = (nb + 3) // 4
        n32 = (n32 + 7) // 8 * 8
        assert self.off + n32 <= self.n32, "arena overflow %d + %d > %d" % (self.off, n32, self.n32)
        ap = self.base[0:shape[0], self.off:self.off + n32]
        self.off += n32
        if dtype != F32:
            ap = ap.bitcast(dtype)
        ap = ap[:, 0:nel]
        if len(shape) == 3:
            ap = ap.rearrange("p (a b) -> p a b", a=shape[1])
        elif len(shape) == 4:
            ap = ap.rearrange("p (a b c) -> p a b c", a=shape[1], b=shape[2])
        self.cnt += 1
        return Buf(ap, name or "ar%d" % self.cnt)


def barrier(k):
    latest = {}
    for n, e in k.engs.items():
        if e.count:
            latest[n] = e.count
        for key, v in zip(e.dma_sems, e.dma_vals):
            if v:
                latest[key] = v
    for n, e in k.engs.items():
        for key, v in latest.items():
            if key == n:
                continue
            k._wait(e, key, v)


def new_prog():
    nc = bass.Bass("TRN2", target_bir_lowering=False)
    k = K(nc)
    m = M()
    m.ins = {}
    m.outs = {}
    setup_consts(k, m)
    m.pb = [k.psum([128, 512], F32, "pb%d" % i) for i in range(8)]
    m.pwi = 0

    def pw():
        b = m.pb[m.pwi % 8]
        m.pwi += 1
        return b
    m.pw = pw
    m.arena = Arena(k, 196 * 1024)
    return nc, k, m


def din(k, m, name, shape, dt=F32):
    b = k.dram(name, shape, dt, kind="ExternalInput")
    m.ins[name] = b
    return b


def dout(k, m, name, shape, dt=F32):
    b = k.dram(name, shape, dt, kind="ExternalOutput")
    m.outs[name] = b
    return b


def load_cols(k, m, rows):
    al = m.arena.alloc
    st = al([128, 128], F32)
    out = al([128, 128], F32)
    k.memset("dve", st[:], 0.0)
    r0 = 0
    for v, n in rows:
        k.dma("sp", st[r0:r0 + n, :], v, nowaw=False)
        r0 += n
    p = m.pw()
    k.transpose(p[:, 0:128], st[:], m.ident[:])
    k.copy("dve", out[:], p[:, 0:128])
    return out


def load_w(k, dst, src, kchunks):
    for kc in range(kchunks):
        k.dma("pool", dst[:, kc, :], src(kc), max_dma_last_dim=4096)


def rstd_of(k, junk, ss, src, n):
    k.act(junk[:, 0:n], src, AF.Square, accum_out=ss[:])
    k.ts("dve", ss[:], ss[:], 1.0 / n, ALU.mult, EPS, ALU.add)
    k.act(ss[:], ss[:], AF.Ln)
    k.act(ss[:], ss[:], AF.Exp, scale=-0.5)


def norm_T(k, m, xs, gains, outs):
    for half in range(2):
        p = m.pw()
        for c4 in range(4):
            c = half * 4 + c4
            k.transpose(p[:, c4 * 128:(c4 + 1) * 128], xs[:, c * 128:(c + 1) * 128], m.ident[:])
        for c4 in range(4):
            c = half * 4 + c4
            for g, o in zip(gains, outs):
                k.ts("dve", o[:, c, :], p[:, c4 * 128:(c4 + 1) * 128], g(c), ALU.mult)


D = 1024
H = 8
DFF = 2816
EPS = 1e-6


def build_dense_ffn(TOK):
    nc, k, m = new_prog()
    T = dict(x=din(k, m, "x", [TOK, D]), aT=din(k, m, "aT", [8, 128, TOK], BF16), w_o=din(k, m, "w_o", [D, D]),
             fnorm=din(k, m, "fnorm", [8, 128]), w_gu=din(k, m, "w_gu", [D, 2 * DFF]), w_d=din(k, m, "w_d", [DFF, D]),
             h=dout(k, m, "h", [TOK, D]))
    emit_dense_ffn(k, m, TOK, T)
    k.final_wait("sp", [T["h"]])
    k.emit()
    return nc


def emit_dense_ffn(k, m, TOK, T, xload=None):
    m.arena.reset()
    al = m.arena.alloc
    d_x, d_aT, d_wo, d_fn, d_wgu, d_wd, d_h = T["x"], T["aT"], T["w_o"], T["fnorm"], T["w_gu"], T["w_d"], T["h"]
    cols = load_cols(k, m, [(d_fn[:], 8)])
    w_o = al([128, 8, D], BF16, "w_o")
    w_gu = al([128, 8, 2 * DFF], BF16, "w_gu")
    w_d = al([128, 22, D], BF16, "w_d")
    load_w(k, w_o, lambda kc: d_wo[kc * 128:(kc + 1) * 128, :], 8)
    load_w(k, w_gu, lambda kc: d_wgu[kc * 128:(kc + 1) * 128, :], 8)
    load_w(k, w_d, lambda kc: d_wd[kc * 128:(kc + 1) * 128, :], 22)
    junk = al([128, 1024], BF16)
    ss = al([128, 1], F32)
    xt = [al([128, D], F32, "xt%d" % i) for i in range(1 if xload is not None else 2)]
    at = [al([128, 8, 128], BF16, "at%d" % i) for i in range(2)]
    h1 = [al([128, D], F32, "h1_%d" % i) for i in range(2)]
    h2 = [al([128, D], F32, "h2_%d" % i) for i in range(1 if xload is not None else 2)]
    xtmp = [al([128, D], F32, "xtmp%d" % i) for i in range(2)] if xload is not None else None
    xs = al([128, D], F32, "xs")
    hnT = al([128, 8, 128], BF16, "hnT")
    hidF = al([128, 22 * 128], BF16, "hidT")
    hidT = Buf(hidF.t.rearrange("p (a b) -> p a b", a=22), "hidT3")
    hidT.reads, hidT.writes = hidF.reads, hidF.writes
    sg = [al([128, 256], F32, "sg%d" % i) for i in range(2)]
    aTv = k.track(d_aT.t.rearrange("h p t -> p h t"), "aTv")
    for ti in range(TOK // 128):
        x_t, a_t, h1_, h2_ = xt[ti % len(xt)], at[ti % 2], h1[ti % 2], h2[ti % len(h2)]
        if xload is None:
            k.dma("sp", x_t[:], d_x[ti * 128:(ti + 1) * 128, :])
        else:
            xload(ti, x_t, xtmp)
        k.dma("sp", a_t[:], aTv[:, :, ti * 128:(ti + 1) * 128])
        for half in range(2):
            py = m.pw()
            for h in range(8):
                k.matmul(py[:], a_t[:, h, :], w_o[:, h, half * 512:(half + 1) * 512], start=(h == 0), stop=(h == 7))
            k.tt("dve", h1_[:, half * 512:(half + 1) * 512], py[:], x_t[:, half * 512:(half + 1) * 512], ALU.add)
        rstd_of(k, junk, ss, h1_[:], D)
        k.ts("dve", xs[:], h1_[:], ss[:], ALU.mult)
        norm_T(k, m, xs, [lambda c: cols[:, c:c + 1]], [hnT])
        for j2 in range(11):
            p = m.pw()
            for q in range(4):
                j = j2 * 2 + (q % 2)
                col = (0 if q < 2 else DFF) + j * 128
                for kc in range(8):
                    k.matmul(p[:, q * 128:(q + 1) * 128], w_gu[:, kc, col:col + 128], hnT[:, kc, :],
                             start=(kc == 0), stop=(kc == 7))
            s_ = sg[j2 % 2]
            k.act(s_[:], p[:, 0:256], AF.Silu)
            k.tt("dve", hidF[:, j2 * 256:(j2 + 1) * 256], p[:, 256:512], s_[:], ALU.mult)
        for half in range(2):
            py = m.pw()
            for j in range(22):
                k.matmul(py[:], hidT[:, j, :], w_d[:, j, half * 512:(half + 1) * 512], start=(j == 0), stop=(j == 21))
            k.tt("dve", h2_[:, half * 512:(half + 1) * 512], py[:], h1_[:, half * 512:(half + 1) * 512], ALU.add)
        k.dma("sp", d_h[ti * 128:(ti + 1) * 128, :], h2_[:])


def build_kvq(TOK):
    nc, k, m = new_prog()
    T = dict(h=din(k, m, "h", [TOK, D]), kvn=din(k, m, "kvn", [8, 128]), an=din(k, m, "an", [8, 128]),
             kg=din(k, m, "kg", [1, 128]), qg=din(k, m, "qg", [1, 128]), w_kv=din(k, m, "w_kv", [D, 2 * D]),
             w_q=din(k, m, "w_q", [D, D]), qT=dout(k, m, "qT", [8, 128, TOK], BF16),
             kT=dout(k, m, "kT", [8, 128, TOK], BF16), v=dout(k, m, "v", [TOK, D], BF16))
    outs = emit_kvq(k, m, TOK, T)
    k.final_wait("sp", outs)
    k.emit()
    return nc


def emit_kvq(k, m, TOK, T):
    m.arena.reset()
    al = m.arena.alloc
    d_h, d_kvn, d_an, d_kg, d_qg, d_wkv, d_wq = T["h"], T["kvn"], T["an"], T["kg"], T["qg"], T["w_kv"], T["w_q"]
    d_qT, d_kT, d_v = T["qT"], T["kT"], T["v"]
    cols = load_cols(k, m, [(d_kvn[:], 8), (d_an[:], 8), (d_kg[:], 1), (d_qg[:], 1)])
    w_kv = al([128, 8, 2 * D], BF16, "w_kv")
    w_q = al([128, 8, D], BF16, "w_q")
    load_w(k, w_kv, lambda kc: d_wkv[kc * 128:(kc + 1) * 128, :], 8)
    load_w(k, w_q, lambda kc: d_wq[kc * 128:(kc + 1) * 128, :], 8)
    junk = al([128, 1024], BF16)
    ss = al([128, 1], F32)
    ht = [al([128, D], F32, "ht%d" % i) for i in range(2)]
    xs = al([128, D], F32, "xs")
    xkT = al([128, 8, 128], BF16, "xkT")
    xqT = al([128, 8, 128], BF16, "xqT")
    vt = [al([128, D], BF16, "vt%d" % i) for i in range(2)]
    ss8 = {n: al([128, 8], F32, "ss8" + n) for n in "kq"}
    nrm = {n: al([128, D], F32, "nrm" + n) for n in "kq"}
    oT = {n: [al([128, 8 * 128], BF16, "oT%s%d" % (n, i)) for i in range(2)] for n in "kq"}
    qTv = k.track(d_qT.t.rearrange("h p t -> p h t"), "qTv")
    kTv = k.track(d_kT.t.rearrange("h p t -> p h t"), "kTv")
    qscale = 128.0 ** -0.5
    for ti in range(TOK // 128):
        h_t = ht[ti % 2]
        k.dma("sp", h_t[:], d_h[ti * 128:(ti + 1) * 128, :])
        rstd_of(k, junk, ss, h_t[:], D)
        k.ts("dve", xs[:], h_t[:], ss[:], ALU.mult)
        norm_T(k, m, xs, [lambda c: cols[:, c:c + 1], lambda c: cols[:, 8 + c:9 + c]], [xkT, xqT])
        v_t = vt[ti % 2]
        for n4 in (2, 3):
            py = m.pw()
            for kc in range(8):
                k.matmul(py[:], xkT[:, kc, :], w_kv[:, kc, n4 * 512:(n4 + 1) * 512], start=(kc == 0), stop=(kc == 7))
            k.copy("act", v_t[:, (n4 - 2) * 512:(n4 - 1) * 512], py[:])
        k.dma("sp", d_v[ti * 128:(ti + 1) * 128, :], v_t[:])
        for which, xT, w, gcol, dview in (("k", xkT, w_kv, 16, kTv), ("q", xqT, w_q, 17, qTv)):
            pys = []
            for n4 in range(2):
                py = m.pw()
                for kc in range(8):
                    k.matmul(py[:], xT[:, kc, :], w[:, kc, n4 * 512:(n4 + 1) * 512], start=(kc == 0), stop=(kc == 7))
                pys.append(py)
                for h4 in range(4):
                    hh = n4 * 4 + h4
                    k.act(junk[:, 0:128], py[:, h4 * 128:(h4 + 1) * 128], AF.Square, accum_out=ss8[which][:, hh:hh + 1])
            s8 = ss8[which]
            k.ts("dve", s8[:], s8[:], 1.0 / 128, ALU.mult, EPS, ALU.add)
            k.act(s8[:], s8[:], AF.Ln)
            k.act(s8[:], s8[:], AF.Exp, scale=-0.5)
            nr = nrm[which]
            for hh in range(8):
                k.ts("dve", nr[:, hh * 128:(hh + 1) * 128], pys[hh // 4][:, (hh % 4) * 128:(hh % 4 + 1) * 128],
                     s8[:, hh:hh + 1], ALU.mult)
            o_ = oT[which][ti % 2]
            for n4 in range(2):
                p = m.pw()
                for h4 in range(4):
                    hh = n4 * 4 + h4
                    k.transpose(p[:, h4 * 128:(h4 + 1) * 128], nr[:, hh * 128:(hh + 1) * 128], m.ident[:])
                if which == "k":
                    k.ts("dve", o_[:, n4 * 512:(n4 + 1) * 512], p[:], cols[:, gcol:gcol + 1], ALU.mult)
                else:
                    k.ts("dve", o_[:, n4 * 512:(n4 + 1) * 512], p[:], cols[:, gcol:gcol + 1], ALU.mult, qscale, ALU.mult)
            o3 = k.track(o_.t.rearrange("p (a b) -> p a b", a=8), "o3")
            o3.reads, o3.writes = o_.reads, o_.writes
            k.dma("sp", dview[:, :, ti * 128:(ti + 1) * 128], o3[:])
    return [d_v, qTv, kTv]


def host_masks():
    import ml_dtypes
    s = np.arange(128)[:, None, None]
    jj = np.arange(4)[None, :, None]
    t = np.arange(512)[None, None, :]
    mk = np.where(jj * 128 + s < t, 0.0, NEG).astype(np.float32)
    return np.ascontiguousarray(mk.astype(ml_dtypes.bfloat16))


NEG = -30000.0


def build_attn(S, NH):
    nc, k, m = new_prog()
    T = dict(qT=din(k, m, "qT", [NH, 128, S], BF16), kT=din(k, m, "kT", [NH, 128, S], BF16),
             v=din(k, m, "v", [S, NH * 128], BF16), masks=din(k, m, "masks", [128, 4, 512], BF16),
             oT=dout(k, m, "oT", [NH, 128, S], BF16))
    emit_attn(k, m, S, NH, T)
    k.final_wait("sp", [T["oT"]])
    k.emit()
    return nc


def emit_attn(k, m, S, NH, T, parity=False, sel=None):
    m.arena.reset()
    al = m.arena.alloc
    d_q, d_k, d_v, d_mk, d_o = T["qT"], T["kT"], T["v"], T["masks"], T["oT"]
    NB = S // 128
    NSB = S // 512
    NM = 8 if parity else 4
    masks = al([128, NM, 512], BF16, "masks_sb")
    k.dma("sp", masks[:], d_mk[:])
    qs = [al([128, S], BF16, "qs%d" % i) for i in range(2)]
    ks = [al([128, S], BF16, "ks%d" % i) for i in range(2)]
    vs = [al([128, NB, 128], BF16, "vs%d" % i) for i in range(2)]
    e_ = [al([128, 512], F32, "e%d" % i) for i in range(3)]
    sp_ = [al([128, 512], BF16, "sp%d" % i) for i in range(3)]
    att_ = [al([128, 512], BF16, "att%d" % i) for i in range(2)]
    fac_ = [al([128, 4], F32, "fac%d" % i) for i in range(2)]
    accs = [al([128, 512], F32, "acc%d" % i) for i in range(2)]
    tmp = al([128, 512], F32, "acctmp")
    ot_ = [al([128, 512], BF16, "ot%d" % i) for i in range(2)]
    qsel = [al([128, 512], BF16, "qsel%d" % i) for i in range(2)]
    qtmp = al([128, 512], BF16, "qtmp")
    vview = k.track(d_v.t.rearrange("(blk p) c -> p blk c", p=128), "vview")
    pds = [Buf(m.pb[7].t[:, j * 64:j * 64 + 64], "pd%d" % j) for j in range(8)]
    ring = m.pb[0:7]
    rs = {"i": 0}

    def pw7():
        b_ = ring[rs["i"] % 7]
        rs["i"] += 1
        return b_
    old_pw = m.pw
    m.pw = pw7
    tiles = []
    for h in range(NH):
        for i in range(NSB // 2 if parity else NSB):
            nkb = 4 * (2 * i + 2) if parity else 4 * (i + 1)
            for kb in range(nkb):
                tiles.append((h, i, kb, nkb))
    state = {"cur": None}

    def stage_a(t):
        h, i, kb, nkb = tiles[t]
        q_h, k_h, v_h = qs[h % 2], ks[h % 2], vs[h % 2]
        if i == 0 and kb == 0:
            k.dma("sp", q_h[:], d_q[h, :, :])
            k.dma("sp", k_h[:], d_k[h, :, :])
            step = 16 if NB >= 16 else NB
            for b0 in range(0, NB, step):
                k.dma("sp", v_h[:, b0:b0 + step, :], vview[:, b0:b0 + step, h * 128:(h + 1) * 128])
        if parity:
            qb = qsel[(h * (NSB // 2) + i) % 2]
            if kb == 0:
                k.ts("dve", qtmp[:], q_h[:, (2 * i) * 512:(2 * i + 1) * 512], sel[:, 0:1], ALU.mult)
                k.stt(qb[:], q_h[:, (2 * i + 1) * 512:(2 * i + 2) * 512], sel[:, 1:2], qtmp[:], ALU.mult, ALU.add)
            qv = qb[:]
        else:
            qv = q_h[:, i * 512:(i + 1) * 512]
        jj = kb - (nkb - NM)
        diag = jj >= 0
        kv_ = k_h[:, kb * 128:(kb + 1) * 128]
        e, sp = e_[t % 3], sp_[t % 3]
        pz = m.pw()
        k.matmul(pz[:], kv_, qv, start=True, stop=not diag)
        if diag:
            k.matmul(pz[:], m.identb[:], masks[:, jj, :], start=False, stop=True)
        k.act(e[:], pz[:], AF.Exp)
        k.act(sp[:], e[:], AF.Ln, bias=1.0)
        pd = pds[t % len(pds)]
        return (kv_, qv, diag, jj, sp, v_h, pd, pz)

    def stage_pd(t, a):
        kb = tiles[t][2]
        sp, pd = a[4], a[6]
        if kb > 0:
            for sub in range(4):
                k.matmul(pd[:, sub:sub + 1], sp[:, sub * 128:(sub + 1) * 128], m.onesb[:, 0:1])

    def stage_b(t, a, mid):
        h, i, kb, nkb = tiles[t]
        kv_, qv, diag, jj, sp, v_h, pd, pz = a
        att, fac = att_[t % 2], fac_[t % 2]
        cur = state["cur"]
        nxt = accs[0] if cur is not accs[0] else accs[1]
        pl = pz
        k.matmul(pl[:], m.ntriS[:], sp[:], start=False, stop=True)
        mid()
        if kb > 0:
            k.act(fac[:], pd[:, 0:4], AF.Exp, scale=-1.0)
            k.tt_bc(tmp[:], cur[:], fac[:, 0:4], 4, ALU.mult)
        k.act(att[:], pl[:], AF.Exp)
        pP = m.pw()
        for sub in range(4):
            k.matmul(pP[:, sub * 128:(sub + 1) * 128], att[:, sub * 128:(sub + 1) * 128], v_h[:, kb, :])
        if kb == 0:
            k.copy("dve", nxt[:], pP[:])
        else:
            k.tt("dve", nxt[:], pP[:], tmp[:], ALU.add)
        state["cur"] = nxt
        if kb == nkb - 1:
            pT = m.pw()
            for sub in range(4):
                k.transpose(pT[:, sub * 128:(sub + 1) * 128], nxt[:, sub * 128:(sub + 1) * 128], m.ident[:])
            o_ = ot_[i % 2]
            k.copy("dve", o_[:], pT[:])
            k.dma("sp", d_o[h, :, i * 512:(i + 1) * 512], o_[:])

    LA = 2
    nt = len(tiles)
    A = {}
    for t0 in range(min(LA, nt)):
        A[t0] = stage_a(t0)
    stage_pd(0, A[0])
    for t in range(nt):
        if t + LA < nt:
            A[t + LA] = stage_a(t + LA)
        if t + 1 < nt:
            stage_b(t, A[t], lambda: stage_pd(t + 1, A[t + 1]))
        else:
            stage_b(t, A[t], lambda: None)
        del A[t]
    m.pw = old_pw


def build_gdn(S, NH=4, NGRP=1):
    nc, k, m = new_prog()
    W = NH * 128
    T = dict(x=din(k, m, "x", [S, D]), an=din(k, m, "an", [8, 128]), w_qkvz=din(k, m, "w_qkvz", [D, NGRP * 4 * W]),
             w_bg=din(k, m, "w_bg", [D, NGRP * 2 * NH]), conv=din(k, m, "conv", [4, NGRP * 3 * W]),
             alog=din(k, m, "alog", [1, NGRP * NH]), dtb=din(k, m, "dtb", [1, NGRP * NH]),
             ogain=din(k, m, "ogain", [1, 128]), ogT=dout(k, m, "ogT", [NGRP * NH, 128, S], BF16))
    ov = emit_gdn(k, m, S, T, NH, NGRP)
    print("gdn arena words used", m.arena.off, "of", m.arena.n32)
    k.final_wait("sp", [ov])
    k.emit()
    return nc


def emit_gdn(k, m, S, T, NH=4, NGRP=1):
    m.arena.reset()
    al = m.arena.alloc
    W = NH * 128
    d_x, d_an, d_w, d_wbg, d_conv, d_alog, d_dtb, d_og, d_o = (T["x"], T["an"], T["w_qkvz"], T["w_bg"], T["conv"],
                                                                  T["alog"], T["dtb"], T["ogain"], T["ogT"])
    NG = 3 * NH
    cols = load_cols(k, m, [(d_an[:], 8), (d_og[:], 1)])
    convc = load_cols(k, m, [(k.track(d_conv.t.rearrange("t (j p) -> (t j) p", p=128), "cwv")[:], 4 * NG * NGRP)])
    w = al([128, 8, NGRP * 4 * W], BF16, "w")
    wbg = al([128, 8, NGRP * 2 * NH], BF16, "wbg")
    load_w(k, w, lambda kc: d_w[kc * 128:(kc + 1) * 128, :], 8)
    load_w(k, wbg, lambda kc: d_wbg[kc * 128:(kc + 1) * 128, :], 8)
    alog = al([128, NGRP * NH], F32, "alog")
    dtb = al([128, NGRP * NH], F32, "dtb")
    k.dma("sp", alog[:], k.track(d_alog.t[0, :].partition_broadcast(128), "alv")[:])
    k.dma("sp", dtb[:], k.track(d_dtb.t[0, :].partition_broadcast(128), "dtv")[:])
    nexpA = al([128, NGRP * NH], F32, "nexpA")
    k.act(nexpA[:], alog[:], AF.Exp)
    k.ts("dve", nexpA[:], nexpA[:], -1.0, ALU.mult)
    ident4 = al([128, W], F32, "ident4")
    umask4 = al([128, W], F32, "umask4")
    lvl4 = [al([128, W], BF16, "lvl4_%d" % i) for i in range(7)]
    for h in range(NH):
        sl = slice(h * 128, (h + 1) * 128)
        k.copy("dve", ident4[:, sl], m.ident[:])
        k.copy("dve", umask4[:, sl], m.umask[:])
        for i in range(7):
            k.copy("dve", lvl4[i][:, sl], m.lvl[i][:])
    junk = al([128, 1024], BF16)
    ss = al([128, 1], F32)
    xt = [al([128, D], F32, "xt%d" % i) for i in range(2)]
    xs = al([128, D], F32, "xs")
    xnT = al([128, 8, 128], BF16, "xnT")
    cbs = [[al([128, NH, 131], BF16, "cb%d_%d" % (G, g)) for g in range(3)] for G in range(NGRP)]
    dgw = al([128, 4 * NG * NGRP, 128], BF16, "dgw")
    for c_ in range(4 * NG * NGRP):
        k.ts("dve", dgw[:, c_, :], m.identb[:], convc[:, c_:c_ + 1], ALU.mult)
    for G in range(NGRP):
        for g in range(3):
            k.memset("dve", cbs[G][g][:], 0.0)
    F2 = lambda n, dt=F32: al([128, W], dt, n)
    Ssts = [F2("S%d" % G) for G in range(NGRP)]
    Sbs = [F2("Sb%d" % G, BF16) for G in range(NGRP)]
    for G in range(NGRP):
        k.memset("dve", Ssts[G][:], 0.0)
        k.memset("dve", Sbs[G][:], 0.0)
    NSLOT = 2 if (NGRP >= 2 and NH <= 2) else 1

    def mk_slot(si):
        F = lambda n, dt=F32: al([128, W], dt, "%s_s%d" % (n, si))
        sl = {}
        sl["sil"] = [F("sil%d" % g_) for g_ in range(3)]
        for n in ("zs", "sq2", "ri", "kTf", "vtm", "ktm", "dg", "tmpF", "Fm", "FU", "Am", "u", "t1", "o_", "on", "St"):
            sl[n] = F(n)
        for n in ("qT", "kT", "Pa", "Pb", "Xa", "Xb", "Y", "attnT", "vb", "kb", "kd", "wT", "vnew"):
            sl[n] = F(n, BF16)
        sl["L"] = [F("L%d" % i, BF16) for i in range(7)]
        sl["ogt"] = [F("ogt%d" % i, BF16) for i in range(2)]
        sl["sm"] = {n: al([128, NH], F32, "%s_s%d" % (n, si)) for n in
                    ("beta", "g", "gc", "ngc", "glast", "egc", "ekd", "eglast", "sckb", "tmp8", "ss4")}
        return sl
    slots = [mk_slot(si) for si in range(NSLOT)]
    oview = k.track(d_o.t.rearrange("h p t -> p h t"), "oview")
    qscale = float(np.log(128.0 ** -0.5))
    HS = [slice(h * 128, (h + 1) * 128) for h in range(NH)]

    def mm4(p, lhs_of, rhs_of):
        for h in range(NH):
            k.matmul(p[:, HS[h]], lhs_of(h), rhs_of(h))

    for ci in range(S // 128):
        x_t = xt[ci % 2]
        k.dma("sp", x_t[:], d_x[ci * 128:(ci + 1) * 128, :])
        rstd_of(k, junk, ss, x_t[:], D)
        k.ts("dve", xs[:], x_t[:], ss[:], ALU.mult)
        norm_T(k, m, xs, [lambda c: cols[:, c:c + 1]], [xnT])
        def gbody(G, sl):
            cb, Sst, Sb = cbs[G], Ssts[G], Sbs[G]
            dtbG, nexpAG = dtb[:, G * NH:(G + 1) * NH], nexpA[:, G * NH:(G + 1) * NH]
            sil, sm, L, ogt = sl["sil"], sl["sm"], sl["L"], sl["ogt"]
            zs, sq2, ri, kTf, vtm, ktm, dg, tmpF, Fm, FU, Am = (sl[n] for n in ("zs", "sq2", "ri", "kTf", "vtm", "ktm", "dg", "tmpF", "Fm", "FU", "Am"))
            u, t1, o_, on, St = (sl[n] for n in ("u", "t1", "o_", "on", "St"))
            qT, kT, Pa, Pb, Xa, Xb, Y, attnT, vb, kb, kd, wT, vnew = (sl[n] for n in ("qT", "kT", "Pa", "Pb", "Xa", "Xb", "Y", "attnT", "vb", "kb", "kd", "wT", "vnew"))
            yield
            banks = []
            for g in range(4):
                p = m.pw()
                for h in range(NH):
                    fc = g * NH + h
                    for kc in range(8):
                        k.matmul(p[:, HS[h]], w[:, kc, G * 4 * W + fc * 128:G * 4 * W + (fc + 1) * 128], xnT[:, kc, :], start=(kc == 0), stop=(kc == 7))
                banks.append(p)
            k.act(zs[:], banks[3][:, 0:W], AF.Silu)
            for g in range(3):
                for h in range(NH):
                    k.copy("act", cb[g][:, h, 3:131], banks[g][:, HS[h]])
            for g in range(3):
                pc = m.pw()
                for h in range(NH):
                    j = g * NH + h
                    for t in range(4):
                        k.matmul(pc[:, HS[h]], dgw[:, t * NG * NGRP + G * NG + j, :], cb[g][:, h, t:t + 128],
                                 start=(t == 0), stop=(t == 3))
                for h in range(NH):
                    k.copy("act", cb[g][:, h, 0:3], cb[g][:, h, 128:131])
                k.act(sil[g][:], pc[:, 0:W], AF.Silu)
            yield
            pbg = m.pw()
            for kc in range(8):
                k.matmul(pbg[:, 0:2 * NH], xnT[:, kc, :], wbg[:, kc, G * 2 * NH:(G + 1) * 2 * NH], start=(kc == 0), stop=(kc == 7))
            k.act(sm["beta"][:], pbg[:, 0:NH], AF.Exp, scale=-1.0)
            k.ts("dve", sm["beta"][:], sm["beta"][:], 1.0, ALU.add)
            k.recip(sm["beta"][:], sm["beta"][:])
            k.tt("dve", sm["g"][:], pbg[:, NH:2 * NH], dtbG, ALU.add)
            k.act(sm["g"][:], sm["g"][:], AF.Exp)
            k.act(sm["g"][:], sm["g"][:], AF.Ln, bias=1.0)
            k.tt("dve", sm["g"][:], sm["g"][:], nexpAG, ALU.mult)
            pg = m.pw()
            k.matmul(pg[:, 0:NH], m.triI[:], sm["g"][:])
            k.matmul(pg[:, 8:8 + NH], m.ones[:], sm["g"][:])
            k.copy("dve", sm["gc"][:], pg[:, 0:NH])
            k.copy("dve", sm["glast"][:], pg[:, 8:8 + NH])
            k.ts("dve", sm["ngc"][:], sm["gc"][:], -1.0, ALU.mult)
            k.act(sm["egc"][:], sm["gc"][:], AF.Exp)
            k.act(sm["eglast"][:], sm["glast"][:], AF.Exp)
            k.tt("dve", sm["tmp8"][:], sm["glast"][:], sm["gc"][:], ALU.subtract)
            k.act(sm["ekd"][:], sm["tmp8"][:], AF.Exp)
            k.tt("dve", sm["sckb"][:], sm["beta"][:], sm["egc"][:], ALU.mult)
            yield
            for g in range(2):
                k.act(sq2[:], sil[g][:], AF.Square)
                ps = m.pw()
                k.matmul(ps[:, 0:W], m.ones[:], sq2[:])
                k.act(ri[:], ps[:, 0:W], AF.Ln, bias=EPS)
                if g == 0:
                    k.act(ri[:], ri[:], AF.Exp, scale=-0.5, bias=qscale)
                    k.tt("dve", qT[:], sil[0][:], ri[:], ALU.mult)
                else:
                    k.act(ri[:], ri[:], AF.Exp, scale=-0.5)
                    k.tt("dve", kTf[:], sil[1][:], ri[:], ALU.mult)
                    k.copy("act", kT[:], kTf[:])
            pt = m.pw()
            for h in range(NH):
                k.transpose(pt[:, HS[h]], kTf[:, HS[h]], m.ident[:])
            k.copy("dve", ktm[:], pt[:, 0:W])
            pt = m.pw()
            for h in range(NH):
                k.transpose(pt[:, HS[h]], sil[2][:, HS[h]], m.ident[:])
            k.copy("dve", vtm[:], pt[:, 0:W])
            yield
            k.tt_bc(dg[:], ident4[:], sm["gc"][:, 0:NH], NH, ALU.mult)
            pR = m.pw()
            k.matmul(pR[:, 0:W], m.ones[:], dg[:])
            for h in range(NH):
                k.act(tmpF[:, HS[h]], pR[:, HS[h]], AF.Abs, bias=sm["ngc"][:, h:h + 1])
            k.act(Fm[:], tmpF[:], AF.Exp, scale=-1.0)
            k.tt("dve", FU[:], Fm[:], umask4[:], ALU.mult)
            pKK = m.pw()
            mm4(pKK, lambda h: kT[:, HS[h]], lambda h: kT[:, HS[h]])
            for h in range(NH):
                k.stt(Am[:, HS[h]], pKK[:, HS[h]], sm["beta"][:, h:h + 1], Fm[:, HS[h]], ALU.mult, ALU.mult)
            pQK = m.pw()
            mm4(pQK, lambda h: kT[:, HS[h]], lambda h: qT[:, HS[h]])
            k.tt("dve", attnT[:], pQK[:, 0:W], FU[:], ALU.mult)
            for i in range(7):
                k.tt("dve", L[i][:], Am[:], lvl4[i][:], ALU.mult)
            yield
            pY = m.pw()
            mm4(pY, lambda h: L[0][:, HS[h]], lambda h: m.identb[:])
            Pc, Pn, Xc, Xn = Pa, Pb, Xa, Xb
            k.stt(Pc[:], pY[:, 0:W], -1.0, ident4[:], ALU.mult, ALU.add)
            k.tt("dve", Xc[:], ident4[:], L[0][:], ALU.subtract)
            for i in range(1, 7):
                pY = m.pw()
                mm4(pY, lambda h: L[i][:, HS[h]], lambda h: Pc[:, HS[h]])
                k.copy("act", Y[:], pY[:, 0:W])
                pZ = m.pw()
                mm4(pZ, lambda h: Xc[:, HS[h]], lambda h: Y[:, HS[h]])
                if i < 6:
                    pZT = m.pw()
                    mm4(pZT, lambda h: Y[:, HS[h]], lambda h: Xc[:, HS[h]])
                k.stt(Pn[:], pZ[:, 0:W], -1.0, Pc[:], ALU.mult, ALU.add)
                Pc, Pn = Pn, Pc
                if i < 6:
                    k.stt(Xn[:], pZT[:, 0:W], -1.0, Xc[:], ALU.mult, ALU.add)
                    Xc, Xn = Xn, Xc
                yield
            yield
            k.tt_bc(vb[:], vtm[:], sm["beta"][:, 0:NH], NH, ALU.mult)
            k.tt_bc(kb[:], ktm[:], sm["sckb"][:, 0:NH], NH, ALU.mult)
            k.tt_bc(kd[:], ktm[:], sm["ekd"][:, 0:NH], NH, ALU.mult)
            pu = m.pw()
            mm4(pu, lambda h: Pc[:, HS[h]], lambda h: vb[:, HS[h]])
            k.copy("dve", u[:], pu[:, 0:W])
            pw_ = m.pw()
            mm4(pw_, lambda h: kb[:, HS[h]], lambda h: Pc[:, HS[h]])
            k.copy("dve", wT[:], pw_[:, 0:W])
            yield
            pws = m.pw()
            mm4(pws, lambda h: wT[:, HS[h]], lambda h: Sb[:, HS[h]])
            k.stt(vnew[:], pws[:, 0:W], -1.0, u[:], ALU.mult, ALU.add)
            po1 = m.pw()
            mm4(po1, lambda h: qT[:, HS[h]], lambda h: Sb[:, HS[h]])
            po2 = m.pw()
            mm4(po2, lambda h: attnT[:, HS[h]], lambda h: vnew[:, HS[h]])
            k.tt_bc(t1[:], po1[:, 0:W], sm["egc"][:, 0:NH], NH, ALU.mult)
            k.tt("dve", o_[:], po2[:, 0:W], t1[:], ALU.add)
            pS = m.pw()
            mm4(pS, lambda h: kd[:, HS[h]], lambda h: vnew[:, HS[h]])
            k.tt_bc(St[:], Sst[:], sm["eglast"][:, 0:NH], NH, ALU.mult)
            k.tt("dve", Sst[:], pS[:, 0:W], St[:], ALU.add)
            k.copy("act", Sb[:], Sst[:])
            yield
            for h in range(NH):
                k.act(junk[:, 0:128], o_[:, HS[h]], AF.Square, accum_out=sm["ss4"][:, h:h + 1])
            k.ts("dve", sm["ss4"][:], sm["ss4"][:], 1.0 / 128, ALU.mult, EPS, ALU.add)
            k.act(sm["ss4"][:], sm["ss4"][:], AF.Ln)
            k.act(sm["ss4"][:], sm["ss4"][:], AF.Exp, scale=-0.5)
            k.tt_bc(on[:], o_[:], sm["ss4"][:, 0:NH], NH, ALU.mult)
            pT = m.pw()
            for h in range(NH):
                k.transpose(pT[:, HS[h]], on[:, HS[h]], m.ident[:])
            og = ogt[(ci * NGRP + G) % 2]
            k.stt(og[:], pT[:, 0:W], cols[:, 8:9], zs[:], ALU.mult, ALU.mult)
            og3 = k.track(og.t.rearrange("p (a b) -> p a b", a=NH), "og3")
            og3.reads, og3.writes = og.reads, og.writes
            k.dma("sp", oview[:, G * NH:(G + 1) * NH, ci * 128:(ci + 1) * 128], og3[:])

        for G0 in range(0, NGRP, NSLOT):
            gens = [gbody(G0 + si, slots[si]) for si in range(NSLOT)]
            alive = list(gens)
            while alive:
                for gen_ in list(alive):
                    try:
                        next(gen_)
                    except StopIteration:
                        alive.remove(gen_)
    return oview


GDN_HPG = 4


def build_fused(S):
    nc, k, m = new_prog()
    NPOS = S // 1024
    TOKO = NPOS * 512
    I = lambda n, shp, dt=F32: din(k, m, n, shp, dt)
    x = I("x", [S, D])
    Tg = dict(x=x, an=I("an0", [8, 128]), w_qkvz=I("w_qkvz", [D, 4096]), w_bg=I("w_bg", [D, 16]),
              conv=I("conv", [4, 3072]), alog=I("alog", [1, 8]), dtb=I("dtb", [1, 8]), ogain=I("ogain", [1, 128]))
    ogT = k.dram("ogT_scr", [8, 128, S], BF16)
    Tg["ogT"] = ogT
    h2 = k.dram("h2_scr", [S, D], F32)
    Tf0 = dict(x=x, aT=ogT, w_o=I("gdn_w_out", [D, D]), fnorm=I("fn0", [8, 128]), w_gu=I("w_gu0", [D, 2 * DFF]),
               w_d=I("w_d0", [DFF, D]), h=h2)
    qT = k.dram("qT_scr", [8, 128, S], BF16)
    kT = k.dram("kT_scr", [8, 128, S], BF16)
    v = k.dram("v_scr", [S, D], BF16)
    Tk = dict(h=h2, kvn=I("kvn", [8, 128]), an=I("an1", [8, 128]), kg=I("kg", [1, 128]), qg=I("qg", [1, 128]),
              w_kv=I("w_kv", [D, 2 * D]), w_q=I("w_q", [D, D]), qT=qT, kT=kT, v=v)
    oTs = k.dram("oT_scr", [8, 128, TOKO], BF16)
    Ta = dict(qT=qT, kT=kT, v=v, masks=I("masks", [128, 8, 512], BF16), oT=oTs)
    d_sel = I("sel", [128, 2])
    out = dout(k, m, "out", [TOKO, D])
    Tf1 = dict(x=None, aT=oTs, w_o=I("sb_w_out", [D, D]), fnorm=I("fn1", [8, 128]), w_gu=I("w_gu1", [D, 2 * DFF]),
               w_d=I("w_d1", [DFF, D]), h=out)
    sel = k.sbuf([128, 2], F32, "sel_sb")
    k.dma("sp", sel[:], d_sel[:])

    emit_gdn(k, m, S, Tg, GDN_HPG, 8 // GDN_HPG)
    barrier(k)
    emit_dense_ffn(k, m, S, Tf0)
    barrier(k)
    emit_kvq(k, m, S, Tk)
    barrier(k)
    emit_attn(k, m, S, 8, Ta, parity=True, sel=sel)
    barrier(k)

    def xload(ti, x_t, xtmp):
        j, r = ti // 4, ti % 4
        ra = (2 * j) * 512 + r * 128
        rb = (2 * j + 1) * 512 + r * 128
        k.dma("sp", xtmp[0][:], h2[ra:ra + 128, :])
        k.dma("sp", xtmp[1][:], h2[rb:rb + 128, :])
        k.ts("dve", xtmp[0][:], xtmp[0][:], sel[:, 0:1], ALU.mult)
        k.stt(x_t[:], xtmp[1][:], sel[:, 1:2], xtmp[0][:], ALU.mult, ALU.add)
    emit_dense_ffn(k, m, TOKO, Tf1, xload=xload)
    k.final_wait("sp", [out])
    k.emit()
    return nc


def host_parity_masks(p):
    c = host_masks()
    z = np.zeros_like(c)
    f = np.full_like(c, NEG)
    return _c(np.concatenate([z, c], axis=1) if p == 1 else np.concatenate([c, f], axis=1))


def kernel_fused(inp, x):
    B, S, _ = x.shape
    cores = list(range(2 * B))
    cf, cb = host_consts()
    w_in = inp["gdn_w_in"][0]
    cw = inp["gdn_conv_w"][0]

    def grp(a, width, hpg=GDN_HPG):
        nblk = a.shape[1] // (8 * width)
        parts = []
        for G in range(8 // hpg):
            for blk in range(nblk):
                parts.append(a[:, blk * 8 * width + G * hpg * width: blk * 8 * width + (G + 1) * hpg * width])
        return _c(np.concatenate(parts, axis=1))
    common = dict(cf=cf, cb=cb, an0=_c(inp["attn_norm"][0].reshape(8, 128)), w_qkvz=grp(w_in[:, 0:4096], 128),
                  w_bg=grp(w_in[:, 4096:4112], 1), conv=grp(cw, 128), alog=_c(inp["gdn_a_log"].reshape(1, 8)),
                  dtb=_c(inp["gdn_dt_bias"].reshape(1, 8)), ogain=_c(inp["gdn_o_gain"].reshape(1, 128)),
                  gdn_w_out=_c(inp["gdn_w_out"][0]), fn0=_c(inp["ffn_norm"][0].reshape(8, 128)),
                  w_gu0=_c(inp["ffn_w_gu"][0]), w_d0=_c(inp["ffn_w_down"][0]),
                  kvn=_c(inp["kv_norm"].reshape(8, 128)), an1=_c(inp["attn_norm"][1].reshape(8, 128)),
                  kg=_c(inp["k_gain"].reshape(1, 128)), qg=_c(inp["sb_q_gain"].reshape(1, 128)),
                  w_kv=_c(inp["w_kv"]), w_q=_c(inp["sb_w_q"][0]), sb_w_out=_c(inp["sb_w_out"][0]),
                  fn1=_c(inp["ffn_norm"][1].reshape(8, 128)), w_gu1=_c(inp["ffn_w_gu"][1]), w_d1=_c(inp["ffn_w_down"][1]))
    feeds = []
    for c in cores:
        b, p = c // 2, c % 2
        selv = np.empty((128, 2), np.float32)
        selv[:, 0] = 1.0 - p
        selv[:, 1] = float(p)
        feeds.append(dict(common, x=_c(x[b]), masks=host_parity_masks(p), sel=selv))
    res = run_bass_kernel_spmd(_prog("fused", build_fused, S), feeds, core_ids=cores).results
    out = np.empty((B, S, D), np.float32)
    for c in cores:
        b, p = c // 2, c % 2
        o = res[c]["out"]
        for j in range(S // 1024):
            out[b, (2 * j + p) * 512:(2 * j + p + 1) * 512] = o[j * 512:(j + 1) * 512]
    return out

from concourse.bass_utils import run_bass_kernel_spmd

_PROGS = {}


def _prog(name, fn, *args):
    key = (name,) + args
    if key not in _PROGS:
        _PROGS[key] = fn(*args)
    return _PROGS[key]


def _c(a):
    return np.ascontiguousarray(a)


def kernel_unfused(**inputs):
    inp = {n: np.asarray(v) for n, v in inputs.items()}
    x = inp["x"].astype(np.float32, copy=False)
    B, S, _ = x.shape
    NC = 8
    HALF = S // 2
    cores = list(range(NC))
    cf, cb = host_consts()
    cst = {"cf": cf, "cb": cb}

    w_in = inp["gdn_w_in"][0]
    cw = inp["gdn_conv_w"][0]
    feeds = []
    for c in cores:
        b, hg = c // 2, c % 2
        hs = slice(hg * 512, (hg + 1) * 512)
        h4 = slice(hg * 4, (hg + 1) * 4)
        feeds.append(dict(cst, x=_c(x[b]), an=_c(inp["attn_norm"][0].reshape(8, 128)),
                          w_qkvz=_c(np.concatenate([w_in[:, 0:1024][:, hs], w_in[:, 1024:2048][:, hs],
                                                    w_in[:, 2048:3072][:, hs], w_in[:, 3072:4096][:, hs]], axis=1)),
                          w_bg=_c(np.concatenate([w_in[:, 4096:4104][:, h4], w_in[:, 4104:4112][:, h4]], axis=1)),
                          conv=_c(np.concatenate([cw[:, 0:1024][:, hs], cw[:, 1024:2048][:, hs], cw[:, 2048:3072][:, hs]], axis=1)),
                          alog=_c(inp["gdn_a_log"][:, h4]), dtb=_c(inp["gdn_dt_bias"][:, h4]),
                          ogain=_c(inp["gdn_o_gain"].reshape(1, 128))))
    r1 = run_bass_kernel_spmd(_prog("gdn", build_gdn, S, 4), feeds, core_ids=cores).results
    ogT = [np.concatenate([r1[2 * b]["ogT"], r1[2 * b + 1]["ogT"]], axis=0) for b in range(B)]

    nc_ffn = _prog("ffn", build_dense_ffn, HALF)
    feeds = []
    for c in cores:
        b, hf = c // 2, c % 2
        ts = slice(hf * HALF, (hf + 1) * HALF)
        feeds.append(dict(cst, x=_c(x[b, ts]), aT=_c(ogT[b][:, :, ts]), w_o=_c(inp["gdn_w_out"][0]),
                          fnorm=_c(inp["ffn_norm"][0].reshape(8, 128)), w_gu=_c(inp["ffn_w_gu"][0]),
                          w_d=_c(inp["ffn_w_down"][0])))
    r2 = run_bass_kernel_spmd(nc_ffn, feeds, core_ids=cores).results
    h2 = [r["h"] for r in r2]

    feeds = []
    for c in cores:
        feeds.append(dict(cst, h=_c(h2[c]), kvn=_c(inp["kv_norm"].reshape(8, 128)),
                          an=_c(inp["attn_norm"][1].reshape(8, 128)), kg=_c(inp["k_gain"].reshape(1, 128)),
                          qg=_c(inp["sb_q_gain"].reshape(1, 128)), w_kv=_c(inp["w_kv"]), w_q=_c(inp["sb_w_q"][0])))
    r3 = run_bass_kernel_spmd(_prog("kvq", build_kvq, HALF), feeds, core_ids=cores).results

    mk = host_masks()
    feeds = []
    for c in cores:
        b, hg = c // 2, c % 2
        h4 = slice(hg * 4, (hg + 1) * 4)
        qT = np.concatenate([r3[2 * b]["qT"][h4], r3[2 * b + 1]["qT"][h4]], axis=2)
        kT = np.concatenate([r3[2 * b]["kT"][h4], r3[2 * b + 1]["kT"][h4]], axis=2)
        v = np.concatenate([r3[2 * b]["v"], r3[2 * b + 1]["v"]], axis=0)[:, hg * 512:(hg + 1) * 512]
        feeds.append(dict(cst, qT=_c(qT), kT=_c(kT), v=_c(v), masks=mk))
    r4 = run_bass_kernel_spmd(_prog("attn", build_attn, S, 4), feeds, core_ids=cores).results
    oT = [np.concatenate([r4[2 * b]["oT"], r4[2 * b + 1]["oT"]], axis=0) for b in range(B)]

    feeds = []
    for c in cores:
        b, hf = c // 2, c % 2
        ts = slice(hf * HALF, (hf + 1) * HALF)
        feeds.append(dict(cst, x=_c(h2[c]), aT=_c(oT[b][:, :, ts]), w_o=_c(inp["sb_w_out"][0]),
                          fnorm=_c(inp["ffn_norm"][1].reshape(8, 128)), w_gu=_c(inp["ffn_w_gu"][1]),
                          w_d=_c(inp["ffn_w_down"][1])))
    r5 = run_bass_kernel_spmd(_prog("ffn_b", build_dense_ffn, HALF), feeds, core_ids=cores).results
    out = np.empty((B, S, D), np.float32)
    for c in cores:
        b, hf = c // 2, c % 2
        out[b, hf * HALF:(hf + 1) * HALF] = r5[c]["h"]
    return out


def kernel(**inputs):
    inp = {n: np.asarray(v) for n, v in inputs.items()}
    x = inp["x"].astype(np.float32, copy=False)
    return kernel_fused(inp, x)
```

```python
from contextlib import ExitStack
import numpy as np
import concourse.bass as bass
import concourse.mybir as mybir

F32 = mybir.dt.float32
BF16 = mybir.dt.bfloat16
ALU = mybir.AluOpType
AF = mybir.ActivationFunctionType
AX = mybir.AxisListType

SAME_ENGINE_SYNC = True
N_DMA_SEMS = 6


class Buf:
    def __init__(self, t, name):
        self.t = t
        self.name = name
        self.reads = {}
        self.writes = {}

    def __getitem__(self, idx):
        return View(self, self.t[idx])


class View:
    def __init__(self, buf, ap):
        self.buf = buf
        self.ap = ap


def _ap(x):
    return x.ap if isinstance(x, View) else x


class Eng:
    def __init__(self, name):
        self.name = name
        self.ops = []
        self.trace = []
        self.count = 0
        self.waited = {}
        self.sem = None
        self.dma_sems = []
        self.dma_vals = []
        self.dma_k = 0


class K:
    def __init__(self, nc):
        self.nc = nc
        self.es = ExitStack()
        self.engs = {n: Eng(n) for n in ("pe", "act", "dve", "pool", "sp")}
        self.sems = {}
        for n, e in self.engs.items():
            e.sem = self.es.enter_context(nc.semaphore("s_" + n))
            self.sems[n] = e.sem
        for n in ("sp", "pool", "act"):
            e = self.engs[n]
            for i in range(N_DMA_SEMS):
                s = self.es.enter_context(nc.semaphore("d_%s%d" % (n, i)))
                key = "d_%s%d" % (n, i)
                self.sems[key] = s
                e.dma_sems.append(key)
                e.dma_vals.append(0)
        self.nbuf = 0

    def sbuf(self, shape, dtype, name=None):
        self.nbuf += 1
        name = name or "sb%d" % self.nbuf
        t = self.es.enter_context(self.nc.sbuf_tensor(name, list(shape), dtype))
        return Buf(t, name)

    def psum(self, shape, dtype, name=None):
        self.nbuf += 1
        name = name or "ps%d" % self.nbuf
        t = self.es.enter_context(self.nc.psum_tensor(name, list(shape), dtype))
        return Buf(t, name)

    def dram(self, name, shape, dtype, kind="Internal"):
        t = self.nc.dram_tensor(name, list(shape), dtype, kind=kind)
        return Buf(t.ap(), name)

    def track(self, ap, name):
        return Buf(ap, name)

    def _wait(self, e, key, val, war=False):
        if key == e.name and (e.name == "pe" or war or not SAME_ENGINE_SYNC):
            return
        if e.waited.get(key, 0) >= val:
            return
        e.waited[key] = val
        sem = self.sems[key]
        e.ops.append(("wait", key, val))
        e.trace.append(("w", key, val))

    def _deps(self, e, reads, writes, nowaw=False):
        for v in reads:
            if isinstance(v, View):
                for k, val in v.buf.writes.items():
                    self._wait(e, k, val)
        for v in writes:
            if isinstance(v, View):
                for k, val in v.buf.reads.items():
                    self._wait(e, k, val, war=True)
                if not nowaw:
                    for k, val in v.buf.writes.items():
                        self._wait(e, k, val)

    def _mark(self, key, val, reads, writes):
        for v in reads:
            if isinstance(v, View):
                if v.buf.reads.get(key, 0) < val:
                    v.buf.reads[key] = val
        for v in writes:
            if isinstance(v, View):
                if v.buf.writes.get(key, 0) < val:
                    v.buf.writes[key] = val

    def op(self, eng, fn, reads, writes, nowaw=False):
        e = self.engs[eng]
        self._deps(e, reads, writes, nowaw)
        e.count += 1
        sem = e.sem
        e.ops.append(("op", fn, e.count))
        e.trace.append(("i", e.name, 1))
        self._mark(e.name, e.count, reads, writes)

    def dma(self, eng, out, in_, nowaw=True, **kw):
        e = self.engs[eng]
        self._deps(e, [in_], [out], nowaw)
        i = e.dma_k % N_DMA_SEMS
        e.dma_k += 1
        key = e.dma_sems[i]
        if e.dma_vals[i] > 0:
            self._wait(e, key, e.dma_vals[i])
        e.dma_vals[i] += 16
        val = e.dma_vals[i]
        sem = self.sems[key]
        o, s = _ap(out), _ap(in_)
        e.ops.append(("dma", o, s, key, kw))
        e.trace.append(("i", key, 16))
        self._mark(key, val, [in_], [out])
        return (key, val)

    def allgather(self, out, in_, groups):
        e = self.engs["pool"]
        self._deps(e, [in_], [out], False)
        i = e.dma_k % N_DMA_SEMS
        e.dma_k += 1
        key = e.dma_sems[i]
        if e.dma_vals[i] > 0:
            self._wait(e, key, e.dma_vals[i])
        e.dma_vals[i] += 16
        val = e.dma_vals[i]
        o, s_ = _ap(out), _ap(in_)
        e.ops.append(("cc", o, s_, key, groups))
        e.trace.append(("i", key, 16))
        self._mark(key, val, [in_], [out])

    def final_wait(self, eng, bufs):
        e = self.engs[eng]
        for b in bufs:
            for k, val in b.writes.items():
                self._wait(e, k, val)

    def matmul(self, out, lhsT, rhs, start=True, stop=True, **kw):
        o, l, r = _ap(out), _ap(lhsT), _ap(rhs)
        self.op("pe", lambda h: h.matmul(o, l, r, start=start, stop=stop, **kw), [lhsT, rhs], [out])

    def transpose(self, out, in_, ident):
        o, i, d = _ap(out), _ap(in_), _ap(ident)
        self.op("pe", lambda h: h.transpose(o, i, d), [in_, ident], [out])

    def act(self, out, in_, func, bias=None, scale=None, accum_out=None, eng="act"):
        o, i = _ap(out), _ap(in_)
        kw = {}
        rd = [in_]
        wr = [out]
        if bias is not None:
            kw["bias"] = _ap(bias)
            rd.append(bias)
        if scale is not None:
            kw["scale"] = _ap(scale)
            rd.append(scale)
        if accum_out is not None:
            kw["accum_out"] = _ap(accum_out)
            wr.append(accum_out)
        self.op("act", lambda h: h.activation(o, i, func, **kw), rd, wr)

    def tt(self, eng, out, in0, in1, op):
        o, a, b = _ap(out), _ap(in0), _ap(in1)
        self.op(eng, lambda h: h.tensor_tensor(o, a, b, op), [in0, in1], [out])

    def ts(self, eng, out, in0, s1, op0, s2=None, op1=None, accum_out=None):
        o, a = _ap(out), _ap(in0)
        rd = [in0]
        wr = [out]
        if isinstance(s1, View):
            rd.append(s1)
        if isinstance(s2, View):
            rd.append(s2)
        kw = {}
        if op1 is not None:
            kw["op1"] = op1
        if accum_out is not None:
            kw["accum_out"] = _ap(accum_out)
            wr.append(accum_out)
        x1, x2 = _ap(s1), _ap(s2)
        self.op(eng, lambda h: h.tensor_scalar(o, a, x1, x2, op0, **kw), rd, wr)

    def tt_bc(self, out, in0, sc, ng, op):
        o3 = _ap(out).rearrange("p (s d) -> p s d", s=ng)
        a3 = _ap(in0).rearrange("p (s d) -> p s d", s=ng)
        b3 = _ap(sc).unsqueeze(2).broadcast_to([128, ng, 128])
        self.op("dve", lambda h: h.tensor_tensor(o3, a3, b3, op), [in0, sc], [out])

    def stt(self, out, in0, scalar, in1, op0, op1, eng="dve"):
        o, a, b = _ap(out), _ap(in0), _ap(in1)
        rd = [in0, in1]
        if isinstance(scalar, View):
            rd.append(scalar)
        s = _ap(scalar)
        self.op(eng, lambda h: h.scalar_tensor_tensor(o, a, s, b, op0, op1), rd, [out])

    def copy(self, eng, out, in_):
        o, i = _ap(out), _ap(in_)
        if eng == "act":
            self.op("act", lambda h: h.copy(o, i), [in_], [out])
        else:
            self.op(eng, lambda h: h.tensor_copy(o, i), [in_], [out])

    def memset(self, eng, out, val):
        o = _ap(out)
        self.op(eng, lambda h: h.memset(o, val), [], [out])

    def recip(self, out, in_):
        o, i = _ap(out), _ap(in_)
        self.op("dve", lambda h: h.reciprocal(o, i), [in_], [out])

    def affine_select(self, out, in_, pattern, cmp, fill, base, cm):
        o, i = _ap(out), _ap(in_)
        self.op("pool", lambda h: h.affine_select(o, i, pattern, cmp, fill, base=base, channel_multiplier=cm),
                [in_], [out])

    def check_deadlock(self):
        vals = {}
        pos = {n: 0 for n in self.engs}
        progress = True
        while progress:
            progress = False
            for n, e in self.engs.items():
                while pos[n] < len(e.trace):
                    kind, key, v = e.trace[pos[n]]
                    if kind == "w":
                        if vals.get(key, 0) >= v:
                            pos[n] += 1
                            progress = True
                        else:
                            break
                    else:
                        vals[key] = vals.get(key, 0) + v
                        pos[n] += 1
                        progress = True
        stuck = {n: (pos[n], len(e.trace), e.trace[pos[n]] if pos[n] < len(e.trace) else None)
                 for n, e in self.engs.items()}
        ok = all(pos[n] == len(e.trace) for n, e in self.engs.items())
        return ok, stuck, vals

    def emit(self):
        ok, stuck, vals = self.check_deadlock()
        if not ok:
            raise RuntimeError("DEADLOCK in sync graph: %s" % (stuck,))
        nc = self.nc
        import bisect
        sig = {n: set() for n in self.engs}
        for n, e in self.engs.items():
            for it in e.ops:
                if it[0] == "wait" and it[1] in sig:
                    sig[it[1]].add(it[2])
        sigl = {n: sorted(v) for n, v in sig.items()}
        sems = self.sems

        def run(n, h):
            e = self.engs[n]
            for it in e.ops:
                if it[0] == "wait":
                    key, val = it[1], it[2]
                    if key in sigl:
                        val = bisect.bisect_right(sigl[key], val)
                    h.wait_ge(sems[key], val)
                elif it[0] == "op":
                    ins = it[1](h)
                    if it[2] in sig[n]:
                        ins.then_inc(e.sem, 1)
                elif it[0] == "cc":
                    _, o, s_, key, groups = it
                    h.collective_compute("AllGather", ALU.bypass, replica_groups=groups, ins=[s_], outs=[o]).then_inc(sems[key], 16)
                else:
                    _, o, s_, key, kw = it
                    h.dma_start(out=o, in_=s_, **kw).then_inc(sems[key], 16)
        self.n_inc = {n: len(v) for n, v in sig.items()}
        with nc.Block() as block:
            @block.tensor
            def _(h):
                run("pe", h)

            @block.scalar
            def _(h):
                run("act", h)

            @block.vector
            def _(h):
                run("dve", h)

            @block.gpsimd
            def _(h):
                run("pool", h)

            @block.sync
            def _(h):
                run("sp", h)
        self.es.close()


D = 1024
H = 8
DH = 128
DFF = 2816
PROJ = 4112
EPS = 1e-6
NEG = -30000.0


class M:
    pass


def host_consts():
    r = np.arange(128)[:, None]
    c = np.arange(128)[None, :]
    mats = [r == c, np.ones((128, 128)), r <= c, r > c, c >= r]
    b = 1
    while b < 128:
        mats.append(((r // b) == (c // b) + 1) & ((r // b) % 2 == 1))
        b *= 2
    cf = np.stack([np.asarray(x, np.float32) for x in mats], axis=1)
    import ml_dtypes
    bmats = [r == c, np.ones((128, 128)), -(r >= c).astype(np.float32)]
    cb = np.stack([np.asarray(x, np.float32) for x in bmats], axis=1).astype(ml_dtypes.bfloat16)
    return np.ascontiguousarray(cf), np.ascontiguousarray(cb)


def setup_consts(k, m):
    cfd = k.dram("cf", [128, 12, 128], F32, kind="ExternalInput")
    cbd = k.dram("cb", [128, 3, 128], BF16, kind="ExternalInput")
    cf = k.sbuf([128, 12, 128], F32, "cfs")
    cb = k.sbuf([128, 3, 128], BF16, "cbs")
    k.dma("sp", cf[:], cfd[:])
    k.dma("sp", cb[:], cbd[:])
    m.cf, m.cb = cf, cb

    class V:
        def __init__(self, buf, j):
            self.buf, self.j = buf, j

        def __getitem__(self, idx):
            if idx == slice(None):
                return self.buf[:, self.j, :]
            a, b = idx
            return self.buf[a, self.j, b]
    m.ident = V(cf, 0)
    m.ones = V(cf, 1)
    m.triI = V(cf, 2)
    m.lmask = V(cf, 3)
    m.umask = V(cf, 4)
    m.lvl = [V(cf, 5 + i) for i in range(7)]
    m.identb = V(cb, 0)
    m.onesb = V(cb, 1)
    m.ntriS = V(cb, 2)


class Arena:
    def __init__(self, k, nbytes):
        self.k = k
        self.n32 = nbytes // 4
        self.base = k.es.enter_context(k.nc.sbuf_tensor("arena", [128, self.n32], F32))
        self.off = 0
        self.cnt = 0

    def reset(self):
        self.off = 0

    def alloc(self, shape, dtype, name=None):
        nel = int(np.prod(shape[1:]))
        nb = nel * (2 if dtype == BF16 else 4)
        n32 = (nb + 3) // 4
        n32 = (n32 + 7) // 8 * 8
        assert self.off + n32 <= self.n32, "arena overflow %d + %d > %d" % (self.off, n32, self.n32)
        ap = self.base[0:shape[0], self.off:self.off + n32]
        self.off += n32
        if dtype != F32:
            ap = ap.bitcast(dtype)
        ap = ap[:, 0:nel]
        if len(shape) == 3:
            ap = ap.rearrange("p (a b) -> p a b", a=shape[1])
        elif len(shape) == 4:
            ap = ap.rearrange("p (a b c) -> p a b c", a=shape[1], b=shape[2])
        self.cnt += 1
        return Buf(ap, name or "ar%d" % self.cnt)


def barrier(k):
    latest = {}
    for n, e in k.engs.items():
        if e.count:
            latest[n] = e.count
        for key, v in zip(e.dma_sems, e.dma_vals):
            if v:
                latest[key] = v
    for n, e in k.engs.items():
        for key, v in latest.items():
            if key == n:
                continue
            k._wait(e, key, v)


def new_prog():
    nc = bass.Bass("TRN2", target_bir_lowering=False)
    k = K(nc)
    m = M()
    m.ins = {}
    m.outs = {}
    setup_consts(k, m)
    m.pb = [k.psum([128, 512], F32, "pb%d" % i) for i in range(8)]
    m.pwi = 0

    def pw():
        b = m.pb[m.pwi % 8]
        m.pwi += 1
        return b
    m.pw = pw
    m.arena = Arena(k, 196 * 1024)
    return nc, k, m


def din(k, m, name, shape, dt=F32):
    b = k.dram(name, shape, dt, kind="ExternalInput")
    m.ins[name] = b
    return b


def dout(k, m, name, shape, dt=F32):
    b = k.dram(name, shape, dt, kind="ExternalOutput")
    m.outs[name] = b
    return b


def load_cols(k, m, rows):
    al = m.arena.alloc
    st = al([128, 128], F32)
    out = al([128, 128], F32)
    k.memset("dve", st[:], 0.0)
    r0 = 0
    for v, n in rows:
        k.dma("sp", st[r0:r0 + n, :], v, nowaw=False)
        r0 += n
    p = m.pw()
    k.transpose(p[:, 0:128], st[:], m.ident[:])
    k.copy("dve", out[:], p[:, 0:128])
    return out


def load_w(k, dst, src, kchunks):
    for kc in range(kchunks):
        k.dma("pool", dst[:, kc, :], src(kc), max_dma_last_dim=4096)


def rstd_of(k, junk, ss, src, n):
    k.act(junk[:, 0:n], src, AF.Square, accum_out=ss[:])
    k.ts("dve", ss[:], ss[:], 1.0 / n, ALU.mult, EPS, ALU.add)
    k.act(ss[:], ss[:], AF.Ln)
    k.act(ss[:], ss[:], AF.Exp, scale=-0.5)


def norm_T(k, m, xs, gains, outs):
    for half in range(2):
        p = m.pw()
        for c4 in range(4):
            c = half * 4 + c4
            k.transpose(p[:, c4 * 128:(c4 + 1) * 128], xs[:, c * 128:(c + 1) * 128], m.ident[:])
        for g, o in zip(gains, outs):
            o3 = _ap(o[:, half * 4:half * 4 + 4, :])
            a3 = _ap(p[:]).rearrange("p (s d) -> p s d", s=4)
            b3 = _ap(g)[:, half * 4:half * 4 + 4].unsqueeze(2).broadcast_to([128, 4, 128])
            k.op("dve", lambda h, o3=o3, a3=a3, b3=b3: h.tensor_tensor(o3, a3, b3, ALU.mult), [p[:], g], [o[:]])


D = 1024
H = 8
DFF = 2816
EPS = 1e-6


def build_dense_ffn(TOK):
    nc, k, m = new_prog()
    T = dict(x=din(k, m, "x", [TOK, D]), aT=din(k, m, "aT", [8, 128, TOK], BF16), w_o=din(k, m, "w_o", [D, D]),
             fnorm=din(k, m, "fnorm", [8, 128]), w_gu=din(k, m, "w_gu", [D, 2 * DFF]), w_d=din(k, m, "w_d", [DFF, D]),
             h=dout(k, m, "h", [TOK, D]))
    emit_dense_ffn(k, m, TOK, T)
    k.final_wait("sp", [T["h"]])
    k.emit()
    return nc


def emit_dense_ffn(k, m, TOK, T, xload=None):
    m.arena.reset()
    al = m.arena.alloc
    d_x, d_aT, d_wo, d_fn, d_wgu, d_wd, d_h = T["x"], T["aT"], T["w_o"], T["fnorm"], T["w_gu"], T["w_d"], T["h"]
    cols = load_cols(k, m, [(d_fn[:], 8)])
    w_o = al([128, 8, D], BF16, "w_o")
    w_gu = al([128, 8, 2 * DFF], BF16, "w_gu")
    w_d = al([128, 22, D], BF16, "w_d")
    load_w(k, w_o, lambda kc: d_wo[kc * 128:(kc + 1) * 128, :], 8)
    load_w(k, w_gu, lambda kc: d_wgu[kc * 128:(kc + 1) * 128, :], 8)
    load_w(k, w_d, lambda kc: d_wd[kc * 128:(kc + 1) * 128, :], 22)
    junk = al([128, 1024], BF16)
    ss = al([128, 1], F32)
    xt = [al([128, D], F32, "xt%d" % i) for i in range(1 if xload is not None else 2)]
    at = [al([128, 8, 128], BF16, "at%d" % i) for i in range(2)]
    h1 = [al([128, D], F32, "h1_%d" % i) for i in range(2)]
    h2 = [al([128, D], F32, "h2_%d" % i) for i in range(1 if xload is not None else 2)]
    xtmp = [al([128, D], F32, "xtmp%d" % i) for i in range(2)] if xload is not None else None
    xs = al([128, D], F32, "xs")
    hnT = al([128, 8, 128], BF16, "hnT")
    hidF = al([128, 22 * 128], BF16, "hidT")
    hidT = Buf(hidF.t.rearrange("p (a b) -> p a b", a=22), "hidT3")
    hidT.reads, hidT.writes = hidF.reads, hidF.writes
    sg = [al([128, 256], F32, "sg%d" % i) for i in range(2)]
    aTv = k.track(d_aT.t.rearrange("h p t -> p h t"), "aTv")
    for ti in range(TOK // 128):
        x_t, a_t, h1_, h2_ = xt[ti % len(xt)], at[ti % 2], h1[ti % 2], h2[ti % len(h2)]
        if xload is None:
            k.dma("sp", x_t[:], d_x[ti * 128:(ti + 1) * 128, :])
        else:
            xload(ti, x_t, xtmp)
        k.dma("sp", a_t[:], aTv[:, :, ti * 128:(ti + 1) * 128])
        for half in range(2):
            py = m.pw()
            for h in range(8):
                k.matmul(py[:], a_t[:, h, :], w_o[:, h, half * 512:(half + 1) * 512], start=(h == 0), stop=(h == 7))
            k.tt("dve", h1_[:, half * 512:(half + 1) * 512], py[:], x_t[:, half * 512:(half + 1) * 512], ALU.add)
        rstd_of(k, junk, ss, h1_[:], D)
        k.ts("dve", xs[:], h1_[:], ss[:], ALU.mult)
        norm_T(k, m, xs, [cols[:, 0:8]], [hnT])
        for j2 in range(11):
            p = m.pw()
            for q in range(4):
                j = j2 * 2 + (q % 2)
                col = (0 if q < 2 else DFF) + j * 128
                for kc in range(8):
                    k.matmul(p[:, q * 128:(q + 1) * 128], w_gu[:, kc, col:col + 128], hnT[:, kc, :],
                             start=(kc == 0), stop=(kc == 7))
            s_ = sg[j2 % 2]
            k.act(s_[:], p[:, 0:256], AF.Silu)
            k.tt("dve", hidF[:, j2 * 256:(j2 + 1) * 256], p[:, 256:512], s_[:], ALU.mult)
        for half in range(2):
            py = m.pw()
            for j in range(22):
                k.matmul(py[:], hidT[:, j, :], w_d[:, j, half * 512:(half + 1) * 512], start=(j == 0), stop=(j == 21))
            k.tt("dve", h2_[:, half * 512:(half + 1) * 512], py[:], h1_[:, half * 512:(half + 1) * 512], ALU.add)
        k.dma("sp", d_h[ti * 128:(ti + 1) * 128, :], h2_[:])


def build_kvq(TOK):
    nc, k, m = new_prog()
    T = dict(h=din(k, m, "h", [TOK, D]), kvn=din(k, m, "kvn", [8, 128]), an=din(k, m, "an", [8, 128]),
             kg=din(k, m, "kg", [1, 128]), qg=din(k, m, "qg", [1, 128]), w_kv=din(k, m, "w_kv", [D, 2 * D]),
             w_q=din(k, m, "w_q", [D, D]), qT=dout(k, m, "qT", [8, 128, TOK], BF16),
             kT=dout(k, m, "kT", [8, 128, TOK], BF16), v=dout(k, m, "v", [TOK, D], BF16))
    outs = emit_kvq(k, m, TOK, T)
    k.final_wait("sp", outs)
    k.emit()
    return nc


def emit_kvq(k, m, TOK, T):
    m.arena.reset()
    al = m.arena.alloc
    d_h, d_kvn, d_an, d_kg, d_qg, d_wkv, d_wq = T["h"], T["kvn"], T["an"], T["kg"], T["qg"], T["w_kv"], T["w_q"]
    d_qT, d_kT, d_v = T["qT"], T["kT"], T["v"]
    cols = load_cols(k, m, [(d_kvn[:], 8), (d_an[:], 8), (d_kg[:], 1), (d_qg[:], 1)])
    w_kv = al([128, 8, 2 * D], BF16, "w_kv")
    w_q = al([128, 8, D], BF16, "w_q")
    load_w(k, w_kv, lambda kc: d_wkv[kc * 128:(kc + 1) * 128, :], 8)
    load_w(k, w_q, lambda kc: d_wq[kc * 128:(kc + 1) * 128, :], 8)
    junk = al([128, 1024], BF16)
    ss = al([128, 1], F32)
    ht = [al([128, D], F32, "ht%d" % i) for i in range(2)]
    xs = al([128, D], F32, "xs")
    xkT = al([128, 8, 128], BF16, "xkT")
    xqT = al([128, 8, 128], BF16, "xqT")
    vt = [al([128, D], BF16, "vt%d" % i) for i in range(2)]
    ss8 = {n: al([128, 8], F32, "ss8" + n) for n in "kq"}
    nrm = {n: al([128, D], F32, "nrm" + n) for n in "kq"}
    oT = {n: [al([128, 8 * 128], BF16, "oT%s%d" % (n, i)) for i in range(2)] for n in "kq"}
    qTv = k.track(d_qT.t.rearrange("h p t -> p h t"), "qTv")
    kTv = k.track(d_kT.t.rearrange("h p t -> p h t"), "kTv")
    qscale = 128.0 ** -0.5
    for ti in range(TOK // 128):
        h_t = ht[ti % 2]
        k.dma("sp", h_t[:], d_h[ti * 128:(ti + 1) * 128, :])
        rstd_of(k, junk, ss, h_t[:], D)
        k.ts("dve", xs[:], h_t[:], ss[:], ALU.mult)
        norm_T(k, m, xs, [cols[:, 0:8], cols[:, 8:16]], [xkT, xqT])
        v_t = vt[ti % 2]
        for n4 in (2, 3):
            py = m.pw()
            for kc in range(8):
                k.matmul(py[:], xkT[:, kc, :], w_kv[:, kc, n4 * 512:(n4 + 1) * 512], start=(kc == 0), stop=(kc == 7))
            k.copy("act", v_t[:, (n4 - 2) * 512:(n4 - 1) * 512], py[:])
        k.dma("sp", d_v[ti * 128:(ti + 1) * 128, :], v_t[:])
        for which, xT, w, gcol, dview in (("k", xkT, w_kv, 16, kTv), ("q", xqT, w_q, 17, qTv)):
            pys = []
            for n4 in range(2):
                py = m.pw()
                for kc in range(8):
                    k.matmul(py[:], xT[:, kc, :], w[:, kc, n4 * 512:(n4 + 1) * 512], start=(kc == 0), stop=(kc == 7))
                pys.append(py)
                for h4 in range(4):
                    hh = n4 * 4 + h4
                    k.act(junk[:, 0:128], py[:, h4 * 128:(h4 + 1) * 128], AF.Square, accum_out=ss8[which][:, hh:hh + 1])
            s8 = ss8[which]
            k.ts("dve", s8[:], s8[:], 1.0 / 128, ALU.mult, EPS, ALU.add)
            k.act(s8[:], s8[:], AF.Ln)
            k.act(s8[:], s8[:], AF.Exp, scale=-0.5)
            nr = nrm[which]
            for n4 in range(2):
                k.tt_bc(nr[:, n4 * 512:(n4 + 1) * 512], pys[n4][:], s8[:, n4 * 4:n4 * 4 + 4], 4, ALU.mult)
            o_ = oT[which][ti % 2]
            for n4 in range(2):
                p = m.pw()
                for h4 in range(4):
                    hh = n4 * 4 + h4
                    k.transpose(p[:, h4 * 128:(h4 + 1) * 128], nr[:, hh * 128:(hh + 1) * 128], m.ident[:])
                if which == "k":
                    k.ts("dve", o_[:, n4 * 512:(n4 + 1) * 512], p[:], cols[:, gcol:gcol + 1], ALU.mult)
                else:
                    k.ts("dve", o_[:, n4 * 512:(n4 + 1) * 512], p[:], cols[:, gcol:gcol + 1], ALU.mult, qscale, ALU.mult)
            o3 = k.track(o_.t.rearrange("p (a b) -> p a b", a=8), "o3")
            o3.reads, o3.writes = o_.reads, o_.writes
            k.dma("sp", dview[:, :, ti * 128:(ti + 1) * 128], o3[:])
    return [d_v, qTv, kTv]


def host_masks():
    import ml_dtypes
    s = np.arange(128)[:, None, None]
    jj = np.arange(4)[None, :, None]
    t = np.arange(512)[None, None, :]
    mk = np.where(jj * 128 + s < t, 0.0, NEG).astype(np.float32)
    return np.ascontiguousarray(mk.astype(ml_dtypes.bfloat16))


NEG = -30000.0


def build_attn(S, NH):
    nc, k, m = new_prog()
    T = dict(qT=din(k, m, "qT", [NH, 128, S], BF16), kT=din(k, m, "kT", [NH, 128, S], BF16),
             v=din(k, m, "v", [S, NH * 128], BF16), masks=din(k, m, "masks", [128, 4, 512], BF16),
             oT=dout(k, m, "oT", [NH, 128, S], BF16))
    emit_attn(k, m, S, NH, T)
    k.final_wait("sp", [T["oT"]])
    k.emit()
    return nc


def emit_attn(k, m, S, NH, T, parity=False, sel=None):
    m.arena.reset()
    al = m.arena.alloc
    d_q, d_k, d_v, d_mk, d_o = T["qT"], T["kT"], T["v"], T["masks"], T["oT"]
    NB = S // 128
    NSB = S // 512
    NM = 8 if parity else 4
    masks = al([128, NM, 512], BF16, "masks_sb")
    k.dma("sp", masks[:], d_mk[:])
    qs = [al([128, S], BF16, "qs%d" % i) for i in range(2)]
    ks = [al([128, S], BF16, "ks%d" % i) for i in range(2)]
    vs = [al([128, NB, 128], BF16, "vs%d" % i) for i in range(2)]
    e_ = [al([128, 512], F32, "e%d" % i) for i in range(3)]
    sp_ = [al([128, 512], BF16, "sp%d" % i) for i in range(3)]
    att_ = [al([128, 512], BF16, "att%d" % i) for i in range(2)]
    fac_ = [al([128, 4], F32, "fac%d" % i) for i in range(2)]
    accs = [al([128, 512], F32, "acc%d" % i) for i in range(2)]
    tmp = al([128, 512], F32, "acctmp")
    ot_ = [al([128, 512], BF16, "ot%d" % i) for i in range(2)]
    qsel = [al([128, 512], BF16, "qsel%d" % i) for i in range(2)]
    qtmp = al([128, 512], BF16, "qtmp")
    vview = k.track(d_v.t.rearrange("(blk p) c -> p blk c", p=128), "vview")
    pds = [Buf(m.pb[7].t[:, j * 64:j * 64 + 64], "pd%d" % j) for j in range(8)]
    ring = m.pb[0:7]
    rs = {"i": 0}

    def pw7():
        b_ = ring[rs["i"] % 7]
        rs["i"] += 1
        return b_
    old_pw = m.pw
    m.pw = pw7
    tiles = []
    for h in range(NH):
        for i in range(NSB // 2 if parity else NSB):
            nkb = 4 * (2 * i + 2) if parity else 4 * (i + 1)
            for kb in range(nkb):
                tiles.append((h, i, kb, nkb))
    state = {"cur": None}

    def stage_a(t):
        h, i, kb, nkb = tiles[t]
        q_h, k_h, v_h = qs[h % 2], ks[h % 2], vs[h % 2]
        if i == 0 and kb == 0:
            k.dma("sp", q_h[:], d_q[h, :, :])
            k.dma("sp", k_h[:], d_k[h, :, :])
            step = 16 if NB >= 16 else NB
            for b0 in range(0, NB, step):
                k.dma("sp", v_h[:, b0:b0 + step, :], vview[:, b0:b0 + step, h * 128:(h + 1) * 128])
        if parity:
            qb = qsel[(h * (NSB // 2) + i) % 2]
            if kb == 0:
                k.ts("dve", qtmp[:], q_h[:, (2 * i) * 512:(2 * i + 1) * 512], sel[:, 0:1], ALU.mult)
                k.stt(qb[:], q_h[:, (2 * i + 1) * 512:(2 * i + 2) * 512], sel[:, 1:2], qtmp[:], ALU.mult, ALU.add)
            qv = qb[:]
        else:
            qv = q_h[:, i * 512:(i + 1) * 512]
        jj = kb - (nkb - NM)
        diag = jj >= 0
        kv_ = k_h[:, kb * 128:(kb + 1) * 128]
        e, sp = e_[t % 3], sp_[t % 3]
        pz = m.pw()
        k.matmul(pz[:], kv_, qv, start=True, stop=not diag)
        if diag:
            k.matmul(pz[:], m.identb[:], masks[:, jj, :], start=False, stop=True)
        k.act(e[:], pz[:], AF.Exp)
        k.act(sp[:], e[:], AF.Ln, bias=1.0)
        pd = pds[t % len(pds)]
        return (kv_, qv, diag, jj, sp, v_h, pd, pz)

    def stage_pd(t, a):
        kb = tiles[t][2]
        sp, pd = a[4], a[6]
        if kb > 0:
            for sub in range(4):
                k.matmul(pd[:, sub:sub + 1], sp[:, sub * 128:(sub + 1) * 128], m.onesb[:, 0:1])

    def stage_b(t, a, mid):
        h, i, kb, nkb = tiles[t]
        kv_, qv, diag, jj, sp, v_h, pd, pz = a
        att, fac = att_[t % 2], fac_[t % 2]
        cur = state["cur"]
        nxt = accs[0] if cur is not accs[0] else accs[1]
        pl = pz
        k.matmul(pl[:], m.ntriS[:], sp[:], start=False, stop=True)
        mid()
        if kb > 0:
            k.act(fac[:], pd[:, 0:4], AF.Exp, scale=-1.0)
            k.tt_bc(tmp[:], cur[:], fac[:, 0:4], 4, ALU.mult)
        k.act(att[:], pl[:], AF.Exp)
        pP = m.pw()
        for sub in range(4):
            k.matmul(pP[:, sub * 128:(sub + 1) * 128], att[:, sub * 128:(sub + 1) * 128], v_h[:, kb, :])
        if kb == 0:
            k.copy("dve", nxt[:], pP[:])
        else:
            k.tt("dve", nxt[:], pP[:], tmp[:], ALU.add)
        state["cur"] = nxt
        if kb == nkb - 1:
            pT = m.pw()
            for sub in range(4):
                k.transpose(pT[:, sub * 128:(sub + 1) * 128], nxt[:, sub * 128:(sub + 1) * 128], m.ident[:])
            o_ = ot_[i % 2]
            k.copy("dve", o_[:], pT[:])
            k.dma("sp", d_o[h, :, i * 512:(i + 1) * 512], o_[:])

    LA = 2
    nt = len(tiles)
    A = {}
    for t0 in range(min(LA, nt)):
        A[t0] = stage_a(t0)
    stage_pd(0, A[0])
    for t in range(nt):
        if t + LA < nt:
            A[t + LA] = stage_a(t + LA)
        if t + 1 < nt:
            stage_b(t, A[t], lambda: stage_pd(t + 1, A[t + 1]))
        else:
            stage_b(t, A[t], lambda: None)
        del A[t]
    m.pw = old_pw


def build_gdn(S, NH=4, NGRP=1):
    nc, k, m = new_prog()
    W = NH * 128
    T = dict(x=din(k, m, "x", [S, D]), an=din(k, m, "an", [8, 128]), w_qkvz=din(k, m, "w_qkvz", [D, NGRP * 4 * W]),
             w_bg=din(k, m, "w_bg", [D, NGRP * 2 * NH]), conv=din(k, m, "conv", [4, NGRP * 3 * W]),
             alog=din(k, m, "alog", [1, NGRP * NH]), dtb=din(k, m, "dtb", [1, NGRP * NH]),
             ogain=din(k, m, "ogain", [1, 128]), ogT=dout(k, m, "ogT", [NGRP * NH, 128, S], BF16))
    ov = emit_gdn(k, m, S, T, NH, NGRP)
    print("gdn arena words used", m.arena.off, "of", m.arena.n32)
    k.final_wait("sp", [ov])
    k.emit()
    return nc


def emit_gdn(k, m, S, T, NH=4, NGRP=1):
    m.arena.reset()
    al = m.arena.alloc
    W = NH * 128
    d_x, d_an, d_w, d_wbg, d_conv, d_alog, d_dtb, d_og, d_o = (T["x"], T["an"], T["w_qkvz"], T["w_bg"], T["conv"],
                                                                  T["alog"], T["dtb"], T["ogain"], T["ogT"])
    NG = 3 * NH
    cols = load_cols(k, m, [(d_an[:], 8), (d_og[:], 1)])
    convc = load_cols(k, m, [(k.track(d_conv.t.rearrange("t (j p) -> (t j) p", p=128), "cwv")[:], 4 * NG * NGRP)])
    w = al([128, 8, NGRP * 4 * W], BF16, "w")
    wbg = al([128, 8, NGRP * 2 * NH], BF16, "wbg")
    load_w(k, w, lambda kc: d_w[kc * 128:(kc + 1) * 128, :], 8)
    load_w(k, wbg, lambda kc: d_wbg[kc * 128:(kc + 1) * 128, :], 8)
    alog = al([128, NGRP * NH], F32, "alog")
    dtb = al([128, NGRP * NH], F32, "dtb")
    k.dma("sp", alog[:], k.track(d_alog.t[0, :].partition_broadcast(128), "alv")[:])
    k.dma("sp", dtb[:], k.track(d_dtb.t[0, :].partition_broadcast(128), "dtv")[:])
    nexpA = al([128, NGRP * NH], F32, "nexpA")
    k.act(nexpA[:], alog[:], AF.Exp)
    k.ts("dve", nexpA[:], nexpA[:], -1.0, ALU.mult)
    ident4 = al([128, W], F32, "ident4")
    umask4 = al([128, W], F32, "umask4")
    lvl4 = [al([128, W], BF16, "lvl4_%d" % i) for i in range(7)]
    for h in range(NH):
        sl = slice(h * 128, (h + 1) * 128)
        k.copy("dve", ident4[:, sl], m.ident[:])
        k.copy("dve", umask4[:, sl], m.umask[:])
        for i in range(7):
            k.copy("dve", lvl4[i][:, sl], m.lvl[i][:])
    junk = al([128, 1024], BF16)
    ss = al([128, 1], F32)
    xt = [al([128, D], F32, "xt%d" % i) for i in range(2)]
    xs = al([128, D], F32, "xs")
    xnT = al([128, 8, 128], BF16, "xnT")
    cbs = [[al([128, NH, 131], BF16, "cb%d_%d" % (G, g)) for g in range(3)] for G in range(NGRP)]
    dgw = al([128, 4 * NG * NGRP, 128], BF16, "dgw")
    for c_ in range(4 * NG * NGRP):
        k.ts("dve", dgw[:, c_, :], m.identb[:], convc[:, c_:c_ + 1], ALU.mult)
    for G in range(NGRP):
        for g in range(3):
            k.memset("dve", cbs[G][g][:], 0.0)
    F2 = lambda n, dt=F32: al([128, W], dt, n)
    Ssts = [F2("S%d" % G) for G in range(NGRP)]
    Sbs = [F2("Sb%d" % G, BF16) for G in range(NGRP)]
    for G in range(NGRP):
        k.memset("dve", Ssts[G][:], 0.0)
        k.memset("dve", Sbs[G][:], 0.0)
    NSLOT = 2 if (NGRP >= 2 and NH <= 2) else 1

    def mk_slot(si):
        F = lambda n, dt=F32: al([128, W], dt, "%s_s%d" % (n, si))
        sl = {}
        sl["sil"] = [F("sil%d" % g_) for g_ in range(3)]
        for n in ("zs", "ri", "kTf", "vtm", "ktm", "dg", "tmpF", "Fm", "FU", "Am", "u", "t1", "o_", "on", "St"):
            sl[n] = F(n)
        sl["sq2"] = F("sq2", BF16)
        for n in ("qT", "kT", "Pa", "Pb", "Xa", "Xb", "Y", "attnT", "vb", "kb", "kd", "wT", "vnew"):
            sl[n] = F(n, BF16)
        sl["L"] = [F("L%d" % i, BF16) for i in range(7)]
        sl["ogt"] = [F("ogt%d" % i, BF16) for i in range(2)]
        sl["sm"] = {n: al([128, NH], F32, "%s_s%d" % (n, si)) for n in
                    ("beta", "g", "gc", "ngc", "glast", "egc", "ekd", "eglast", "sckb", "tmp8", "ss4")}
        return sl
    slots = [mk_slot(si) for si in range(NSLOT)]
    oview = k.track(d_o.t.rearrange("h p t -> p h t"), "oview")
    qscale = float(np.log(128.0 ** -0.5))
    HS = [slice(h * 128, (h + 1) * 128) for h in range(NH)]

    def mm4(p, lhs_of, rhs_of):
        for h in range(NH):
            k.matmul(p[:, HS[h]], lhs_of(h), rhs_of(h))

    for ci in range(S // 128):
        x_t = xt[ci % 2]
        k.dma("sp", x_t[:], d_x[ci * 128:(ci + 1) * 128, :])
        rstd_of(k, junk, ss, x_t[:], D)
        k.ts("dve", xs[:], x_t[:], ss[:], ALU.mult)
        norm_T(k, m, xs, [cols[:, 0:8]], [xnT])
        def gbody(G, sl):
            cb, Sst, Sb = cbs[G], Ssts[G], Sbs[G]
            dtbG, nexpAG = dtb[:, G * NH:(G + 1) * NH], nexpA[:, G * NH:(G + 1) * NH]
            sil, sm, L, ogt = sl["sil"], sl["sm"], sl["L"], sl["ogt"]
            zs, sq2, ri, kTf, vtm, ktm, dg, tmpF, Fm, FU, Am = (sl[n] for n in ("zs", "sq2", "ri", "kTf", "vtm", "ktm", "dg", "tmpF", "Fm", "FU", "Am"))
            u, t1, o_, on, St = (sl[n] for n in ("u", "t1", "o_", "on", "St"))
            qT, kT, Pa, Pb, Xa, Xb, Y, attnT, vb, kb, kd, wT, vnew = (sl[n] for n in ("qT", "kT", "Pa", "Pb", "Xa", "Xb", "Y", "attnT", "vb", "kb", "kd", "wT", "vnew"))
            yield
            banks = []
            for g in range(4):
                p = m.pw()
                for h in range(NH):
                    fc = g * NH + h
                    for kc in range(8):
                        k.matmul(p[:, HS[h]], w[:, kc, G * 4 * W + fc * 128:G * 4 * W + (fc + 1) * 128], xnT[:, kc, :], start=(kc == 0), stop=(kc == 7))
                banks.append(p)
            k.act(zs[:], banks[3][:, 0:W], AF.Silu)
            for g in range(3):
                src3 = View(banks[g], banks[g].t[:, 0:W].rearrange("p (s d) -> p s d", s=NH))
                k.copy("act", cb[g][:, :, 3:131], src3)
            for g in range(3):
                pc = m.pw()
                for h in range(NH):
                    j = g * NH + h
                    for t in range(4):
                        k.matmul(pc[:, HS[h]], dgw[:, t * NG * NGRP + G * NG + j, :], cb[g][:, h, t:t + 128],
                                 start=(t == 0), stop=(t == 3))
                k.copy("act", cb[g][:, :, 0:3], cb[g][:, :, 128:131])
                k.act(sil[g][:], pc[:, 0:W], AF.Silu)
            yield
            pbg = m.pw()
            for kc in range(8):
                k.matmul(pbg[:, 0:2 * NH], xnT[:, kc, :], wbg[:, kc, G * 2 * NH:(G + 1) * 2 * NH], start=(kc == 0), stop=(kc == 7))
            k.act(sm["beta"][:], pbg[:, 0:NH], AF.Exp, scale=-1.0)
            k.ts("dve", sm["beta"][:], sm["beta"][:], 1.0, ALU.add)
            k.recip(sm["beta"][:], sm["beta"][:])
            k.tt("dve", sm["g"][:], pbg[:, NH:2 * NH], dtbG, ALU.add)
            k.act(sm["g"][:], sm["g"][:], AF.Exp)
            k.act(sm["g"][:], sm["g"][:], AF.Ln, bias=1.0)
            k.tt("dve", sm["g"][:], sm["g"][:], nexpAG, ALU.mult)
            pg = m.pw()
            k.matmul(pg[:, 0:NH], m.triI[:], sm["g"][:])
            k.matmul(pg[:, 8:8 + NH], m.ones[:], sm["g"][:])
            k.copy("dve", sm["gc"][:], pg[:, 0:NH])
            k.copy("dve", sm["glast"][:], pg[:, 8:8 + NH])
            k.ts("dve", sm["ngc"][:], sm["gc"][:], -1.0, ALU.mult)
            k.act(sm["egc"][:], sm["gc"][:], AF.Exp)
            k.act(sm["eglast"][:], sm["glast"][:], AF.Exp)
            k.tt("dve", sm["tmp8"][:], sm["glast"][:], sm["gc"][:], ALU.subtract)
            k.act(sm["ekd"][:], sm["tmp8"][:], AF.Exp)
            k.tt("dve", sm["sckb"][:], sm["beta"][:], sm["egc"][:], ALU.mult)
            yield
            for g in range(2):
                k.act(sq2[:], sil[g][:], AF.Square)
                ps = m.pw()
                k.matmul(ps[:, 0:W], m.onesb[:], sq2[:])
                k.act(ri[:], ps[:, 0:W], AF.Ln, bias=EPS)
                if g == 0:
                    k.act(ri[:], ri[:], AF.Exp, scale=-0.5, bias=qscale)
                    k.tt("dve", qT[:], sil[0][:], ri[:], ALU.mult)
                else:
                    k.act(ri[:], ri[:], AF.Exp, scale=-0.5)
                    k.tt("dve", kTf[:], sil[1][:], ri[:], ALU.mult)
                    k.copy("act", kT[:], kTf[:])
            pt = m.pw()
            for h in range(NH):
                k.transpose(pt[:, HS[h]], kTf[:, HS[h]], m.ident[:])
            k.copy("dve", ktm[:], pt[:, 0:W])
            pt = m.pw()
            for h in range(NH):
                k.transpose(pt[:, HS[h]], sil[2][:, HS[h]], m.ident[:])
            k.copy("dve", vtm[:], pt[:, 0:W])
            yield
            k.tt_bc(dg[:], ident4[:], sm["gc"][:, 0:NH], NH, ALU.mult)
            pR = m.pw()
            k.matmul(pR[:, 0:W], m.ones[:], dg[:])
            for h in range(NH):
                k.act(tmpF[:, HS[h]], pR[:, HS[h]], AF.Abs, bias=sm["ngc"][:, h:h + 1])
            k.act(Fm[:], tmpF[:], AF.Exp, scale=-1.0)
            k.tt("dve", FU[:], Fm[:], umask4[:], ALU.mult)
            pKK = m.pw()
            mm4(pKK, lambda h: kT[:, HS[h]], lambda h: kT[:, HS[h]])
            for h in range(NH):
                k.stt(Am[:, HS[h]], pKK[:, HS[h]], sm["beta"][:, h:h + 1], Fm[:, HS[h]], ALU.mult, ALU.mult)
            pQK = m.pw()
            mm4(pQK, lambda h: kT[:, HS[h]], lambda h: qT[:, HS[h]])
            k.tt("dve", attnT[:], pQK[:, 0:W], FU[:], ALU.mult)
            for i in range(7):
                k.tt("dve", L[i][:], Am[:], lvl4[i][:], ALU.mult)
            yield
            pY = m.pw()
            mm4(pY, lambda h: L[0][:, HS[h]], lambda h: m.identb[:])
            Pc, Pn, Xc, Xn = Pa, Pb, Xa, Xb
            k.stt(Pc[:], pY[:, 0:W], -1.0, ident4[:], ALU.mult, ALU.add)
            k.tt("dve", Xc[:], ident4[:], L[0][:], ALU.subtract)
            for i in range(1, 7):
                pY = m.pw()
                mm4(pY, lambda h: L[i][:, HS[h]], lambda h: Pc[:, HS[h]])
                k.copy("act", Y[:], pY[:, 0:W])
                pZ = m.pw()
                mm4(pZ, lambda h: Xc[:, HS[h]], lambda h: Y[:, HS[h]])
                if i < 6:
                    pZT = m.pw()
                    mm4(pZT, lambda h: Y[:, HS[h]], lambda h: Xc[:, HS[h]])
                k.stt(Pn[:], pZ[:, 0:W], -1.0, Pc[:], ALU.mult, ALU.add)
                Pc, Pn = Pn, Pc
                if i < 6:
                    k.stt(Xn[:], pZT[:, 0:W], -1.0, Xc[:], ALU.mult, ALU.add)
                    Xc, Xn = Xn, Xc
                yield
            yield
            k.tt_bc(vb[:], vtm[:], sm["beta"][:, 0:NH], NH, ALU.mult)
            k.tt_bc(kb[:], ktm[:], sm["sckb"][:, 0:NH], NH, ALU.mult)
            k.tt_bc(kd[:], ktm[:], sm["ekd"][:, 0:NH], NH, ALU.mult)
            pu = m.pw()
            mm4(pu, lambda h: Pc[:, HS[h]], lambda h: vb[:, HS[h]])
            k.copy("dve", u[:], pu[:, 0:W])
            pw_ = m.pw()
            mm4(pw_, lambda h: kb[:, HS[h]], lambda h: Pc[:, HS[h]])
            k.copy("dve", wT[:], pw_[:, 0:W])
            yield
            pws = m.pw()
            mm4(pws, lambda h: wT[:, HS[h]], lambda h: Sb[:, HS[h]])
            k.stt(vnew[:], pws[:, 0:W], -1.0, u[:], ALU.mult, ALU.add)
            po1 = m.pw()
            mm4(po1, lambda h: qT[:, HS[h]], lambda h: Sb[:, HS[h]])
            po2 = m.pw()
            mm4(po2, lambda h: attnT[:, HS[h]], lambda h: vnew[:, HS[h]])
            k.tt_bc(t1[:], po1[:, 0:W], sm["egc"][:, 0:NH], NH, ALU.mult)
            k.tt("dve", o_[:], po2[:, 0:W], t1[:], ALU.add)
            pS = m.pw()
            mm4(pS, lambda h: kd[:, HS[h]], lambda h: vnew[:, HS[h]])
            k.tt_bc(St[:], Sst[:], sm["eglast"][:, 0:NH], NH, ALU.mult)
            k.tt("dve", Sst[:], pS[:, 0:W], St[:], ALU.add)
            k.copy("act", Sb[:], Sst[:])
            yield
            for h in range(NH):
                k.act(junk[:, 0:128], o_[:, HS[h]], AF.Square, accum_out=sm["ss4"][:, h:h + 1])
            k.ts("dve", sm["ss4"][:], sm["ss4"][:], 1.0 / 128, ALU.mult, EPS, ALU.add)
            k.act(sm["ss4"][:], sm["ss4"][:], AF.Ln)
            k.act(sm["ss4"][:], sm["ss4"][:], AF.Exp, scale=-0.5)
            k.tt_bc(on[:], o_[:], sm["ss4"][:, 0:NH], NH, ALU.mult)
            pT = m.pw()
            for h in range(NH):
                k.transpose(pT[:, HS[h]], on[:, HS[h]], m.ident[:])
            og = ogt[(ci * NGRP + G) % 2]
            k.stt(og[:], pT[:, 0:W], cols[:, 8:9], zs[:], ALU.mult, ALU.mult)
            og3 = k.track(og.t.rearrange("p (a b) -> p a b", a=NH), "og3")
            og3.reads, og3.writes = og.reads, og.writes
            k.dma("sp", oview[:, G * NH:(G + 1) * NH, ci * 128:(ci + 1) * 128], og3[:])

        for G0 in range(0, NGRP, NSLOT):
            gens = [gbody(G0 + si, slots[si]) for si in range(NSLOT)]
            alive = list(gens)
            while alive:
                for gen_ in list(alive):
                    try:
                        next(gen_)
                    except StopIteration:
                        alive.remove(gen_)
    return oview


GDN_HPG = 4


def build_fused(S):
    nc, k, m = new_prog()
    NPOS = S // 1024
    TOKO = NPOS * 512
    I = lambda n, shp, dt=F32: din(k, m, n, shp, dt)
    x = I("x", [S, D])
    Tg = dict(x=x, an=I("an0", [8, 128]), w_qkvz=I("w_qkvz", [D, 4096]), w_bg=I("w_bg", [D, 16]),
              conv=I("conv", [4, 3072]), alog=I("alog", [1, 8]), dtb=I("dtb", [1, 8]), ogain=I("ogain", [1, 128]))
    ogT = k.dram("ogT_scr", [8, 128, S], BF16)
    Tg["ogT"] = ogT
    h2 = k.dram("h2_scr", [S, D], F32)
    Tf0 = dict(x=x, aT=ogT, w_o=I("gdn_w_out", [D, D]), fnorm=I("fn0", [8, 128]), w_gu=I("w_gu0", [D, 2 * DFF]),
               w_d=I("w_d0", [DFF, D]), h=h2)
    qT = k.dram("qT_scr", [8, 128, S], BF16)
    kT = k.dram("kT_scr", [8, 128, S], BF16)
    v = k.dram("v_scr", [S, D], BF16)
    Tk = dict(h=h2, kvn=I("kvn", [8, 128]), an=I("an1", [8, 128]), kg=I("kg", [1, 128]), qg=I("qg", [1, 128]),
              w_kv=I("w_kv", [D, 2 * D]), w_q=I("w_q", [D, D]), qT=qT, kT=kT, v=v)
    oTs = k.dram("oT_scr", [8, 128, TOKO], BF16)
    Ta = dict(qT=qT, kT=kT, v=v, masks=I("masks", [128, 8, 512], BF16), oT=oTs)
    d_sel = I("sel", [128, 2])
    out = dout(k, m, "out", [TOKO, D])
    Tf1 = dict(x=None, aT=oTs, w_o=I("sb_w_out", [D, D]), fnorm=I("fn1", [8, 128]), w_gu=I("w_gu1", [D, 2 * DFF]),
               w_d=I("w_d1", [DFF, D]), h=out)
    sel = k.sbuf([128, 2], F32, "sel_sb")
    k.dma("sp", sel[:], d_sel[:])

    emit_gdn(k, m, S, Tg, GDN_HPG, 8 // GDN_HPG)
    barrier(k)
    emit_dense_ffn(k, m, S, Tf0)
    barrier(k)
    emit_kvq(k, m, S, Tk)
    barrier(k)
    emit_attn(k, m, S, 8, Ta, parity=True, sel=sel)
    barrier(k)

    def xload(ti, x_t, xtmp):
        j, r = ti // 4, ti % 4
        ra = (2 * j) * 512 + r * 128
        rb = (2 * j + 1) * 512 + r * 128
        k.dma("sp", xtmp[0][:], h2[ra:ra + 128, :])
        k.dma("sp", xtmp[1][:], h2[rb:rb + 128, :])
        k.ts("dve", xtmp[0][:], xtmp[0][:], sel[:, 0:1], ALU.mult)
        k.stt(x_t[:], xtmp[1][:], sel[:, 1:2], xtmp[0][:], ALU.mult, ALU.add)
    emit_dense_ffn(k, m, TOKO, Tf1, xload=xload)
    k.final_wait("sp", [out])
    k.emit()
    return nc


def host_parity_masks(p):
    c = host_masks()
    z = np.zeros_like(c)
    f = np.full_like(c, NEG)
    return _c(np.concatenate([z, c], axis=1) if p == 1 else np.concatenate([c, f], axis=1))


def kernel_fused(inp, x):
    B, S, _ = x.shape
    cores = list(range(2 * B))
    cf, cb = host_consts()
    w_in = inp["gdn_w_in"][0]
    cw = inp["gdn_conv_w"][0]

    def grp(a, width, hpg=GDN_HPG):
        nblk = a.shape[1] // (8 * width)
        parts = []
        for G in range(8 // hpg):
            for blk in range(nblk):
                parts.append(a[:, blk * 8 * width + G * hpg * width: blk * 8 * width + (G + 1) * hpg * width])
        return _c(np.concatenate(parts, axis=1))
    common = dict(cf=cf, cb=cb, an0=_c(inp["attn_norm"][0].reshape(8, 128)), w_qkvz=grp(w_in[:, 0:4096], 128),
                  w_bg=grp(w_in[:, 4096:4112], 1), conv=grp(cw, 128), alog=_c(inp["gdn_a_log"].reshape(1, 8)),
                  dtb=_c(inp["gdn_dt_bias"].reshape(1, 8)), ogain=_c(inp["gdn_o_gain"].reshape(1, 128)),
                  gdn_w_out=_c(inp["gdn_w_out"][0]), fn0=_c(inp["ffn_norm"][0].reshape(8, 128)),
                  w_gu0=_c(inp["ffn_w_gu"][0]), w_d0=_c(inp["ffn_w_down"][0]),
                  kvn=_c(inp["kv_norm"].reshape(8, 128)), an1=_c(inp["attn_norm"][1].reshape(8, 128)),
                  kg=_c(inp["k_gain"].reshape(1, 128)), qg=_c(inp["sb_q_gain"].reshape(1, 128)),
                  w_kv=_c(inp["w_kv"]), w_q=_c(inp["sb_w_q"][0]), sb_w_out=_c(inp["sb_w_out"][0]),
                  fn1=_c(inp["ffn_norm"][1].reshape(8, 128)), w_gu1=_c(inp["ffn_w_gu"][1]), w_d1=_c(inp["ffn_w_down"][1]))
    feeds = []
    for c in cores:
        b, p = c // 2, c % 2
        selv = np.empty((128, 2), np.float32)
        selv[:, 0] = 1.0 - p
        selv[:, 1] = float(p)
        feeds.append(dict(common, x=_c(x[b]), masks=host_parity_masks(p), sel=selv))
    res = run_bass_kernel_spmd(_prog("fused", build_fused, S), feeds, core_ids=cores).results
    out = np.empty((B, S, D), np.float32)
    for c in cores:
        b, p = c // 2, c % 2
        o = res[c]["out"]
        for j in range(S // 1024):
            out[b, (2 * j + p) * 512:(2 * j + p + 1) * 512] = o[j * 512:(j + 1) * 512]
    return out

from concourse.bass_utils import run_bass_kernel_spmd

_PROGS = {}


def _prog(name, fn, *args):
    key = (name,) + args
    if key not in _PROGS:
        _PROGS[key] = fn(*args)
    return _PROGS[key]


def _c(a):
    return np.ascontiguousarray(a)


def kernel_unfused(**inputs):
    inp = {n: np.asarray(v) for n, v in inputs.items()}
    x = inp["x"].astype(np.float32, copy=False)
    B, S, _ = x.shape
    NC = 8
    HALF = S // 2
    cores = list(range(NC))
    cf, cb = host_consts()
    cst = {"cf": cf, "cb": cb}

    w_in = inp["gdn_w_in"][0]
    cw = inp["gdn_conv_w"][0]
    feeds = []
    for c in cores:
        b, hg = c // 2, c % 2
        hs = slice(hg * 512, (hg + 1) * 512)
        h4 = slice(hg * 4, (hg + 1) * 4)
        feeds.append(dict(cst, x=_c(x[b]), an=_c(inp["attn_norm"][0].reshape(8, 128)),
                          w_qkvz=_c(np.concatenate([w_in[:, 0:1024][:, hs], w_in[:, 1024:2048][:, hs],
                                                    w_in[:, 2048:3072][:, hs], w_in[:, 3072:4096][:, hs]], axis=1)),
                          w_bg=_c(np.concatenate([w_in[:, 4096:4104][:, h4], w_in[:, 4104:4112][:, h4]], axis=1)),
                          conv=_c(np.concatenate([cw[:, 0:1024][:, hs], cw[:, 1024:2048][:, hs], cw[:, 2048:3072][:, hs]], axis=1)),
                          alog=_c(inp["gdn_a_log"][:, h4]), dtb=_c(inp["gdn_dt_bias"][:, h4]),
                          ogain=_c(inp["gdn_o_gain"].reshape(1, 128))))
    r1 = run_bass_kernel_spmd(_prog("gdn", build_gdn, S, 4), feeds, core_ids=cores).results
    ogT = [np.concatenate([r1[2 * b]["ogT"], r1[2 * b + 1]["ogT"]], axis=0) for b in range(B)]

    nc_ffn = _prog("ffn", build_dense_ffn, HALF)
    feeds = []
    for c in cores:
        b, hf = c // 2, c % 2
        ts = slice(hf * HALF, (hf + 1) * HALF)
        feeds.append(dict(cst, x=_c(x[b, ts]), aT=_c(ogT[b][:, :, ts]), w_o=_c(inp["gdn_w_out"][0]),
                          fnorm=_c(inp["ffn_norm"][0].reshape(8, 128)), w_gu=_c(inp["ffn_w_gu"][0]),
                          w_d=_c(inp["ffn_w_down"][0])))
    r2 = run_bass_kernel_spmd(nc_ffn, feeds, core_ids=cores).results
    h2 = [r["h"] for r in r2]

    feeds = []
    for c in cores:
        feeds.append(dict(cst, h=_c(h2[c]), kvn=_c(inp["kv_norm"].reshape(8, 128)),
                          an=_c(inp["attn_norm"][1].reshape(8, 128)), kg=_c(inp["k_gain"].reshape(1, 128)),
                          qg=_c(inp["sb_q_gain"].reshape(1, 128)), w_kv=_c(inp["w_kv"]), w_q=_c(inp["sb_w_q"][0])))
    r3 = run_bass_kernel_spmd(_prog("kvq", build_kvq, HALF), feeds, core_ids=cores).results

    mk = host_masks()
    feeds = []
    for c in cores:
        b, hg = c // 2, c % 2
        h4 = slice(hg * 4, (hg + 1) * 4)
        qT = np.concatenate([r3[2 * b]["qT"][h4], r3[2 * b + 1]["qT"][h4]], axis=2)
        kT = np.concatenate([r3[2 * b]["kT"][h4], r3[2 * b + 1]["kT"][h4]], axis=2)
        v = np.concatenate([r3[2 * b]["v"], r3[2 * b + 1]["v"]], axis=0)[:, hg * 512:(hg + 1) * 512]
        feeds.append(dict(cst, qT=_c(qT), kT=_c(kT), v=_c(v), masks=mk))
    r4 = run_bass_kernel_spmd(_prog("attn", build_attn, S, 4), feeds, core_ids=cores).results
    oT = [np.concatenate([r4[2 * b]["oT"], r4[2 * b + 1]["oT"]], axis=0) for b in range(B)]

    feeds = []
    for c in cores:
        b, hf = c // 2, c % 2
        ts = slice(hf * HALF, (hf + 1) * HALF)
        feeds.append(dict(cst, x=_c(h2[c]), aT=_c(oT[b][:, :, ts]), w_o=_c(inp["sb_w_out"][0]),
                          fnorm=_c(inp["ffn_norm"][1].reshape(8, 128)), w_gu=_c(inp["ffn_w_gu"][1]),
                          w_d=_c(inp["ffn_w_down"][1])))
    r5 = run_bass_kernel_spmd(_prog("ffn_b", build_dense_ffn, HALF), feeds, core_ids=cores).results
    out = np.empty((B, S, D), np.float32)
    for c in cores:
        b, hf = c // 2, c % 2
        out[b, hf * HALF:(hf + 1) * HALF] = r5[c]["h"]
    return out


def kernel(**inputs):
    inp = {n: np.asarray(v) for n, v in inputs.items()}
    x = inp["x"].astype(np.float32, copy=False)
    return kernel_fused(inp, x)
```

```python
from contextlib import ExitStack
import numpy as np
import concourse.bass as bass
import concourse.mybir as mybir

F32 = mybir.dt.float32
BF16 = mybir.dt.bfloat16
ALU = mybir.AluOpType
AF = mybir.ActivationFunctionType
AX = mybir.AxisListType

SAME_ENGINE_SYNC = True
N_DMA_SEMS = 6


class Buf:
    def __init__(self, t, name):
        self.t = t
        self.name = name
        self.reads = {}
        self.writes = {}

    def __getitem__(self, idx):
        return View(self, self.t[idx])


class View:
    def __init__(self, buf, ap):
        self.buf = buf
        self.ap = ap


def _ap(x):
    return x.ap if isinstance(x, View) else x


class Eng:
    def __init__(self, name):
        self.name = name
        self.ops = []
        self.trace = []
        self.count = 0
        self.waited = {}
        self.sem = None
        self.dma_sems = []
        self.dma_vals = []
        self.dma_k = 0


class K:
    def __init__(self, nc):
        self.nc = nc
        self.es = ExitStack()
        self.engs = {n: Eng(n) for n in ("pe", "act", "dve", "pool", "sp")}
        self.sems = {}
        for n, e in self.engs.items():
            e.sem = self.es.enter_context(nc.semaphore("s_" + n))
            self.sems[n] = e.sem
        for n in ("sp", "pool", "act"):
            e = self.engs[n]
            for i in range(N_DMA_SEMS):
                s = self.es.enter_context(nc.semaphore("d_%s%d" % (n, i)))
                key = "d_%s%d" % (n, i)
                self.sems[key] = s
                e.dma_sems.append(key)
                e.dma_vals.append(0)
        self.nbuf = 0

    def sbuf(self, shape, dtype, name=None):
        self.nbuf += 1
        name = name or "sb%d" % self.nbuf
        t = self.es.enter_context(self.nc.sbuf_tensor(name, list(shape), dtype))
        return Buf(t, name)

    def psum(self, shape, dtype, name=None):
        self.nbuf += 1
        name = name or "ps%d" % self.nbuf
        t = self.es.enter_context(self.nc.psum_tensor(name, list(shape), dtype))
        return Buf(t, name)

    def dram(self, name, shape, dtype, kind="Internal"):
        t = self.nc.dram_tensor(name, list(shape), dtype, kind=kind)
        return Buf(t.ap(), name)

    def track(self, ap, name):
        return Buf(ap, name)

    def _wait(self, e, key, val, war=False):
        if key == e.name and (e.name == "pe" or war or not SAME_ENGINE_SYNC):
            return
        if e.waited.get(key, 0) >= val:
            return
        e.waited[key] = val
        sem = self.sems[key]
        e.ops.append(("wait", key, val))
        e.trace.append(("w", key, val))

    def _deps(self, e, reads, writes, nowaw=False):
        for v in reads:
            if isinstance(v, View):
                for k, val in v.buf.writes.items():
                    self._wait(e, k, val)
        for v in writes:
            if isinstance(v, View):
                for k, val in v.buf.reads.items():
                    self._wait(e, k, val, war=True)
                if not nowaw:
                    for k, val in v.buf.writes.items():
                        self._wait(e, k, val)

    def _mark(self, key, val, reads, writes):
        for v in reads:
            if isinstance(v, View):
                if v.buf.reads.get(key, 0) < val:
                    v.buf.reads[key] = val
        for v in writes:
            if isinstance(v, View):
                if v.buf.writes.get(key, 0) < val:
                    v.buf.writes[key] = val

    def op(self, eng, fn, reads, writes, nowaw=False):
        e = self.engs[eng]
        self._deps(e, reads, writes, nowaw)
        e.count += 1
        sem = e.sem
        e.ops.append(("op", fn, e.count))
        e.trace.append(("i", e.name, 1))
        self._mark(e.name, e.count, reads, writes)

    def dma(self, eng, out, in_, nowaw=True, **kw):
        e = self.engs[eng]
        self._deps(e, [in_], [out], nowaw)
        i = e.dma_k % N_DMA_SEMS
        e.dma_k += 1
        key = e.dma_sems[i]
        if e.dma_vals[i] > 0:
            self._wait(e, key, e.dma_vals[i])
        e.dma_vals[i] += 16
        val = e.dma_vals[i]
        sem = self.sems[key]
        o, s = _ap(out), _ap(in_)
        e.ops.append(("dma", o, s, key, kw))
        e.trace.append(("i", key, 16))
        self._mark(key, val, [in_], [out])
        return (key, val)

    def allgather(self, out, in_, groups):
        e = self.engs["pool"]
        self._deps(e, [in_], [out], False)
        i = e.dma_k % N_DMA_SEMS
        e.dma_k += 1
        key = e.dma_sems[i]
        if e.dma_vals[i] > 0:
            self._wait(e, key, e.dma_vals[i])
        e.dma_vals[i] += 16
        val = e.dma_vals[i]
        o, s_ = _ap(out), _ap(in_)
        e.ops.append(("cc", o, s_, key, groups))
        e.trace.append(("i", key, 16))
        self._mark(key, val, [in_], [out])

    def final_wait(self, eng, bufs):
        e = self.engs[eng]
        for b in bufs:
            for k, val in b.writes.items():
                self._wait(e, k, val)

    def matmul(self, out, lhsT, rhs, start=True, stop=True, **kw):
        o, l, r = _ap(out), _ap(lhsT), _ap(rhs)
        self.op("pe", lambda h: h.matmul(o, l, r, start=start, stop=stop, **kw), [lhsT, rhs], [out])

    def transpose(self, out, in_, ident):
        o, i, d = _ap(out), _ap(in_), _ap(ident)
        self.op("pe", lambda h: h.transpose(o, i, d), [in_, ident], [out])

    def act(self, out, in_, func, bias=None, scale=None, accum_out=None, eng="act"):
        o, i = _ap(out), _ap(in_)
        kw = {}
        rd = [in_]
        wr = [out]
        if bias is not None:
            kw["bias"] = _ap(bias)
            rd.append(bias)
        if scale is not None:
            kw["scale"] = _ap(scale)
            rd.append(scale)
        if accum_out is not None:
            kw["accum_out"] = _ap(accum_out)
            wr.append(accum_out)
        self.op("act", lambda h: h.activation(o, i, func, **kw), rd, wr)

    def tt(self, eng, out, in0, in1, op):
        o, a, b = _ap(out), _ap(in0), _ap(in1)
        self.op(eng, lambda h: h.tensor_tensor(o, a, b, op), [in0, in1], [out])

    def ts(self, eng, out, in0, s1, op0, s2=None, op1=None, accum_out=None):
        o, a = _ap(out), _ap(in0)
        rd = [in0]
        wr = [out]
        if isinstance(s1, View):
            rd.append(s1)
        if isinstance(s2, View):
            rd.append(s2)
        kw = {}
        if op1 is not None:
            kw["op1"] = op1
        if accum_out is not None:
            kw["accum_out"] = _ap(accum_out)
            wr.append(accum_out)
        x1, x2 = _ap(s1), _ap(s2)
        self.op(eng, lambda h: h.tensor_scalar(o, a, x1, x2, op0, **kw), rd, wr)

    def tt_bc(self, out, in0, sc, ng, op):
        o3 = _ap(out).rearrange("p (s d) -> p s d", s=ng)
        a3 = _ap(in0).rearrange("p (s d) -> p s d", s=ng)
        b3 = _ap(sc).unsqueeze(2).broadcast_to([128, ng, 128])
        self.op("dve", lambda h: h.tensor_tensor(o3, a3, b3, op), [in0, sc], [out])

    def stt(self, out, in0, scalar, in1, op0, op1, eng="dve"):
        o, a, b = _ap(out), _ap(in0), _ap(in1)
        rd = [in0, in1]
        if isinstance(scalar, View):
            rd.append(scalar)
        s = _ap(scalar)
        self.op(eng, lambda h: h.scalar_tensor_tensor(o, a, s, b, op0, op1), rd, [out])

    def copy(self, eng, out, in_):
        o, i = _ap(out), _ap(in_)
        if eng == "act":
            self.op("act", lambda h: h.copy(o, i), [in_], [out])
        else:
            self.op(eng, lambda h: h.tensor_copy(o, i), [in_], [out])

    def memset(self, eng, out, val):
        o = _ap(out)
        self.op(eng, lambda h: h.memset(o, val), [], [out])

    def recip(self, out, in_):
        o, i = _ap(out), _ap(in_)
        self.op("dve", lambda h: h.reciprocal(o, i), [in_], [out])

    def affine_select(self, out, in_, pattern, cmp, fill, base, cm):
        o, i = _ap(out), _ap(in_)
        self.op("pool", lambda h: h.affine_select(o, i, pattern, cmp, fill, base=base, channel_multiplier=cm),
                [in_], [out])

    def check_deadlock(self):
        vals = {}
        pos = {n: 0 for n in self.engs}
        progress = True
        while progress:
            progress = False
            for n, e in self.engs.items():
                while pos[n] < len(e.trace):
                    kind, key, v = e.trace[pos[n]]
                    if kind == "w":
                        if vals.get(key, 0) >= v:
                            pos[n] += 1
                            progress = True
                        else:
                            break
                    else:
                        vals[key] = vals.get(key, 0) + v
                        pos[n] += 1
                        progress = True
        stuck = {n: (pos[n], len(e.trace), e.trace[pos[n]] if pos[n] < len(e.trace) else None)
                 for n, e in self.engs.items()}
        ok = all(pos[n] == len(e.trace) for n, e in self.engs.items())
        return ok, stuck, vals

    def emit(self):
        ok, stuck, vals = self.check_deadlock()
        if not ok:
            raise RuntimeError("DEADLOCK in sync graph: %s" % (stuck,))
        nc = self.nc
        import bisect
        sig = {n: set() for n in self.engs}
        for n, e in self.engs.items():
            for it in e.ops:
                if it[0] == "wait" and it[1] in sig:
                    sig[it[1]].add(it[2])
        sigl = {n: sorted(v) for n, v in sig.items()}
        sems = self.sems

        def run(n, h):
            e = self.engs[n]
            for it in e.ops:
                if it[0] == "wait":
                    key, val = it[1], it[2]
                    if key in sigl:
                        val = bisect.bisect_right(sigl[key], val)
                    h.wait_ge(sems[key], val)
                elif it[0] == "op":
                    ins = it[1](h)
                    if it[2] in sig[n]:
                        ins.then_inc(e.sem, 1)
                elif it[0] == "cc":
                    _, o, s_, key, groups = it
                    h.collective_compute("AllGather", ALU.bypass, replica_groups=groups, ins=[s_], outs=[o]).then_inc(sems[key], 16)
                else:
                    _, o, s_, key, kw = it
                    h.dma_start(out=o, in_=s_, **kw).then_inc(sems[key], 16)
        self.n_inc = {n: len(v) for n, v in sig.items()}
        with nc.Block() as block:
            @block.tensor
            def _(h):
                run("pe", h)

            @block.scalar
            def _(h):
                run("act", h)

            @block.vector
            def _(h):
                run("dve", h)

            @block.gpsimd
            def _(h):
                run("pool", h)

            @block.sync
            def _(h):
                run("sp", h)
        self.es.close()


D = 1024
H = 8
DH = 128
DFF = 2816
PROJ = 4112
EPS = 1e-6
NEG = -30000.0


class M:
    pass


def host_consts():
    r = np.arange(128)[:, None]
    c = np.arange(128)[None, :]
    mats = [r == c, np.ones((128, 128)), r <= c, r > c, c >= r]
    b = 1
    while b < 128:
        mats.append(((r // b) == (c // b) + 1) & ((r // b) % 2 == 1))
        b *= 2
    cf = np.stack([np.asarray(x, np.float32) for x in mats], axis=1)
    import ml_dtypes
    bmats = [r == c, np.ones((128, 128)), -(r >= c).astype(np.float32)]
    cb = np.stack([np.asarray(x, np.float32) for x in bmats], axis=1).astype(ml_dtypes.bfloat16)
    return np.ascontiguousarray(cf), np.ascontiguousarray(cb)


def setup_consts(k, m):
    cfd = k.dram("cf", [128, 12, 128], F32, kind="ExternalInput")
    cbd = k.dram("cb", [128, 3, 128], BF16, kind="ExternalInput")
    cf = k.sbuf([128, 12, 128], F32, "cfs")
    cb = k.sbuf([128, 3, 128], BF16, "cbs")
    k.dma("sp", cf[:], cfd[:])
    k.dma("sp", cb[:], cbd[:])
    m.cf, m.cb = cf, cb

    class V:
        def __init__(self, buf, j):
            self.buf, self.j = buf, j

        def __getitem__(self, idx):
            if idx == slice(None):
                return self.buf[:, self.j, :]
            a, b = idx
            return self.buf[a, self.j, b]
    m.ident = V(cf, 0)
    m.ones = V(cf, 1)
    m.triI = V(cf, 2)
    m.lmask = V(cf, 3)
    m.umask = V(cf, 4)
    m.lvl = [V(cf, 5 + i) for i in range(7)]
    m.identb = V(cb, 0)
    m.onesb = V(cb, 1)
    m.ntriS = V(cb, 2)


class Arena:
    def __init__(self, k, nbytes):
        self.k = k
        self.n32 = nbytes // 4
        self.base = k.es.enter_context(k.nc.sbuf_tensor("arena", [128, self.n32], F32))
        self.off = 0
        self.cnt = 0

    def reset(self):
        self.off = 0

    def alloc(self, shape, dtype, name=None):
        nel = int(np.prod(shape[1:]))
        nb = nel * (2 if dtype == BF16 else 4)
        n32 = (nb + 3) // 4
        n32 = (n32 + 7) // 8 * 8
        assert self.off + n32 <= self.n32, "arena overflow %d + %d > %d" % (self.off, n32, self.n32)
        ap = self.base[0:shape[0], self.off:self.off + n32]
        self.off += n32
        if dtype != F32:
            ap = ap.bitcast(dtype)
        ap = ap[:, 0:nel]
        if len(shape) == 3:
            ap = ap.rearrange("p (a b) -> p a b", a=shape[1])
        elif len(shape) == 4:
            ap = ap.rearrange("p (a b c) -> p a b c", a=shape[1], b=shape[2])
        self.cnt += 1
        return Buf(ap, name or "ar%d" % self.cnt)


def barrier(k):
    latest = {}
    for n, e in k.engs.items():
        if e.count:
            latest[n] = e.count
        for key, v in zip(e.dma_sems, e.dma_vals):
            if v:
                latest[key] = v
    for n, e in k.engs.items():
        for key, v in latest.items():
            if key == n:
                continue
            k._wait(e, key, v)


def new_prog():
    nc = bass.Bass("TRN2", target_bir_lowering=False)
    k = K(nc)
    m = M()
    m.ins = {}
    m.outs = {}
    setup_consts(k, m)
    m.pb = [k.psum([128, 512], F32, "pb%d" % i) for i in range(8)]
    m.pwi = 0

    def pw():
        b = m.pb[m.pwi % 8]
        m.pwi += 1
        return b
    m.pw = pw
    m.arena = Arena(k, 196 * 1024)
    return nc, k, m


def din(k, m, name, shape, dt=F32):
    b = k.dram(name, shape, dt, kind="ExternalInput")
    m.ins[name] = b
    return b


def dout(k, m, name, shape, dt=F32):
    b = k.dram(name, shape, dt, kind="ExternalOutput")
    m.outs[name] = b
    return b


def load_cols(k, m, rows):
    al = m.arena.alloc
    st = al([128, 128], F32)
    out = al([128, 128], F32)
    k.memset("dve", st[:], 0.0)
    r0 = 0
    for v, n in rows:
        k.dma("sp", st[r0:r0 + n, :], v, nowaw=False)
        r0 += n
    p = m.pw()
    k.transpose(p[:, 0:128], st[:], m.ident[:])
    k.copy("dve", out[:], p[:, 0:128])
    return out


def load_w(k, dst, src, kchunks):
    for kc in range(kchunks):
        k.dma("pool", dst[:, kc, :], src(kc), max_dma_last_dim=4096)


def rstd_of(k, junk, ss, src, n):
    k.act(junk[:, 0:n], src, AF.Square, accum_out=ss[:])
    k.ts("dve", ss[:], ss[:], 1.0 / n, ALU.mult, EPS, ALU.add)
    k.act(ss[:], ss[:], AF.Ln)
    k.act(ss[:], ss[:], AF.Exp, scale=-0.5)


def norm_T(k, m, xs, gains, outs):
    for half in range(2):
        p = m.pw()
        for c4 in range(4):
            c = half * 4 + c4
            k.transpose(p[:, c4 * 128:(c4 + 1) * 128], xs[:, c * 128:(c + 1) * 128], m.ident[:])
        for g, o in zip(gains, outs):
            o3 = _ap(o[:, half * 4:half * 4 + 4, :])
            a3 = _ap(p[:]).rearrange("p (s d) -> p s d", s=4)
            b3 = _ap(g)[:, half * 4:half * 4 + 4].unsqueeze(2).broadcast_to([128, 4, 128])
            k.op("dve", lambda h, o3=o3, a3=a3, b3=b3: h.tensor_tensor(o3, a3, b3, ALU.mult), [p[:], g], [o[:]])


D = 1024
H = 8
DFF = 2816
EPS = 1e-6


def build_dense_ffn(TOK):
    nc, k, m = new_prog()
    T = dict(x=din(k, m, "x", [TOK, D]), aT=din(k, m, "aT", [8, 128, TOK], BF16), w_o=din(k, m, "w_o", [D, D]),
             fnorm=din(k, m, "fnorm", [8, 128]), w_gu=din(k, m, "w_gu", [D, 2 * DFF]), w_d=din(k, m, "w_d", [DFF, D]),
             h=dout(k, m, "h", [TOK, D]))
    emit_dense_ffn(k, m, TOK, T)
    k.final_wait("sp", [T["h"]])
    k.emit()
    return nc


def emit_dense_ffn(k, m, TOK, T, xload=None):
    m.arena.reset()
    al = m.arena.alloc
    d_x, d_aT, d_wo, d_fn, d_wgu, d_wd, d_h = T["x"], T["aT"], T["w_o"], T["fnorm"], T["w_gu"], T["w_d"], T["h"]
    cols = load_cols(k, m, [(d_fn[:], 8)])
    w_o = al([128, 8, D], BF16, "w_o")
    w_gu = al([128, 8, 2 * DFF], BF16, "w_gu")
    w_d = al([128, 22, D], BF16, "w_d")
    load_w(k, w_o, lambda kc: d_wo[kc * 128:(kc + 1) * 128, :], 8)
    load_w(k, w_gu, lambda kc: d_wgu[kc * 128:(kc + 1) * 128, :], 8)
    load_w(k, w_d, lambda kc: d_wd[kc * 128:(kc + 1) * 128, :], 22)
    junk = al([128, 1024], BF16)
    ss = al([128, 1], F32)
    xt = [al([128, D], F32, "xt%d" % i) for i in range(1 if xload is not None else 2)]
    at = [al([128, 8, 128], BF16, "at%d" % i) for i in range(2)]
    h1 = [al([128, D], F32, "h1_%d" % i) for i in range(2)]
    h2 = [al([128, D], F32, "h2_%d" % i) for i in range(1 if xload is not None else 2)]
    xtmp = [al([128, D], F32, "xtmp%d" % i) for i in range(2)] if xload is not None else None
    xs = al([128, D], F32, "xs")
    hnT = al([128, 8, 128], BF16, "hnT")
    hidF = al([128, 22 * 128], BF16, "hidT")
    hidT = Buf(hidF.t.rearrange("p (a b) -> p a b", a=22), "hidT3")
    hidT.reads, hidT.writes = hidF.reads, hidF.writes
    sg = [al([128, 256], F32, "sg%d" % i) for i in range(2)]
    aTv = k.track(d_aT.t.rearrange("h p t -> p h t"), "aTv")
    for ti in range(TOK // 128):
        x_t, a_t, h1_, h2_ = xt[ti % len(xt)], at[ti % 2], h1[ti % 2], h2[ti % len(h2)]
        if xload is None:
            k.dma("sp", x_t[:], d_x[ti * 128:(ti + 1) * 128, :])
        else:
            xload(ti, x_t, xtmp)
        k.dma("sp", a_t[:], aTv[:, :, ti * 128:(ti + 1) * 128])
        for half in range(2):
            py = m.pw()
            for h in range(8):
                k.matmul(py[:], a_t[:, h, :], w_o[:, h, half * 512:(half + 1) * 512], start=(h == 0), stop=(h == 7))
            k.tt("dve", h1_[:, half * 512:(half + 1) * 512], py[:], x_t[:, half * 512:(half + 1) * 512], ALU.add)
        rstd_of(k, junk, ss, h1_[:], D)
        k.ts("dve", xs[:], h1_[:], ss[:], ALU.mult)
        norm_T(k, m, xs, [cols[:, 0:8]], [hnT])
        for j2 in range(11):
            p = m.pw()
            for q in range(4):
                j = j2 * 2 + (q % 2)
                col = (0 if q < 2 else DFF) + j * 128
                for kc in range(8):
                    k.matmul(p[:, q * 128:(q + 1) * 128], w_gu[:, kc, col:col + 128], hnT[:, kc, :],
                             start=(kc == 0), stop=(kc == 7))
            s_ = sg[j2 % 2]
            k.act(s_[:], p[:, 0:256], AF.Silu)
            k.tt("dve", hidF[:, j2 * 256:(j2 + 1) * 256], p[:, 256:512], s_[:], ALU.mult)
        for half in range(2):
            py = m.pw()
            for j in range(22):
                k.matmul(py[:], hidT[:, j, :], w_d[:, j, half * 512:(half + 1) * 512], start=(j == 0), stop=(j == 21))
            k.tt("dve", h2_[:, half * 512:(half + 1) * 512], py[:], h1_[:, half * 512:(half + 1) * 512], ALU.add)
        k.dma("sp", d_h[ti * 128:(ti + 1) * 128, :], h2_[:])


def build_kvq(TOK):
    nc, k, m = new_prog()
    T = dict(h=din(k, m, "h", [TOK, D]), kvn=din(k, m, "kvn", [8, 128]), an=din(k, m, "an", [8, 128]),
             kg=din(k, m, "kg", [1, 128]), qg=din(k, m, "qg", [1, 128]), w_kv=din(k, m, "w_kv", [D, 2 * D]),
             w_q=din(k, m, "w_q", [D, D]), qT=dout(k, m, "qT", [8, 128, TOK], BF16),
             kT=dout(k, m, "kT", [8, 128, TOK], BF16), v=dout(k, m, "v", [TOK, D], BF16))
    outs = emit_kvq(k, m, TOK, T)
    k.final_wait("sp", outs)
    k.emit()
    return nc


def emit_kvq(k, m, TOK, T):
    m.arena.reset()
    al = m.arena.alloc
    d_h, d_kvn, d_an, d_kg, d_qg, d_wkv, d_wq = T["h"], T["kvn"], T["an"], T["kg"], T["qg"], T["w_kv"], T["w_q"]
    d_qT, d_kT, d_v = T["qT"], T["kT"], T["v"]
    cols = load_cols(k, m, [(d_kvn[:], 8), (d_an[:], 8), (d_kg[:], 1), (d_qg[:], 1)])
    w_kv = al([128, 8, 2 * D], BF16, "w_kv")
    w_q = al([128, 8, D], BF16, "w_q")
    load_w(k, w_kv, lambda kc: d_wkv[kc * 128:(kc + 1) * 128, :], 8)
    load_w(k, w_q, lambda kc: d_wq[kc * 128:(kc + 1) * 128, :], 8)
    junk = al([128, 1024], BF16)
    ss = al([128, 1], F32)
    ht = [al([128, D], F32, "ht%d" % i) for i in range(2)]
    xs = al([128, D], F32, "xs")
    xkT = al([128, 8, 128], BF16, "xkT")
    xqT = al([128, 8, 128], BF16, "xqT")
    vt = [al([128, D], BF16, "vt%d" % i) for i in range(2)]
    ss8 = {n: al([128, 8], F32, "ss8" + n) for n in "kq"}
    nrm = {n: al([128, D], F32, "nrm" + n) for n in "kq"}
    oT = {n: [al([128, 8 * 128], BF16, "oT%s%d" % (n, i)) for i in range(2)] for n in "kq"}
    qTv = k.track(d_qT.t.rearrange("h p t -> p h t"), "qTv")
    kTv = k.track(d_kT.t.rearrange("h p t -> p h t"), "kTv")
    qscale = 128.0 ** -0.5
    for ti in range(TOK // 128):
        h_t = ht[ti % 2]
        k.dma("sp", h_t[:], d_h[ti * 128:(ti + 1) * 128, :])
        rstd_of(k, junk, ss, h_t[:], D)
        k.ts("dve", xs[:], h_t[:], ss[:], ALU.mult)
        norm_T(k, m, xs, [cols[:, 0:8], cols[:, 8:16]], [xkT, xqT])
        v_t = vt[ti % 2]
        for n4 in (2, 3):
            py = m.pw()
            for kc in range(8):
                k.matmul(py[:], xkT[:, kc, :], w_kv[:, kc, n4 * 512:(n4 + 1) * 512], start=(kc == 0), stop=(kc == 7))
            k.copy("act", v_t[:, (n4 - 2) * 512:(n4 - 1) * 512], py[:])
        k.dma("sp", d_v[ti * 128:(ti + 1) * 128, :], v_t[:])
        for which, xT, w, gcol, dview in (("k", xkT, w_kv, 16, kTv), ("q", xqT, w_q, 17, qTv)):
            pys = []
            for n4 in range(2):
                py = m.pw()
                for kc in range(8):
                    k.matmul(py[:], xT[:, kc, :], w[:, kc, n4 * 512:(n4 + 1) * 512], start=(kc == 0), stop=(kc == 7))
                pys.append(py)
                for h4 in range(4):
                    hh = n4 * 4 + h4
                    k.act(junk[:, 0:128], py[:, h4 * 128:(h4 + 1) * 128], AF.Square, accum_out=ss8[which][:, hh:hh + 1])
            s8 = ss8[which]
            k.ts("dve", s8[:], s8[:], 1.0 / 128, ALU.mult, EPS, ALU.add)
            k.act(s8[:], s8[:], AF.Ln)
            k.act(s8[:], s8[:], AF.Exp, scale=-0.5)
            nr = nrm[which]
            for n4 in range(2):
                k.tt_bc(nr[:, n4 * 512:(n4 + 1) * 512], pys[n4][:], s8[:, n4 * 4:n4 * 4 + 4], 4, ALU.mult)
            o_ = oT[which][ti % 2]
            for n4 in range(2):
                p = m.pw()
                for h4 in range(4):
                    hh = n4 * 4 + h4
                    k.transpose(p[:, h4 * 128:(h4 + 1) * 128], nr[:, hh * 128:(hh + 1) * 128], m.ident[:])
                if which == "k":
                    k.ts("dve", o_[:, n4 * 512:(n4 + 1) * 512], p[:], cols[:, gcol:gcol + 1], ALU.mult)
                else:
                    k.ts("dve", o_[:, n4 * 512:(n4 + 1) * 512], p[:], cols[:, gcol:gcol + 1], ALU.mult, qscale, ALU.mult)
            o3 = k.track(o_.t.rearrange("p (a b) -> p a b", a=8), "o3")
            o3.reads, o3.writes = o_.reads, o_.writes
            k.dma("sp", dview[:, :, ti * 128:(ti + 1) * 128], o3[:])
    return [d_v, qTv, kTv]


def host_masks():
    import ml_dtypes
    s = np.arange(128)[:, None, None]
    jj = np.arange(4)[None, :, None]
    t = np.arange(512)[None, None, :]
    mk = np.where(jj * 128 + s < t, 0.0, NEG).astype(np.float32)
    return np.ascontiguousarray(mk.astype(ml_dtypes.bfloat16))


NEG = -30000.0


def build_attn(S, NH):
    nc, k, m = new_prog()
    T = dict(qT=din(k, m, "qT", [NH, 128, S], BF16), kT=din(k, m, "kT", [NH, 128, S], BF16),
             v=din(k, m, "v", [S, NH * 128], BF16), masks=din(k, m, "masks", [128, 4, 512], BF16),
             oT=dout(k, m, "oT", [NH, 128, S], BF16))
    emit_attn(k, m, S, NH, T)
    k.final_wait("sp", [T["oT"]])
    k.emit()
    return nc


def emit_attn(k, m, S, NH, T, parity=False, sel=None):
    m.arena.reset()
    al = m.arena.alloc
    d_q, d_k, d_v, d_mk, d_o = T["qT"], T["kT"], T["v"], T["masks"], T["oT"]
    NB = S // 128
    NSB = S // 512
    NM = 8 if parity else 4
    masks = al([128, NM, 512], BF16, "masks_sb")
    k.dma("sp", masks[:], d_mk[:])
    qs = [al([128, S], BF16, "qs%d" % i) for i in range(2)]
    ks = [al([128, S], BF16, "ks%d" % i) for i in range(2)]
    vs = [al([128, NB, 128], BF16, "vs%d" % i) for i in range(2)]
    e_ = [al([128, 512], F32, "e%d" % i) for i in range(3)]
    sp_ = [al([128, 512], BF16, "sp%d" % i) for i in range(3)]
    att_ = [al([128, 512], BF16, "att%d" % i) for i in range(2)]
    fac_ = [al([128, 4], F32, "fac%d" % i) for i in range(2)]
    accs = [al([128, 512], F32, "acc%d" % i) for i in range(2)]
    tmp = al([128, 512], F32, "acctmp")
    ot_ = [al([128, 512], BF16, "ot%d" % i) for i in range(2)]
    qsel = [al([128, 512], BF16, "qsel%d" % i) for i in range(2)]
    qtmp = al([128, 512], BF16, "qtmp")
    vview = k.track(d_v.t.rearrange("(blk p) c -> p blk c", p=128), "vview")
    pds = [Buf(m.pb[7].t[:, j * 64:j * 64 + 64], "pd%d" % j) for j in range(8)]
    ring = m.pb[0:7]
    rs = {"i": 0}

    def pw7():
        b_ = ring[rs["i"] % 7]
        rs["i"] += 1
        return b_
    old_pw = m.pw
    m.pw = pw7
    tiles = []
    for h in range(NH):
        for i in range(NSB // 2 if parity else NSB):
            nkb = 4 * (2 * i + 2) if parity else 4 * (i + 1)
            for kb in range(nkb):
                tiles.append((h, i, kb, nkb))
    state = {"cur": None}

    def stage_a(t):
        h, i, kb, nkb = tiles[t]
        q_h, k_h, v_h = qs[h % 2], ks[h % 2], vs[h % 2]
        if i == 0 and kb == 0:
            k.dma("sp", q_h[:], d_q[h, :, :])
            k.dma("sp", k_h[:], d_k[h, :, :])
            step = 16 if NB >= 16 else NB
            for b0 in range(0, NB, step):
                k.dma("sp", v_h[:, b0:b0 + step, :], vview[:, b0:b0 + step, h * 128:(h + 1) * 128])
        if parity:
            qb = qsel[(h * (NSB // 2) + i) % 2]
            if kb == 0:
                k.ts("dve", qtmp[:], q_h[:, (2 * i) * 512:(2 * i + 1) * 512], sel[:, 0:1], ALU.mult)
                k.stt(qb[:], q_h[:, (2 * i + 1) * 512:(2 * i + 2) * 512], sel[:, 1:2], qtmp[:], ALU.mult, ALU.add)
            qv = qb[:]
        else:
            qv = q_h[:, i * 512:(i + 1) * 512]
        jj = kb - (nkb - NM)
        diag = jj >= 0
        kv_ = k_h[:, kb * 128:(kb + 1) * 128]
        e, sp = e_[t % 3], sp_[t % 3]
        pz = m.pw()
        k.matmul(pz[:], kv_, qv, start=True, stop=not diag)
        if diag:
            k.matmul(pz[:], m.identb[:], masks[:, jj, :], start=False, stop=True)
        k.act(e[:], pz[:], AF.Exp)
        k.act(sp[:], e[:], AF.Ln, bias=1.0)
        pd = pds[t % len(pds)]
        return (kv_, qv, diag, jj, sp, v_h, pd, pz)

    def stage_pd(t, a):
        kb = tiles[t][2]
        sp, pd = a[4], a[6]
        if kb > 0:
            for sub in range(4):
                k.matmul(pd[:, sub:sub + 1], sp[:, sub * 128:(sub + 1) * 128], m.onesb[:, 0:1])

    def stage_b(t, a, mid):
        h, i, kb, nkb = tiles[t]
        kv_, qv, diag, jj, sp, v_h, pd, pz = a
        att, fac = att_[t % 2], fac_[t % 2]
        cur = state["cur"]
        nxt = accs[0] if cur is not accs[0] else accs[1]
        pl = pz
        k.matmul(pl[:], m.ntriS[:], sp[:], start=False, stop=True)
        mid()
        if kb > 0:
            k.act(fac[:], pd[:, 0:4], AF.Exp, scale=-1.0)
            k.tt_bc(tmp[:], cur[:], fac[:, 0:4], 4, ALU.mult)
        k.act(att[:], pl[:], AF.Exp)
        pP = m.pw()
        for sub in range(4):
            k.matmul(pP[:, sub * 128:(sub + 1) * 128], att[:, sub * 128:(sub + 1) * 128], v_h[:, kb, :])
        if kb == 0:
            k.copy("dve", nxt[:], pP[:])
        else:
            k.tt("dve", nxt[:], pP[:], tmp[:], ALU.add)
        state["cur"] = nxt
        if kb == nkb - 1:
            pT = m.pw()
            for sub in range(4):
                k.transpose(pT[:, sub * 128:(sub + 1) * 128], nxt[:, sub * 128:(sub + 1) * 128], m.ident[:])
            o_ = ot_[i % 2]
            k.copy("dve", o_[:], pT[:])
            k.dma("sp", d_o[h, :, i * 512:(i + 1) * 512], o_[:])

    LA = 2
    nt = len(tiles)
    A = {}
    for t0 in range(min(LA, nt)):
        A[t0] = stage_a(t0)
    stage_pd(0, A[0])
    for t in range(nt):
        if t + LA < nt:
            A[t + LA] = stage_a(t + LA)
        if t + 1 < nt:
            stage_b(t, A[t], lambda: stage_pd(t + 1, A[t + 1]))
        else:
            stage_b(t, A[t], lambda: None)
        del A[t]
    m.pw = old_pw


def build_gdn(S, NH=4, NGRP=1):
    nc, k, m = new_prog()
    W = NH * 128
    T = dict(x=din(k, m, "x", [S, D]), an=din(k, m, "an", [8, 128]), w_qkvz=din(k, m, "w_qkvz", [D, NGRP * 4 * W]),
             w_bg=din(k, m, "w_bg", [D, NGRP * 2 * NH]), conv=din(k, m, "conv", [4, NGRP * 3 * W]),
             alog=din(k, m, "alog", [1, NGRP * NH]), dtb=din(k, m, "dtb", [1, NGRP * NH]),
             ogain=din(k, m, "ogain", [1, 128]), ogT=dout(k, m, "ogT", [NGRP * NH, 128, S], BF16))
    ov = emit_gdn(k, m, S, T, NH, NGRP)
    print("gdn arena words used", m.arena.off, "of", m.arena.n32)
    k.final_wait("sp", [ov])
    k.emit()
    return nc


def emit_gdn(k, m, S, T, NH=4, NGRP=1):
    m.arena.reset()
    al = m.arena.alloc
    W = NH * 128
    d_x, d_an, d_w, d_wbg, d_conv, d_alog, d_dtb, d_og, d_o = (T["x"], T["an"], T["w_qkvz"], T["w_bg"], T["conv"],
                                                                  T["alog"], T["dtb"], T["ogain"], T["ogT"])
    NG = 3 * NH
    cols = load_cols(k, m, [(d_an[:], 8), (d_og[:], 1)])
    convc = load_cols(k, m, [(k.track(d_conv.t.rearrange("t (j p) -> (t j) p", p=128), "cwv")[:], 4 * NG * NGRP)])
    w = al([128, 8, NGRP * 4 * W], BF16, "w")
    wbg = al([128, 8, NGRP * 2 * NH], BF16, "wbg")
    load_w(k, w, lambda kc: d_w[kc * 128:(kc + 1) * 128, :], 8)
    load_w(k, wbg, lambda kc: d_wbg[kc * 128:(kc + 1) * 128, :], 8)
    alog = al([128, NGRP * NH], F32, "alog")
    dtb = al([128, NGRP * NH], F32, "dtb")
    k.dma("sp", alog[:], k.track(d_alog.t[0, :].partition_broadcast(128), "alv")[:])
    k.dma("sp", dtb[:], k.track(d_dtb.t[0, :].partition_broadcast(128), "dtv")[:])
    nexpA = al([128, NGRP * NH], F32, "nexpA")
    k.act(nexpA[:], alog[:], AF.Exp)
    k.ts("dve", nexpA[:], nexpA[:], -1.0, ALU.mult)
    ident4 = al([128, W], F32, "ident4")
    umask4 = al([128, W], F32, "umask4")
    lvl4 = [al([128, W], BF16, "lvl4_%d" % i) for i in range(7)]
    for h in range(NH):
        sl = slice(h * 128, (h + 1) * 128)
        k.copy("dve", ident4[:, sl], m.ident[:])
        k.copy("dve", umask4[:, sl], m.umask[:])
        for i in range(7):
            k.copy("dve", lvl4[i][:, sl], m.lvl[i][:])
    junk = al([128, 1024], BF16)
    junk2 = al([128, 1024], BF16)
    ss = al([128, 1], F32)
    xt = [al([128, D], F32, "xt%d" % i) for i in range(2)]
    xs = al([128, D], F32, "xs")
    xnTs = [al([128, 8, 128], BF16, "xnT%d" % i) for i in range(2)]
    cbs = [[al([128, NH, 131], BF16, "cb%d_%d" % (G, g)) for g in range(3)] for G in range(NGRP)]
    dgw = al([128, 4 * NG * NGRP, 128], BF16, "dgw")
    for c_ in range(4 * NG * NGRP):
        k.ts("dve", dgw[:, c_, :], m.identb[:], convc[:, c_:c_ + 1], ALU.mult)
    for G in range(NGRP):
        for g in range(3):
            k.memset("dve", cbs[G][g][:], 0.0)
    F2 = lambda n, dt=F32: al([128, W], dt, n)
    Ssts = [F2("S%d" % G) for G in range(NGRP)]
    Sbs = [F2("Sb%d" % G, BF16) for G in range(NGRP)]
    for G in range(NGRP):
        k.memset("dve", Ssts[G][:], 0.0)
        k.memset("dve", Sbs[G][:], 0.0)
    NSLOT = 2 if (NGRP >= 2 and NH <= 2) else 1

    def mk_slot(si):
        F = lambda n, dt=F32: al([128, W], dt, "%s_s%d" % (n, si))
        sl = {}
        sl["sil"] = [F("sil%d" % g_) for g_ in range(3)]
        for n in ("zs", "ri", "kTf", "vtm", "ktm", "dg", "tmpF", "Fm", "FU", "Am", "u", "t1", "o_", "on", "St"):
            sl[n] = F(n)
        sl["sq2"] = F("sq2", BF16)
        for n in ("qT", "kT", "Pa", "Pb", "Xa", "Xb", "Y", "attnT", "vb", "kb", "kd", "wT", "vnew"):
            sl[n] = F(n, BF16)
        sl["L"] = [F("L%d" % i, BF16) for i in range(7)]
        sl["ogt"] = [F("ogt%d" % i, BF16) for i in range(2)]
        sl["sm"] = {n: al([128, NH], F32, "%s_s%d" % (n, si)) for n in
                    ("beta", "g", "gc", "ngc", "glast", "egc", "ekd", "eglast", "sckb", "tmp8", "ss4")}
        return sl
    slots = [mk_slot(si) for si in range(NSLOT)]
    oview = k.track(d_o.t.rearrange("h p t -> p h t"), "oview")
    qscale = float(np.log(128.0 ** -0.5))
    HS = [slice(h * 128, (h + 1) * 128) for h in range(NH)]

    def mm4(p, lhs_of, rhs_of):
        for h in range(NH):
            k.matmul(p[:, HS[h]], lhs_of(h), rhs_of(h))

    NCH = S // 128

    def prologue(cj):
        x_t = xt[cj % 2]
        k.dma("sp", x_t[:], d_x[cj * 128:(cj + 1) * 128, :])
        rstd_of(k, junk2, ss, x_t[:], D)
        k.ts("dve", xs[:], x_t[:], ss[:], ALU.mult)
        norm_T(k, m, xs, [cols[:, 0:8]], [xnTs[cj % 2]])

    for ci in range(NCH):
        if ci == 0:
            prologue(0)
        xnT = xnTs[ci % 2]

        def gbody(G, sl):
            cb, Sst, Sb = cbs[G], Ssts[G], Sbs[G]
            dtbG, nexpAG = dtb[:, G * NH:(G + 1) * NH], nexpA[:, G * NH:(G + 1) * NH]
            sil, sm, L, ogt = sl["sil"], sl["sm"], sl["L"], sl["ogt"]
            zs, sq2, ri, kTf, vtm, ktm, dg, tmpF, Fm, FU, Am = (sl[n] for n in ("zs", "sq2", "ri", "kTf", "vtm", "ktm", "dg", "tmpF", "Fm", "FU", "Am"))
            u, t1, o_, on, St = (sl[n] for n in ("u", "t1", "o_", "on", "St"))
            qT, kT, Pa, Pb, Xa, Xb, Y, attnT, vb, kb, kd, wT, vnew = (sl[n] for n in ("qT", "kT", "Pa", "Pb", "Xa", "Xb", "Y", "attnT", "vb", "kb", "kd", "wT", "vnew"))
            yield
            banks = []
            for g in range(4):
                p = m.pw()
                for h in range(NH):
                    fc = g * NH + h
                    for kc in range(8):
                        k.matmul(p[:, HS[h]], w[:, kc, G * 4 * W + fc * 128:G * 4 * W + (fc + 1) * 128], xnT[:, kc, :], start=(kc == 0), stop=(kc == 7))
                banks.append(p)
            k.act(zs[:], banks[3][:, 0:W], AF.Silu)
            for g in range(3):
                src3 = View(banks[g], banks[g].t[:, 0:W].rearrange("p (s d) -> p s d", s=NH))
                k.copy("act", cb[g][:, :, 3:131], src3)
            for g in range(3):
                pc = m.pw()
                for h in range(NH):
                    j = g * NH + h
                    for t in range(4):
                        k.matmul(pc[:, HS[h]], dgw[:, t * NG * NGRP + G * NG + j, :], cb[g][:, h, t:t + 128],
                                 start=(t == 0), stop=(t == 3))
                k.copy("act", cb[g][:, :, 0:3], cb[g][:, :, 128:131])
                k.act(sil[g][:], pc[:, 0:W], AF.Silu)
            yield
            pbg = m.pw()
            for kc in range(8):
                k.matmul(pbg[:, 0:2 * NH], xnT[:, kc, :], wbg[:, kc, G * 2 * NH:(G + 1) * 2 * NH], start=(kc == 0), stop=(kc == 7))
            k.act(sm["beta"][:], pbg[:, 0:NH], AF.Exp, scale=-1.0)
            k.ts("dve", sm["beta"][:], sm["beta"][:], 1.0, ALU.add)
            k.recip(sm["beta"][:], sm["beta"][:])
            k.tt("dve", sm["g"][:], pbg[:, NH:2 * NH], dtbG, ALU.add)
            k.act(sm["g"][:], sm["g"][:], AF.Exp)
            k.act(sm["g"][:], sm["g"][:], AF.Ln, bias=1.0)
            k.tt("dve", sm["g"][:], sm["g"][:], nexpAG, ALU.mult)
            pg = m.pw()
            k.matmul(pg[:, 0:NH], m.triI[:], sm["g"][:])
            k.matmul(pg[:, 8:8 + NH], m.ones[:], sm["g"][:])
            k.copy("dve", sm["gc"][:], pg[:, 0:NH])
            k.copy("dve", sm["glast"][:], pg[:, 8:8 + NH])
            k.ts("dve", sm["ngc"][:], sm["gc"][:], -1.0, ALU.mult)
            k.act(sm["egc"][:], sm["gc"][:], AF.Exp)
            k.act(sm["eglast"][:], sm["glast"][:], AF.Exp)
            k.tt("dve", sm["tmp8"][:], sm["glast"][:], sm["gc"][:], ALU.subtract)
            k.act(sm["ekd"][:], sm["tmp8"][:], AF.Exp)
            k.tt("dve", sm["sckb"][:], sm["beta"][:], sm["egc"][:], ALU.mult)
            yield
            for g in range(2):
                k.act(sq2[:], sil[g][:], AF.Square)
                ps = m.pw()
                k.matmul(ps[:, 0:W], m.onesb[:], sq2[:])
                k.act(ri[:], ps[:, 0:W], AF.Ln, bias=EPS)
                if g == 0:
                    k.act(ri[:], ri[:], AF.Exp, scale=-0.5, bias=qscale)
                    k.tt("dve", qT[:], sil[0][:], ri[:], ALU.mult)
                else:
                    k.act(ri[:], ri[:], AF.Exp, scale=-0.5)
                    k.tt("dve", kTf[:], sil[1][:], ri[:], ALU.mult)
                    k.copy("act", kT[:], kTf[:])
            pt = m.pw()
            for h in range(NH):
                k.transpose(pt[:, HS[h]], kTf[:, HS[h]], m.ident[:])
            k.copy("dve", ktm[:], pt[:, 0:W])
            pt = m.pw()
            for h in range(NH):
                k.transpose(pt[:, HS[h]], sil[2][:, HS[h]], m.ident[:])
            k.copy("dve", vtm[:], pt[:, 0:W])
            yield
            k.tt_bc(dg[:], ident4[:], sm["gc"][:, 0:NH], NH, ALU.mult)
            pR = m.pw()
            k.matmul(pR[:, 0:W], m.ones[:], dg[:])
            for h in range(NH):
                k.act(tmpF[:, HS[h]], pR[:, HS[h]], AF.Abs, bias=sm["ngc"][:, h:h + 1])
            k.act(Fm[:], tmpF[:], AF.Exp, scale=-1.0)
            k.tt("dve", FU[:], Fm[:], umask4[:], ALU.mult)
            pKK = m.pw()
            mm4(pKK, lambda h: kT[:, HS[h]], lambda h: kT[:, HS[h]])
            for h in range(NH):
                k.stt(Am[:, HS[h]], pKK[:, HS[h]], sm["beta"][:, h:h + 1], Fm[:, HS[h]], ALU.mult, ALU.mult)
            pQK = m.pw()
            mm4(pQK, lambda h: kT[:, HS[h]], lambda h: qT[:, HS[h]])
            k.tt("dve", attnT[:], pQK[:, 0:W], FU[:], ALU.mult)
            for i in range(7):
                k.tt("dve", L[i][:], Am[:], lvl4[i][:], ALU.mult)
            yield
            pY = m.pw()
            mm4(pY, lambda h: L[0][:, HS[h]], lambda h: m.identb[:])
            Pc, Pn, Xc, Xn = Pa, Pb, Xa, Xb
            k.stt(Pc[:], pY[:, 0:W], -1.0, ident4[:], ALU.mult, ALU.add)
            k.tt("dve", Xc[:], ident4[:], L[0][:], ALU.subtract)
            for i in range(1, 7):
                pY = m.pw()
                mm4(pY, lambda h: L[i][:, HS[h]], lambda h: Pc[:, HS[h]])
                k.copy("act", Y[:], pY[:, 0:W])
                pZ = m.pw()
                mm4(pZ, lambda h: Xc[:, HS[h]], lambda h: Y[:, HS[h]])
                if i < 6:
                    pZT = m.pw()
                    mm4(pZT, lambda h: Y[:, HS[h]], lambda h: Xc[:, HS[h]])
                k.stt(Pn[:], pZ[:, 0:W], -1.0, Pc[:], ALU.mult, ALU.add)
                Pc, Pn = Pn, Pc
                if i < 6:
                    k.stt(Xn[:], pZT[:, 0:W], -1.0, Xc[:], ALU.mult, ALU.add)
                    Xc, Xn = Xn, Xc
                yield
            yield
            k.tt_bc(vb[:], vtm[:], sm["beta"][:, 0:NH], NH, ALU.mult)
            k.tt_bc(kb[:], ktm[:], sm["sckb"][:, 0:NH], NH, ALU.mult)
            k.tt_bc(kd[:], ktm[:], sm["ekd"][:, 0:NH], NH, ALU.mult)
            pu = m.pw()
            mm4(pu, lambda h: Pc[:, HS[h]], lambda h: vb[:, HS[h]])
            k.copy("dve", u[:], pu[:, 0:W])
            pw_ = m.pw()
            mm4(pw_, lambda h: kb[:, HS[h]], lambda h: Pc[:, HS[h]])
            k.copy("dve", wT[:], pw_[:, 0:W])
            yield
            pws = m.pw()
            mm4(pws, lambda h: wT[:, HS[h]], lambda h: Sb[:, HS[h]])
            k.stt(vnew[:], pws[:, 0:W], -1.0, u[:], ALU.mult, ALU.add)
            po1 = m.pw()
            mm4(po1, lambda h: qT[:, HS[h]], lambda h: Sb[:, HS[h]])
            po2 = m.pw()
            mm4(po2, lambda h: attnT[:, HS[h]], lambda h: vnew[:, HS[h]])
            k.tt_bc(t1[:], po1[:, 0:W], sm["egc"][:, 0:NH], NH, ALU.mult)
            k.tt("dve", o_[:], po2[:, 0:W], t1[:], ALU.add)
            pS = m.pw()
            mm4(pS, lambda h: kd[:, HS[h]], lambda h: vnew[:, HS[h]])
            k.tt_bc(St[:], Sst[:], sm["eglast"][:, 0:NH], NH, ALU.mult)
            k.tt("dve", Sst[:], pS[:, 0:W], St[:], ALU.add)
            k.copy("act", Sb[:], Sst[:])
            yield
            for h in range(NH):
                k.act(junk[:, 0:128], o_[:, HS[h]], AF.Square, accum_out=sm["ss4"][:, h:h + 1])
            k.ts("dve", sm["ss4"][:], sm["ss4"][:], 1.0 / 128, ALU.mult, EPS, ALU.add)
            k.act(sm["ss4"][:], sm["ss4"][:], AF.Ln)
            k.act(sm["ss4"][:], sm["ss4"][:], AF.Exp, scale=-0.5)
            k.tt_bc(on[:], o_[:], sm["ss4"][:, 0:NH], NH, ALU.mult)
            pT = m.pw()
            for h in range(NH):
                k.transpose(pT[:, HS[h]], on[:, HS[h]], m.ident[:])
            og = ogt[(ci * NGRP + G) % 2]
            k.stt(og[:], pT[:, 0:W], cols[:, 8:9], zs[:], ALU.mult, ALU.mult)
            og3 = k.track(og.t.rearrange("p (a b) -> p a b", a=NH), "og3")
            og3.reads, og3.writes = og.reads, og.writes
            k.dma("sp", oview[:, G * NH:(G + 1) * NH, ci * 128:(ci + 1) * 128], og3[:])

        for G0 in range(0, NGRP, NSLOT):
            gens = [gbody(G0 + si, slots[si]) for si in range(NSLOT)]
            alive = list(gens)
            step = 0
            while alive:
                for gen_ in list(alive):
                    try:
                        next(gen_)
                    except StopIteration:
                        alive.remove(gen_)
                step += 1
                if G0 + NSLOT >= NGRP and step == 5 and ci + 1 < NCH:
                    prologue(ci + 1)
    return oview


GDN_HPG = 4


def build_fused(S):
    nc, k, m = new_prog()
    NPOS = S // 1024
    TOKO = NPOS * 512
    I = lambda n, shp, dt=F32: din(k, m, n, shp, dt)
    x = I("x", [S, D])
    Tg = dict(x=x, an=I("an0", [8, 128]), w_qkvz=I("w_qkvz", [D, 4096]), w_bg=I("w_bg", [D, 16]),
              conv=I("conv", [4, 3072]), alog=I("alog", [1, 8]), dtb=I("dtb", [1, 8]), ogain=I("ogain", [1, 128]))
    ogT = k.dram("ogT_scr", [8, 128, S], BF16)
    Tg["ogT"] = ogT
    h2 = k.dram("h2_scr", [S, D], F32)
    Tf0 = dict(x=x, aT=ogT, w_o=I("gdn_w_out", [D, D]), fnorm=I("fn0", [8, 128]), w_gu=I("w_gu0", [D, 2 * DFF]),
               w_d=I("w_d0", [DFF, D]), h=h2)
    qT = k.dram("qT_scr", [8, 128, S], BF16)
    kT = k.dram("kT_scr", [8, 128, S], BF16)
    v = k.dram("v_scr", [S, D], BF16)
    Tk = dict(h=h2, kvn=I("kvn", [8, 128]), an=I("an1", [8, 128]), kg=I("kg", [1, 128]), qg=I("qg", [1, 128]),
              w_kv=I("w_kv", [D, 2 * D]), w_q=I("w_q", [D, D]), qT=qT, kT=kT, v=v)
    oTs = k.dram("oT_scr", [8, 128, TOKO], BF16)
    Ta = dict(qT=qT, kT=kT, v=v, masks=I("masks", [128, 8, 512], BF16), oT=oTs)
    d_sel = I("sel", [128, 2])
    out = dout(k, m, "out", [TOKO, D])
    Tf1 = dict(x=None, aT=oTs, w_o=I("sb_w_out", [D, D]), fnorm=I("fn1", [8, 128]), w_gu=I("w_gu1", [D, 2 * DFF]),
               w_d=I("w_d1", [DFF, D]), h=out)
    sel = k.sbuf([128, 2], F32, "sel_sb")
    k.dma("sp", sel[:], d_sel[:])

    emit_gdn(k, m, S, Tg, GDN_HPG, 8 // GDN_HPG)
    barrier(k)
    emit_dense_ffn(k, m, S, Tf0)
    barrier(k)
    emit_kvq(k, m, S, Tk)
    barrier(k)
    emit_attn(k, m, S, 8, Ta, parity=True, sel=sel)
    barrier(k)

    def xload(ti, x_t, xtmp):
        j, r = ti // 4, ti % 4
        ra = (2 * j) * 512 + r * 128
        rb = (2 * j + 1) * 512 + r * 128
        k.dma("sp", xtmp[0][:], h2[ra:ra + 128, :])
        k.dma("sp", xtmp[1][:], h2[rb:rb + 128, :])
        k.ts("dve", xtmp[0][:], xtmp[0][:], sel[:, 0:1], ALU.mult)
        k.stt(x_t[:], xtmp[1][:], sel[:, 1:2], xtmp[0][:], ALU.mult, ALU.add)
    emit_dense_ffn(k, m, TOKO, Tf1, xload=xload)
    k.final_wait("sp", [out])
    k.emit()
    return nc


def host_parity_masks(p):
    c = host_masks()
    z = np.zeros_like(c)
    f = np.full_like(c, NEG)
    return _c(np.concatenate([z, c], axis=1) if p == 1 else np.concatenate([c, f], axis=1))


def kernel_fused(inp, x):
    B, S, _ = x.shape
    cores = list(range(2 * B))
    cf, cb = host_consts()
    w_in = inp["gdn_w_in"][0]
    cw = inp["gdn_conv_w"][0]

    def grp(a, width, hpg=GDN_HPG):
        nblk = a.shape[1] // (8 * width)
        parts = []
        for G in range(8 // hpg):
            for blk in range(nblk):
                parts.append(a[:, blk * 8 * width + G * hpg * width: blk * 8 * width + (G + 1) * hpg * width])
        return _c(np.concatenate(parts, axis=1))
    common = dict(cf=cf, cb=cb, an0=_c(inp["attn_norm"][0].reshape(8, 128)), w_qkvz=grp(w_in[:, 0:4096], 128),
                  w_bg=grp(w_in[:, 4096:4112], 1), conv=grp(cw, 128), alog=_c(inp["gdn_a_log"].reshape(1, 8)),
                  dtb=_c(inp["gdn_dt_bias"].reshape(1, 8)), ogain=_c(inp["gdn_o_gain"].reshape(1, 128)),
                  gdn_w_out=_c(inp["gdn_w_out"][0]), fn0=_c(inp["ffn_norm"][0].reshape(8, 128)),
                  w_gu0=_c(inp["ffn_w_gu"][0]), w_d0=_c(inp["ffn_w_down"][0]),
                  kvn=_c(inp["kv_norm"].reshape(8, 128)), an1=_c(inp["attn_norm"][1].reshape(8, 128)),
                  kg=_c(inp["k_gain"].reshape(1, 128)), qg=_c(inp["sb_q_gain"].reshape(1, 128)),
                  w_kv=_c(inp["w_kv"]), w_q=_c(inp["sb_w_q"][0]), sb_w_out=_c(inp["sb_w_out"][0]),
                  fn1=_c(inp["ffn_norm"][1].reshape(8, 128)), w_gu1=_c(inp["ffn_w_gu"][1]), w_d1=_c(inp["ffn_w_down"][1]))
    feeds = []
    for c in cores:
        b, p = c // 2, c % 2
        selv = np.empty((128, 2), np.float32)
        selv[:, 0] = 1.0 - p
        selv[:, 1] = float(p)
        feeds.append(dict(common, x=_c(x[b]), masks=host_parity_masks(p), sel=selv))
    res = run_bass_kernel_spmd(_prog("fused", build_fused, S), feeds, core_ids=cores).results
    out = np.empty((B, S, D), np.float32)
    for c in cores:
        b, p = c // 2, c % 2
        o = res[c]["out"]
        for j in range(S // 1024):
            out[b, (2 * j + p) * 512:(2 * j + p + 1) * 512] = o[j * 512:(j + 1) * 512]
    return out

from concourse.bass_utils import run_bass_kernel_spmd

_PROGS = {}


def _prog(name, fn, *args):
    key = (name,) + args
    if key not in _PROGS:
        _PROGS[key] = fn(*args)
    return _PROGS[key]


def _c(a):
    return np.ascontiguousarray(a)


def kernel_unfused(**inputs):
    inp = {n: np.asarray(v) for n, v in inputs.items()}
    x = inp["x"].astype(np.float32, copy=False)
    B, S, _ = x.shape
    NC = 8
    HALF = S // 2
    cores = list(range(NC))
    cf, cb = host_consts()
    cst = {"cf": cf, "cb": cb}

    w_in = inp["gdn_w_in"][0]
    cw = inp["gdn_conv_w"][0]
    feeds = []
    for c in cores:
        b, hg = c // 2, c % 2
        hs = slice(hg * 512, (hg + 1) * 512)
        h4 = slice(hg * 4, (hg + 1) * 4)
        feeds.append(dict(cst, x=_c(x[b]), an=_c(inp["attn_norm"][0].reshape(8, 128)),
                          w_qkvz=_c(np.concatenate([w_in[:, 0:1024][:, hs], w_in[:, 1024:2048][:, hs],
                                                    w_in[:, 2048:3072][:, hs], w_in[:, 3072:4096][:, hs]], axis=1)),
                          w_bg=_c(np.concatenate([w_in[:, 4096:4104][:, h4], w_in[:, 4104:4112][:, h4]], axis=1)),
                          conv=_c(np.concatenate([cw[:, 0:1024][:, hs], cw[:, 1024:2048][:, hs], cw[:, 2048:3072][:, hs]], axis=1)),
                          alog=_c(inp["gdn_a_log"][:, h4]), dtb=_c(inp["gdn_dt_bias"][:, h4]),
                          ogain=_c(inp["gdn_o_gain"].reshape(1, 128))))
    r1 = run_bass_kernel_spmd(_prog("gdn", build_gdn, S, 4), feeds, core_ids=cores).results
    ogT = [np.concatenate([r1[2 * b]["ogT"], r1[2 * b + 1]["ogT"]], axis=0) for b in range(B)]

    nc_ffn = _prog("ffn", build_dense_ffn, HALF)
    feeds = []
    for c in cores:
        b, hf = c // 2, c % 2
        ts = slice(hf * HALF, (hf + 1) * HALF)
        feeds.append(dict(cst, x=_c(x[b, ts]), aT=_c(ogT[b][:, :, ts]), w_o=_c(inp["gdn_w_out"][0]),
                          fnorm=_c(inp["ffn_norm"][0].reshape(8, 128)), w_gu=_c(inp["ffn_w_gu"][0]),
                          w_d=_c(inp["ffn_w_down"][0])))
    r2 = run_bass_kernel_spmd(nc_ffn, feeds, core_ids=cores).results
    h2 = [r["h"] for r in r2]

    feeds = []
    for c in cores:
        feeds.append(dict(cst, h=_c(h2[c]), kvn=_c(inp["kv_norm"].reshape(8, 128)),
                          an=_c(inp["attn_norm"][1].reshape(8, 128)), kg=_c(inp["k_gain"].reshape(1, 128)),
                          qg=_c(inp["sb_q_gain"].reshape(1, 128)), w_kv=_c(inp["w_kv"]), w_q=_c(inp["sb_w_q"][0])))
    r3 = run_bass_kernel_spmd(_prog("kvq", build_kvq, HALF), feeds, core_ids=cores).results

    mk = host_masks()
    feeds = []
    for c in cores:
        b, hg = c // 2, c % 2
        h4 = slice(hg * 4, (hg + 1) * 4)
        qT = np.concatenate([r3[2 * b]["qT"][h4], r3[2 * b + 1]["qT"][h4]], axis=2)
        kT = np.concatenate([r3[2 * b]["kT"][h4], r3[2 * b + 1]["kT"][h4]], axis=2)
        v = np.concatenate([r3[2 * b]["v"], r3[2 * b + 1]["v"]], axis=0)[:, hg * 512:(hg + 1) * 512]
        feeds.append(dict(cst, qT=_c(qT), kT=_c(kT), v=_c(v), masks=mk))
    r4 = run_bass_kernel_spmd(_prog("attn", build_attn, S, 4), feeds, core_ids=cores).results
    oT = [np.concatenate([r4[2 * b]["oT"], r4[2 * b + 1]["oT"]], axis=0) for b in range(B)]

    feeds = []
    for c in cores:
        b, hf = c // 2, c % 2
        ts = slice(hf * HALF, (hf + 1) * HALF)
        feeds.append(dict(cst, x=_c(h2[c]), aT=_c(oT[b][:, :, ts]), w_o=_c(inp["sb_w_out"][0]),
                          fnorm=_c(inp["ffn_norm"][1].reshape(8, 128)), w_gu=_c(inp["ffn_w_gu"][1]),
                          w_d=_c(inp["ffn_w_down"][1])))
    r5 = run_bass_kernel_spmd(_prog("ffn_b", build_dense_ffn, HALF), feeds, core_ids=cores).results
    out = np.empty((B, S, D), np.float32)
    for c in cores:
        b, hf = c // 2, c % 2
        out[b, hf * HALF:(hf + 1) * HALF] = r5[c]["h"]
    return out


def kernel(**inputs):
    inp = {n: np.asarray(v) for n, v in inputs.items()}
    x = inp["x"].astype(np.float32, copy=False)
    return kernel_fused(inp, x)
```

```python
from contextlib import ExitStack
import numpy as np
import concourse.bass as bass
import concourse.mybir as mybir

F32 = mybir.dt.float32
BF16 = mybir.dt.bfloat16
ALU = mybir.AluOpType
AF = mybir.ActivationFunctionType
AX = mybir.AxisListType

SAME_ENGINE_SYNC = True
N_DMA_SEMS = 6


class Buf:
    def __init__(self, t, name):
        self.t = t
        self.name = name
        self.reads = {}
        self.writes = {}

    def __getitem__(self, idx):
        return View(self, self.t[idx])


class View:
    def __init__(self, buf, ap):
        self.buf = buf
        self.ap = ap


def _ap(x):
    return x.ap if isinstance(x, View) else x


class Eng:
    def __init__(self, name):
        self.name = name
        self.ops = []
        self.trace = []
        self.count = 0
        self.waited = {}
        self.sem = None
        self.dma_sems = []
        self.dma_vals = []
        self.dma_k = 0


class K:
    def __init__(self, nc):
        self.nc = nc
        self.es = ExitStack()
        self.engs = {n: Eng(n) for n in ("pe", "act", "dve", "pool", "sp")}
        self.sems = {}
        for n, e in self.engs.items():
            e.sem = self.es.enter_context(nc.semaphore("s_" + n))
            self.sems[n] = e.sem
        for n in ("sp", "pool", "act"):
            e = self.engs[n]
            for i in range(N_DMA_SEMS):
                s = self.es.enter_context(nc.semaphore("d_%s%d" % (n, i)))
                key = "d_%s%d" % (n, i)
                self.sems[key] = s
                e.dma_sems.append(key)
                e.dma_vals.append(0)
        self.nbuf = 0

    def sbuf(self, shape, dtype, name=None):
        self.nbuf += 1
        name = name or "sb%d" % self.nbuf
        t = self.es.enter_context(self.nc.sbuf_tensor(name, list(shape), dtype))
        return Buf(t, name)

    def psum(self, shape, dtype, name=None):
        self.nbuf += 1
        name = name or "ps%d" % self.nbuf
        t = self.es.enter_context(self.nc.psum_tensor(name, list(shape), dtype))
        return Buf(t, name)

    def dram(self, name, shape, dtype, kind="Internal"):
        t = self.nc.dram_tensor(name, list(shape), dtype, kind=kind)
        return Buf(t.ap(), name)

    def track(self, ap, name):
        return Buf(ap, name)

    def _wait(self, e, key, val, war=False):
        if key == e.name and (e.name == "pe" or war or not SAME_ENGINE_SYNC):
            return
        if e.waited.get(key, 0) >= val:
            return
        e.waited[key] = val
        sem = self.sems[key]
        e.ops.append(("wait", key, val))
        e.trace.append(("w", key, val))

    def _deps(self, e, reads, writes, nowaw=False):
        for v in reads:
            if isinstance(v, View):
                for k, val in v.buf.writes.items():
                    self._wait(e, k, val)
        for v in writes:
            if isinstance(v, View):
                for k, val in v.buf.reads.items():
                    self._wait(e, k, val, war=True)
                if not nowaw:
                    for k, val in v.buf.writes.items():
                        self._wait(e, k, val)

    def _mark(self, key, val, reads, writes):
        for v in reads:
            if isinstance(v, View):
                if v.buf.reads.get(key, 0) < val:
                    v.buf.reads[key] = val
        for v in writes:
            if isinstance(v, View):
                if v.buf.writes.get(key, 0) < val:
                    v.buf.writes[key] = val

    def op(self, eng, fn, reads, writes, nowaw=False):
        e = self.engs[eng]
        self._deps(e, reads, writes, nowaw)
        e.count += 1
        sem = e.sem
        e.ops.append(("op", fn, e.count))
        e.trace.append(("i", e.name, 1))
        self._mark(e.name, e.count, reads, writes)

    def dma(self, eng, out, in_, nowaw=True, **kw):
        e = self.engs[eng]
        self._deps(e, [in_], [out], nowaw)
        i = e.dma_k % N_DMA_SEMS
        e.dma_k += 1
        key = e.dma_sems[i]
        if e.dma_vals[i] > 0:
            self._wait(e, key, e.dma_vals[i])
        e.dma_vals[i] += 16
        val = e.dma_vals[i]
        sem = self.sems[key]
        o, s = _ap(out), _ap(in_)
        e.ops.append(("dma", o, s, key, kw))
        e.trace.append(("i", key, 16))
        self._mark(key, val, [in_], [out])
        return (key, val)

    def allgather(self, out, in_, groups):
        e = self.engs["pool"]
        self._deps(e, [in_], [out], False)
        i = e.dma_k % N_DMA_SEMS
        e.dma_k += 1
        key = e.dma_sems[i]
        if e.dma_vals[i] > 0:
            self._wait(e, key, e.dma_vals[i])
        e.dma_vals[i] += 16
        val = e.dma_vals[i]
        o, s_ = _ap(out), _ap(in_)
        e.ops.append(("cc", o, s_, key, groups))
        e.trace.append(("i", key, 16))
        self._mark(key, val, [in_], [out])

    def final_wait(self, eng, bufs):
        e = self.engs[eng]
        for b in bufs:
            for k, val in b.writes.items():
                self._wait(e, k, val)

    def matmul(self, out, lhsT, rhs, start=True, stop=True, **kw):
        o, l, r = _ap(out), _ap(lhsT), _ap(rhs)
        self.op("pe", lambda h: h.matmul(o, l, r, start=start, stop=stop, **kw), [lhsT, rhs], [out])

    def transpose(self, out, in_, ident):
        o, i, d = _ap(out), _ap(in_), _ap(ident)
        self.op("pe", lambda h: h.transpose(o, i, d), [in_, ident], [out])

    def act(self, out, in_, func, bias=None, scale=None, accum_out=None, eng="act"):
        o, i = _ap(out), _ap(in_)
        kw = {}
        rd = [in_]
        wr = [out]
        if bias is not None:
            kw["bias"] = _ap(bias)
            rd.append(bias)
        if scale is not None:
            kw["scale"] = _ap(scale)
            rd.append(scale)
        if accum_out is not None:
            kw["accum_out"] = _ap(accum_out)
            wr.append(accum_out)
        self.op("act", lambda h: h.activation(o, i, func, **kw), rd, wr)

    def tt(self, eng, out, in0, in1, op):
        o, a, b = _ap(out), _ap(in0), _ap(in1)
        self.op(eng, lambda h: h.tensor_tensor(o, a, b, op), [in0, in1], [out])

    def ts(self, eng, out, in0, s1, op0, s2=None, op1=None, accum_out=None):
        o, a = _ap(out), _ap(in0)
        rd = [in0]
        wr = [out]
        if isinstance(s1, View):
            rd.append(s1)
        if isinstance(s2, View):
            rd.append(s2)
        kw = {}
        if op1 is not None:
            kw["op1"] = op1
        if accum_out is not None:
            kw["accum_out"] = _ap(accum_out)
            wr.append(accum_out)
        x1, x2 = _ap(s1), _ap(s2)
        self.op(eng, lambda h: h.tensor_scalar(o, a, x1, x2, op0, **kw), rd, wr)

    def tt_bc(self, out, in0, sc, ng, op):
        o3 = _ap(out).rearrange("p (s d) -> p s d", s=ng)
        a3 = _ap(in0).rearrange("p (s d) -> p s d", s=ng)
        b3 = _ap(sc).unsqueeze(2).broadcast_to([128, ng, 128])
        self.op("dve", lambda h: h.tensor_tensor(o3, a3, b3, op), [in0, sc], [out])

    def stt(self, out, in0, scalar, in1, op0, op1, eng="dve"):
        o, a, b = _ap(out), _ap(in0), _ap(in1)
        rd = [in0, in1]
        if isinstance(scalar, View):
            rd.append(scalar)
        s = _ap(scalar)
        self.op(eng, lambda h: h.scalar_tensor_tensor(o, a, s, b, op0, op1), rd, [out])

    def copy(self, eng, out, in_):
        o, i = _ap(out), _ap(in_)
        if eng == "act":
            self.op("act", lambda h: h.copy(o, i), [in_], [out])
        else:
            self.op(eng, lambda h: h.tensor_copy(o, i), [in_], [out])

    def memset(self, eng, out, val):
        o = _ap(out)
        self.op(eng, lambda h: h.memset(o, val), [], [out])

    def recip(self, out, in_):
        o, i = _ap(out), _ap(in_)
        self.op("dve", lambda h: h.reciprocal(o, i), [in_], [out])

    def affine_select(self, out, in_, pattern, cmp, fill, base, cm):
        o, i = _ap(out), _ap(in_)
        self.op("pool", lambda h: h.affine_select(o, i, pattern, cmp, fill, base=base, channel_multiplier=cm),
                [in_], [out])

    def check_deadlock(self):
        vals = {}
        pos = {n: 0 for n in self.engs}
        progress = True
        while progress:
            progress = False
            for n, e in self.engs.items():
                while pos[n] < len(e.trace):
                    kind, key, v = e.trace[pos[n]]
                    if kind == "w":
                        if vals.get(key, 0) >= v:
                            pos[n] += 1
                            progress = True
                        else:
                            break
                    else:
                        vals[key] = vals.get(key, 0) + v
                        pos[n] += 1
                        progress = True
        stuck = {n: (pos[n], len(e.trace), e.trace[pos[n]] if pos[n] < len(e.trace) else None)
                 for n, e in self.engs.items()}
        ok = all(pos[n] == len(e.trace) for n, e in self.engs.items())
        return ok, stuck, vals

    def emit(self):
        ok, stuck, vals = self.check_deadlock()
        if not ok:
            raise RuntimeError("DEADLOCK in sync graph: %s" % (stuck,))
        nc = self.nc
        import bisect
        sig = {n: set() for n in self.engs}
        for n, e in self.engs.items():
            for it in e.ops:
                if it[0] == "wait" and it[1] in sig:
                    sig[it[1]].add(it[2])
        sigl = {n: sorted(v) for n, v in sig.items()}
        sems = self.sems

        def run(n, h):
            e = self.engs[n]
            for it in e.ops:
                if it[0] == "wait":
                    key, val = it[1], it[2]
                    if key in sigl:
                        val = bisect.bisect_right(sigl[key], val)
                    h.wait_ge(sems[key], val)
                elif it[0] == "op":
                    ins = it[1](h)
                    if it[2] in sig[n]:
                        ins.then_inc(e.sem, 1)
                elif it[0] == "cc":
                    _, o, s_, key, groups = it
                    h.collective_compute("AllGather", ALU.bypass, replica_groups=groups, ins=[s_], outs=[o]).then_inc(sems[key], 16)
                else:
                    _, o, s_, key, kw = it
                    h.dma_start(out=o, in_=s_, **kw).then_inc(sems[key], 16)
        self.n_inc = {n: len(v) for n, v in sig.items()}
        with nc.Block() as block:
            @block.tensor
            def _(h):
                run("pe", h)

            @block.scalar
            def _(h):
                run("act", h)

            @block.vector
            def _(h):
                run("dve", h)

            @block.gpsimd
            def _(h):
                run("pool", h)

            @block.sync
            def _(h):
                run("sp", h)
        self.es.close()


D = 1024
H = 8
DH = 128
DFF = 2816
PROJ = 4112
EPS = 1e-6
NEG = -30000.0


class M:
    pass


def host_consts():
    r = np.arange(128)[:, None]
    c = np.arange(128)[None, :]
    mats = [r == c, np.ones((128, 128)), r <= c, r > c, c >= r]
    b = 1
    while b < 128:
        mats.append(((r // b) == (c // b) + 1) & ((r // b) % 2 == 1))
        b *= 2
    cf = np.stack([np.asarray(x, np.float32) for x in mats], axis=1)
    import ml_dtypes
    bmats = [r == c, np.ones((128, 128)), -(r >= c).astype(np.float32)]
    cb = np.stack([np.asarray(x, np.float32) for x in bmats], axis=1).astype(ml_dtypes.bfloat16)
    return np.ascontiguousarray(cf), np.ascontiguousarray(cb)


def setup_consts(k, m):
    cfd = k.dram("cf", [128, 12, 128], F32, kind="ExternalInput")
    cbd = k.dram("cb", [128, 3, 128], BF16, kind="ExternalInput")
    cf = k.sbuf([128, 12, 128], F32, "cfs")
    cb = k.sbuf([128, 3, 128], BF16, "cbs")
    k.dma("sp", cf[:], cfd[:])
    k.dma("sp", cb[:], cbd[:])
    m.cf, m.cb = cf, cb

    class V:
        def __init__(self, buf, j):
            self.buf, self.j = buf, j

        def __getitem__(self, idx):
            if idx == slice(None):
                return self.buf[:, self.j, :]
            a, b = idx
            return self.buf[a, self.j, b]
    m.ident = V(cf, 0)
    m.ones = V(cf, 1)
    m.triI = V(cf, 2)
    m.lmask = V(cf, 3)
    m.umask = V(cf, 4)
    m.lvl = [V(cf, 5 + i) for i in range(7)]
    m.identb = V(cb, 0)
    m.onesb = V(cb, 1)
    m.ntriS = V(cb, 2)


class Arena:
    def __init__(self, k, nbytes):
        self.k = k
        self.n32 = nbytes // 4
        self.base = k.es.enter_context(k.nc.sbuf_tensor("arena", [128, self.n32], F32))
        self.off = 0
        self.cnt = 0

    def reset(self):
        self.off = 0

    def alloc(self, shape, dtype, name=None):
        nel = int(np.prod(shape[1:]))
        nb = nel * (2 if dtype == BF16 else 4)
        n32 = (nb + 3) // 4
        n32 = (n32 + 7) // 8 * 8
        assert self.off + n32 <= self.n32, "arena overflow %d + %d > %d" % (self.off, n32, self.n32)
        ap = self.base[0:shape[0], self.off:self.off + n32]
        self.off += n32
        if dtype != F32:
            ap = ap.bitcast(dtype)
        ap = ap[:, 0:nel]
        if len(shape) == 3:
            ap = ap.rearrange("p (a b) -> p a b", a=shape[1])
        elif len(shape) == 4:
            ap = ap.rearrange("p (a b c) -> p a b c", a=shape[1], b=shape[2])
        self.cnt += 1
        return Buf(ap, name or "ar%d" % self.cnt)


def barrier(k):
    latest = {}
    for n, e in k.engs.items():
        if e.count:
            latest[n] = e.count
        for key, v in zip(e.dma_sems, e.dma_vals):
            if v:
                latest[key] = v
    for n, e in k.engs.items():
        for key, v in latest.items():
            if key == n:
                continue
            k._wait(e, key, v)


def new_prog():
    nc = bass.Bass("TRN2", target_bir_lowering=False)
    k = K(nc)
    m = M()
    m.ins = {}
    m.outs = {}
    setup_consts(k, m)
    m.pb = [k.psum([128, 512], F32, "pb%d" % i) for i in range(8)]
    m.pwi = 0

    def pw():
        b = m.pb[m.pwi % 8]
        m.pwi += 1
        return b
    m.pw = pw
    m.arena = Arena(k, 196 * 1024)
    return nc, k, m


def din(k, m, name, shape, dt=F32):
    b = k.dram(name, shape, dt, kind="ExternalInput")
    m.ins[name] = b
    return b


def dout(k, m, name, shape, dt=F32):
    b = k.dram(name, shape, dt, kind="ExternalOutput")
    m.outs[name] = b
    return b


def load_cols(k, m, rows):
    al = m.arena.alloc
    st = al([128, 128], F32)
    out = al([128, 128], F32)
    k.memset("dve", st[:], 0.0)
    r0 = 0
    for v, n in rows:
        k.dma("sp", st[r0:r0 + n, :], v, nowaw=False)
        r0 += n
    p = m.pw()
    k.transpose(p[:, 0:128], st[:], m.ident[:])
    k.copy("dve", out[:], p[:, 0:128])
    return out


def load_w(k, dst, src, kchunks):
    for kc in range(kchunks):
        k.dma("pool", dst[:, kc, :], src(kc), max_dma_last_dim=4096)


def rstd_of(k, junk, ss, src, n):
    k.act(junk[:, 0:n], src, AF.Square, accum_out=ss[:])
    k.ts("dve", ss[:], ss[:], 1.0 / n, ALU.mult, EPS, ALU.add)
    k.act(ss[:], ss[:], AF.Ln)
    k.act(ss[:], ss[:], AF.Exp, scale=-0.5)


def norm_T(k, m, xs, gains, outs):
    for half in range(2):
        p = m.pw()
        for c4 in range(4):
            c = half * 4 + c4
            k.transpose(p[:, c4 * 128:(c4 + 1) * 128], xs[:, c * 128:(c + 1) * 128], m.ident[:])
        for g, o in zip(gains, outs):
            o3 = _ap(o[:, half * 4:half * 4 + 4, :])
            a3 = _ap(p[:]).rearrange("p (s d) -> p s d", s=4)
            b3 = _ap(g)[:, half * 4:half * 4 + 4].unsqueeze(2).broadcast_to([128, 4, 128])
            k.op("dve", lambda h, o3=o3, a3=a3, b3=b3: h.tensor_tensor(o3, a3, b3, ALU.mult), [p[:], g], [o[:]])


D = 1024
H = 8
DFF = 2816
EPS = 1e-6


def build_dense_ffn(TOK):
    nc, k, m = new_prog()
    T = dict(x=din(k, m, "x", [TOK, D]), aT=din(k, m, "aT", [8, 128, TOK], BF16), w_o=din(k, m, "w_o", [D, D]),
             fnorm=din(k, m, "fnorm", [8, 128]), w_gu=din(k, m, "w_gu", [D, 2 * DFF]), w_d=din(k, m, "w_d", [DFF, D]),
             h=dout(k, m, "h", [TOK, D]))
    emit_dense_ffn(k, m, TOK, T)
    k.final_wait("sp", [T["h"]])
    k.emit()
    return nc


def emit_dense_ffn(k, m, TOK, T, xload=None):
    m.arena.reset()
    al = m.arena.alloc
    d_x, d_aT, d_wo, d_fn, d_wgu, d_wd, d_h = T["x"], T["aT"], T["w_o"], T["fnorm"], T["w_gu"], T["w_d"], T["h"]
    cols = load_cols(k, m, [(d_fn[:], 8)])
    w_o = al([128, 8, D], BF16, "w_o")
    w_gu = al([128, 8, 2 * DFF], BF16, "w_gu")
    w_d = al([128, 22, D], BF16, "w_d")
    load_w(k, w_o, lambda kc: d_wo[kc * 128:(kc + 1) * 128, :], 8)
    load_w(k, w_gu, lambda kc: d_wgu[kc * 128:(kc + 1) * 128, :], 8)
    load_w(k, w_d, lambda kc: d_wd[kc * 128:(kc + 1) * 128, :], 22)
    junk = al([128, 1024], BF16)
    ss = al([128, 1], F32)
    xt = [al([128, D], F32, "xt%d" % i) for i in range(1 if xload is not None else 2)]
    at = [al([128, 8, 128], BF16, "at%d" % i) for i in range(2)]
    h1 = [al([128, D], F32, "h1_%d" % i) for i in range(2)]
    h2 = [al([128, D], F32, "h2_%d" % i) for i in range(1 if xload is not None else 2)]
    xtmp = [al([128, D], F32, "xtmp%d" % i) for i in range(2)] if xload is not None else None
    xs = al([128, D], F32, "xs")
    hnT = al([128, 8, 128], BF16, "hnT")
    hidF = al([128, 22 * 128], BF16, "hidT")
    hidT = Buf(hidF.t.rearrange("p (a b) -> p a b", a=22), "hidT3")
    hidT.reads, hidT.writes = hidF.reads, hidF.writes
    sg = [al([128, 256], F32, "sg%d" % i) for i in range(2)]
    aTv = k.track(d_aT.t.rearrange("h p t -> p h t"), "aTv")
    for ti in range(TOK // 128):
        x_t, a_t, h1_, h2_ = xt[ti % len(xt)], at[ti % 2], h1[ti % 2], h2[ti % len(h2)]
        if xload is None:
            k.dma("sp", x_t[:], d_x[ti * 128:(ti + 1) * 128, :])
        else:
            xload(ti, x_t, xtmp)
        k.dma("sp", a_t[:], aTv[:, :, ti * 128:(ti + 1) * 128])
        for half in range(2):
            py = m.pw()
            for h in range(8):
                k.matmul(py[:], a_t[:, h, :], w_o[:, h, half * 512:(half + 1) * 512], start=(h == 0), stop=(h == 7))
            k.tt("dve", h1_[:, half * 512:(half + 1) * 512], py[:], x_t[:, half * 512:(half + 1) * 512], ALU.add)
        rstd_of(k, junk, ss, h1_[:], D)
        k.ts("dve", xs[:], h1_[:], ss[:], ALU.mult)
        norm_T(k, m, xs, [cols[:, 0:8]], [hnT])
        for j2 in range(11):
            p = m.pw()
            for q in range(4):
                j = j2 * 2 + (q % 2)
                col = (0 if q < 2 else DFF) + j * 128
                for kc in range(8):
                    k.matmul(p[:, q * 128:(q + 1) * 128], w_gu[:, kc, col:col + 128], hnT[:, kc, :],
                             start=(kc == 0), stop=(kc == 7))
            s_ = sg[j2 % 2]
            k.act(s_[:], p[:, 0:256], AF.Silu)
            k.tt("dve", hidF[:, j2 * 256:(j2 + 1) * 256], p[:, 256:512], s_[:], ALU.mult)
        for half in range(2):
            py = m.pw()
            for j in range(22):
                k.matmul(py[:], hidT[:, j, :], w_d[:, j, half * 512:(half + 1) * 512], start=(j == 0), stop=(j == 21))
            k.tt("dve", h2_[:, half * 512:(half + 1) * 512], py[:], h1_[:, half * 512:(half + 1) * 512], ALU.add)
        k.dma("sp", d_h[ti * 128:(ti + 1) * 128, :], h2_[:])


def build_kvq(TOK):
    nc, k, m = new_prog()
    T = dict(h=din(k, m, "h", [TOK, D]), kvn=din(k, m, "kvn", [8, 128]), an=din(k, m, "an", [8, 128]),
             kg=din(k, m, "kg", [1, 128]), qg=din(k, m, "qg", [1, 128]), w_kv=din(k, m, "w_kv", [D, 2 * D]),
             w_q=din(k, m, "w_q", [D, D]), qT=dout(k, m, "qT", [8, 128, TOK], BF16),
             kT=dout(k, m, "kT", [8, 128, TOK], BF16), v=dout(k, m, "v", [TOK, D], BF16))
    outs = emit_kvq(k, m, TOK, T)
    k.final_wait("sp", outs)
    k.emit()
    return nc


def emit_kvq(k, m, TOK, T):
    m.arena.reset()
    al = m.arena.alloc
    d_h, d_kvn, d_an, d_kg, d_qg, d_wkv, d_wq = T["h"], T["kvn"], T["an"], T["kg"], T["qg"], T["w_kv"], T["w_q"]
    d_qT, d_kT, d_v = T["qT"], T["kT"], T["v"]
    cols = load_cols(k, m, [(d_kvn[:], 8), (d_an[:], 8), (d_kg[:], 1), (d_qg[:], 1)])
    w_kv = al([128, 8, 2 * D], BF16, "w_kv")
    w_q = al([128, 8, D], BF16, "w_q")
    load_w(k, w_kv, lambda kc: d_wkv[kc * 128:(kc + 1) * 128, :], 8)
    load_w(k, w_q, lambda kc: d_wq[kc * 128:(kc + 1) * 128, :], 8)
    junk = al([128, 1024], BF16)
    ss = al([128, 1], F32)
    ht = [al([128, D], F32, "ht%d" % i) for i in range(2)]
    xs = al([128, D], F32, "xs")
    xkTs = [al([128, 8, 128], BF16, "xkT%d" % i) for i in range(2)]
    xqTs = [al([128, 8, 128], BF16, "xqT%d" % i) for i in range(2)]
    vt = [al([128, D], BF16, "vt%d" % i) for i in range(2)]
    ss8 = {n: al([128, 8], F32, "ss8" + n) for n in "kq"}
    nrm = {n: al([128, D], F32, "nrm" + n) for n in "kq"}
    oT = {n: [al([128, 8 * 128], BF16, "oT%s%d" % (n, i)) for i in range(2)] for n in "kq"}
    qTv = k.track(d_qT.t.rearrange("h p t -> p h t"), "qTv")
    kTv = k.track(d_kT.t.rearrange("h p t -> p h t"), "kTv")
    qscale = 128.0 ** -0.5
    NT = TOK // 128
    junk2 = al([128, 1024], BF16)

    def prologue(tj):
        h_t = ht[tj % 2]
        k.dma("sp", h_t[:], d_h[tj * 128:(tj + 1) * 128, :])
        rstd_of(k, junk2, ss, h_t[:], D)
        k.ts("dve", xs[:], h_t[:], ss[:], ALU.mult)
        norm_T(k, m, xs, [cols[:, 0:8], cols[:, 8:16]], [xkTs[tj % 2], xqTs[tj % 2]])

    for ti in range(NT):
        if ti == 0:
            prologue(0)
        xkT, xqT = xkTs[ti % 2], xqTs[ti % 2]
        v_t = vt[ti % 2]
        for n4 in (2, 3):
            py = m.pw()
            for kc in range(8):
                k.matmul(py[:], xkT[:, kc, :], w_kv[:, kc, n4 * 512:(n4 + 1) * 512], start=(kc == 0), stop=(kc == 7))
            k.copy("act", v_t[:, (n4 - 2) * 512:(n4 - 1) * 512], py[:])
        k.dma("sp", d_v[ti * 128:(ti + 1) * 128, :], v_t[:])
        if ti + 1 < NT:
            prologue(ti + 1)
        for which, xT, w, gcol, dview in (("k", xkT, w_kv, 16, kTv), ("q", xqT, w_q, 17, qTv)):
            pys = []
            for n4 in range(2):
                py = m.pw()
                for kc in range(8):
                    k.matmul(py[:], xT[:, kc, :], w[:, kc, n4 * 512:(n4 + 1) * 512], start=(kc == 0), stop=(kc == 7))
                pys.append(py)
                for h4 in range(4):
                    hh = n4 * 4 + h4
                    k.act(junk[:, 0:128], py[:, h4 * 128:(h4 + 1) * 128], AF.Square, accum_out=ss8[which][:, hh:hh + 1])
            s8 = ss8[which]
            k.ts("dve", s8[:], s8[:], 1.0 / 128, ALU.mult, EPS, ALU.add)
            k.act(s8[:], s8[:], AF.Ln)
            k.act(s8[:], s8[:], AF.Exp, scale=-0.5)
            nr = nrm[which]
            for n4 in range(2):
                k.tt_bc(nr[:, n4 * 512:(n4 + 1) * 512], pys[n4][:], s8[:, n4 * 4:n4 * 4 + 4], 4, ALU.mult)
            o_ = oT[which][ti % 2]
            for n4 in range(2):
                p = m.pw()
                for h4 in range(4):
                    hh = n4 * 4 + h4
                    k.transpose(p[:, h4 * 128:(h4 + 1) * 128], nr[:, hh * 128:(hh + 1) * 128], m.ident[:])
                if which == "k":
                    k.ts("dve", o_[:, n4 * 512:(n4 + 1) * 512], p[:], cols[:, gcol:gcol + 1], ALU.mult)
                else:
                    k.ts("dve", o_[:, n4 * 512:(n4 + 1) * 512], p[:], cols[:, gcol:gcol + 1], ALU.mult, qscale, ALU.mult)
            o3 = k.track(o_.t.rearrange("p (a b) -> p a b", a=8), "o3")
            o3.reads, o3.writes = o_.reads, o_.writes
            k.dma("sp", dview[:, :, ti * 128:(ti + 1) * 128], o3[:])
    return [d_v, qTv, kTv]


def host_masks():
    import ml_dtypes
    s = np.arange(128)[:, None, None]
    jj = np.arange(4)[None, :, None]
    t = np.arange(512)[None, None, :]
    mk = np.where(jj * 128 + s < t, 0.0, NEG).astype(np.float32)
    return np.ascontiguousarray(mk.astype(ml_dtypes.bfloat16))


NEG = -30000.0


def build_attn(S, NH):
    nc, k, m = new_prog()
    T = dict(qT=din(k, m, "qT", [NH, 128, S], BF16), kT=din(k, m, "kT", [NH, 128, S], BF16),
             v=din(k, m, "v", [S, NH * 128], BF16), masks=din(k, m, "masks", [128, 4, 512], BF16),
             oT=dout(k, m, "oT", [NH, 128, S], BF16))
    emit_attn(k, m, S, NH, T)
    k.final_wait("sp", [T["oT"]])
    k.emit()
    return nc


def emit_attn(k, m, S, NH, T, parity=False, sel=None):
    m.arena.reset()
    al = m.arena.alloc
    d_q, d_k, d_v, d_mk, d_o = T["qT"], T["kT"], T["v"], T["masks"], T["oT"]
    NB = S // 128
    NSB = S // 512
    NM = 8 if parity else 4
    masks = al([128, NM, 512], BF16, "masks_sb")
    k.dma("sp", masks[:], d_mk[:])
    qs = [al([128, S], BF16, "qs%d" % i) for i in range(2)]
    ks = [al([128, S], BF16, "ks%d" % i) for i in range(2)]
    vs = [al([128, NB, 128], BF16, "vs%d" % i) for i in range(2)]
    e_ = [al([128, 512], F32, "e%d" % i) for i in range(3)]
    sp_ = [al([128, 512], BF16, "sp%d" % i) for i in range(3)]
    att_ = [al([128, 512], BF16, "att%d" % i) for i in range(2)]
    fac_ = [al([128, 4], F32, "fac%d" % i) for i in range(2)]
    accs = [al([128, 512], F32, "acc%d" % i) for i in range(2)]
    tmp = al([128, 512], F32, "acctmp")
    ot_ = [al([128, 512], BF16, "ot%d" % i) for i in range(2)]
    qsel = [al([128, 512], BF16, "qsel%d" % i) for i in range(2)]
    qtmp = al([128, 512], BF16, "qtmp")
    vview = k.track(d_v.t.rearrange("(blk p) c -> p blk c", p=128), "vview")
    pds = [Buf(m.pb[7].t[:, j * 64:j * 64 + 64], "pd%d" % j) for j in range(8)]
    ring = m.pb[0:7]
    rs = {"i": 0}

    def pw7():
        b_ = ring[rs["i"] % 7]
        rs["i"] += 1
        return b_
    old_pw = m.pw
    m.pw = pw7
    tiles = []
    for h in range(NH):
        for i in range(NSB // 2 if parity else NSB):
            nkb = 4 * (2 * i + 2) if parity else 4 * (i + 1)
            for kb in range(nkb):
                tiles.append((h, i, kb, nkb))
    state = {"cur": None}

    def stage_a(t):
        h, i, kb, nkb = tiles[t]
        q_h, k_h, v_h = qs[h % 2], ks[h % 2], vs[h % 2]
        if i == 0 and kb == 0:
            k.dma("sp", q_h[:], d_q[h, :, :])
            k.dma("sp", k_h[:], d_k[h, :, :])
            step = 16 if NB >= 16 else NB
            for b0 in range(0, NB, step):
                k.dma("sp", v_h[:, b0:b0 + step, :], vview[:, b0:b0 + step, h * 128:(h + 1) * 128])
        if parity:
            qb = qsel[(h * (NSB // 2) + i) % 2]
            if kb == 0:
                k.ts("dve", qtmp[:], q_h[:, (2 * i) * 512:(2 * i + 1) * 512], sel[:, 0:1], ALU.mult)
                k.stt(qb[:], q_h[:, (2 * i + 1) * 512:(2 * i + 2) * 512], sel[:, 1:2], qtmp[:], ALU.mult, ALU.add)
            qv = qb[:]
        else:
            qv = q_h[:, i * 512:(i + 1) * 512]
        jj = kb - (nkb - NM)
        diag = jj >= 0
        kv_ = k_h[:, kb * 128:(kb + 1) * 128]
        e, sp = e_[t % 3], sp_[t % 3]
        pz = m.pw()
        k.matmul(pz[:], kv_, qv, start=True, stop=not diag)
        if diag:
            k.matmul(pz[:], m.identb[:], masks[:, jj, :], start=False, stop=True)
        k.act(e[:], pz[:], AF.Exp)
        k.act(sp[:], e[:], AF.Ln, bias=1.0)
        pd = pds[t % len(pds)]
        return (kv_, qv, diag, jj, sp, v_h, pd, pz)

    def stage_pd(t, a):
        kb = tiles[t][2]
        sp, pd = a[4], a[6]
        if kb > 0:
            for sub in range(4):
                k.matmul(pd[:, sub:sub + 1], sp[:, sub * 128:(sub + 1) * 128], m.onesb[:, 0:1])

    def stage_b(t, a, mid):
        h, i, kb, nkb = tiles[t]
        kv_, qv, diag, jj, sp, v_h, pd, pz = a
        att, fac = att_[t % 2], fac_[t % 2]
        cur = state["cur"]
        nxt = accs[0] if cur is not accs[0] else accs[1]
        pl = pz
        k.matmul(pl[:], m.ntriS[:], sp[:], start=False, stop=True)
        mid()
        if kb > 0:
            k.act(fac[:], pd[:, 0:4], AF.Exp, scale=-1.0)
            k.tt_bc(tmp[:], cur[:], fac[:, 0:4], 4, ALU.mult)
        k.act(att[:], pl[:], AF.Exp)
        pP = m.pw()
        for sub in range(4):
            k.matmul(pP[:, sub * 128:(sub + 1) * 128], att[:, sub * 128:(sub + 1) * 128], v_h[:, kb, :])
        if kb == 0:
            k.copy("dve", nxt[:], pP[:])
        else:
            k.tt("dve", nxt[:], pP[:], tmp[:], ALU.add)
        state["cur"] = nxt
        if kb == nkb - 1:
            pT = m.pw()
            for sub in range(4):
                k.transpose(pT[:, sub * 128:(sub + 1) * 128], nxt[:, sub * 128:(sub + 1) * 128], m.ident[:])
            o_ = ot_[i % 2]
            k.copy("dve", o_[:], pT[:])
            k.dma("sp", d_o[h, :, i * 512:(i + 1) * 512], o_[:])

    LA = 2
    nt = len(tiles)
    A = {}
    for t0 in range(min(LA, nt)):
        A[t0] = stage_a(t0)
    stage_pd(0, A[0])
    for t in range(nt):
        if t + LA < nt:
            A[t + LA] = stage_a(t + LA)
        if t + 1 < nt:
            stage_b(t, A[t], lambda: stage_pd(t + 1, A[t + 1]))
        else:
            stage_b(t, A[t], lambda: None)
        del A[t]
    m.pw = old_pw


def build_gdn(S, NH=4, NGRP=1):
    nc, k, m = new_prog()
    W = NH * 128
    T = dict(x=din(k, m, "x", [S, D]), an=din(k, m, "an", [8, 128]), w_qkvz=din(k, m, "w_qkvz", [D, NGRP * 4 * W]),
             w_bg=din(k, m, "w_bg", [D, NGRP * 2 * NH]), conv=din(k, m, "conv", [4, NGRP * 3 * W]),
             alog=din(k, m, "alog", [1, NGRP * NH]), dtb=din(k, m, "dtb", [1, NGRP * NH]),
             ogain=din(k, m, "ogain", [1, 128]), ogT=dout(k, m, "ogT", [NGRP * NH, 128, S], BF16))
    ov = emit_gdn(k, m, S, T, NH, NGRP)
    print("gdn arena words used", m.arena.off, "of", m.arena.n32)
    k.final_wait("sp", [ov])
    k.emit()
    return nc


def emit_gdn(k, m, S, T, NH=4, NGRP=1):
    m.arena.reset()
    al = m.arena.alloc
    W = NH * 128
    d_x, d_an, d_w, d_wbg, d_conv, d_alog, d_dtb, d_og, d_o = (T["x"], T["an"], T["w_qkvz"], T["w_bg"], T["conv"],
                                                                  T["alog"], T["dtb"], T["ogain"], T["ogT"])
    NG = 3 * NH
    cols = load_cols(k, m, [(d_an[:], 8), (d_og[:], 1)])
    convc = load_cols(k, m, [(k.track(d_conv.t.rearrange("t (j p) -> (t j) p", p=128), "cwv")[:], 4 * NG * NGRP)])
    w = al([128, 8, NGRP * 4 * W], BF16, "w")
    wbg = al([128, 8, NGRP * 2 * NH], BF16, "wbg")
    load_w(k, w, lambda kc: d_w[kc * 128:(kc + 1) * 128, :], 8)
    load_w(k, wbg, lambda kc: d_wbg[kc * 128:(kc + 1) * 128, :], 8)
    alog = al([128, NGRP * NH], F32, "alog")
    dtb = al([128, NGRP * NH], F32, "dtb")
    k.dma("sp", alog[:], k.track(d_alog.t[0, :].partition_broadcast(128), "alv")[:])
    k.dma("sp", dtb[:], k.track(d_dtb.t[0, :].partition_broadcast(128), "dtv")[:])
    nexpA = al([128, NGRP * NH], F32, "nexpA")
    k.act(nexpA[:], alog[:], AF.Exp)
    k.ts("dve", nexpA[:], nexpA[:], -1.0, ALU.mult)
    ident4 = al([128, W], F32, "ident4")
    umask4 = al([128, W], F32, "umask4")
    lvl4 = [al([128, W], BF16, "lvl4_%d" % i) for i in range(7)]
    for h in range(NH):
        sl = slice(h * 128, (h + 1) * 128)
        k.copy("dve", ident4[:, sl], m.ident[:])
        k.copy("dve", umask4[:, sl], m.umask[:])
        for i in range(7):
            k.copy("dve", lvl4[i][:, sl], m.lvl[i][:])
    junk = al([128, 1024], BF16)
    junk2 = al([128, 1024], BF16)
    ss = al([128, 1], F32)
    xt = [al([128, D], F32, "xt%d" % i) for i in range(2)]
    xs = al([128, D], F32, "xs")
    xnTs = [al([128, 8, 128], BF16, "xnT%d" % i) for i in range(2)]
    cbs = [[al([128, NH, 131], BF16, "cb%d_%d" % (G, g)) for g in range(3)] for G in range(NGRP)]
    dgw = al([128, 4 * NG * NGRP, 128], BF16, "dgw")
    for c_ in range(4 * NG * NGRP):
        k.ts("dve", dgw[:, c_, :], m.identb[:], convc[:, c_:c_ + 1], ALU.mult)
    for G in range(NGRP):
        for g in range(3):
            k.memset("dve", cbs[G][g][:], 0.0)
    F2 = lambda n, dt=F32: al([128, W], dt, n)
    Ssts = [F2("S%d" % G) for G in range(NGRP)]
    Sbs = [F2("Sb%d" % G, BF16) for G in range(NGRP)]
    for G in range(NGRP):
        k.memset("dve", Ssts[G][:], 0.0)
        k.memset("dve", Sbs[G][:], 0.0)
    NSLOT = 2 if (NGRP >= 2 and NH <= 2) else 1

    def mk_slot(si):
        F = lambda n, dt=F32: al([128, W], dt, "%s_s%d" % (n, si))
        sl = {}
        sl["sil"] = [F("sil%d" % g_) for g_ in range(3)]
        for n in ("zs", "ri", "kTf", "vtm", "ktm", "dg", "tmpF", "Fm", "FU", "Am", "u", "t1", "o_", "on", "St"):
            sl[n] = F(n)
        sl["sq2"] = F("sq2", BF16)
        for n in ("qT", "kT", "Pa", "Pb", "Xa", "Xb", "Y", "attnT", "vb", "kb", "kd", "wT", "vnew"):
            sl[n] = F(n, BF16)
        sl["L"] = [F("L%d" % i, BF16) for i in range(7)]
        sl["ogt"] = [F("ogt%d" % i, BF16) for i in range(2)]
        sl["sm"] = {n: al([128, NH], F32, "%s_s%d" % (n, si)) for n in
                    ("beta", "g", "gc", "ngc", "glast", "egc", "ekd", "eglast", "sckb", "tmp8", "ss4")}
        return sl
    slots = [mk_slot(si) for si in range(NSLOT)]
    oview = k.track(d_o.t.rearrange("h p t -> p h t"), "oview")
    qscale = float(np.log(128.0 ** -0.5))
    HS = [slice(h * 128, (h + 1) * 128) for h in range(NH)]

    def mm4(p, lhs_of, rhs_of):
        for h in range(NH):
            k.matmul(p[:, HS[h]], lhs_of(h), rhs_of(h))

    NCH = S // 128

    def prologue(cj):
        x_t = xt[cj % 2]
        k.dma("sp", x_t[:], d_x[cj * 128:(cj + 1) * 128, :])
        rstd_of(k, junk2, ss, x_t[:], D)
        k.ts("dve", xs[:], x_t[:], ss[:], ALU.mult)
        norm_T(k, m, xs, [cols[:, 0:8]], [xnTs[cj % 2]])

    for ci in range(NCH):
        if ci == 0:
            prologue(0)
        xnT = xnTs[ci % 2]

        def gbody(G, sl):
            cb, Sst, Sb = cbs[G], Ssts[G], Sbs[G]
            dtbG, nexpAG = dtb[:, G * NH:(G + 1) * NH], nexpA[:, G * NH:(G + 1) * NH]
            sil, sm, L, ogt = sl["sil"], sl["sm"], sl["L"], sl["ogt"]
            zs, sq2, ri, kTf, vtm, ktm, dg, tmpF, Fm, FU, Am = (sl[n] for n in ("zs", "sq2", "ri", "kTf", "vtm", "ktm", "dg", "tmpF", "Fm", "FU", "Am"))
            u, t1, o_, on, St = (sl[n] for n in ("u", "t1", "o_", "on", "St"))
            qT, kT, Pa, Pb, Xa, Xb, Y, attnT, vb, kb, kd, wT, vnew = (sl[n] for n in ("qT", "kT", "Pa", "Pb", "Xa", "Xb", "Y", "attnT", "vb", "kb", "kd", "wT", "vnew"))
            yield
            banks = []
            for g in range(4):
                p = m.pw()
                for h in range(NH):
                    fc = g * NH + h
                    for kc in range(8):
                        k.matmul(p[:, HS[h]], w[:, kc, G * 4 * W + fc * 128:G * 4 * W + (fc + 1) * 128], xnT[:, kc, :], start=(kc == 0), stop=(kc == 7))
                banks.append(p)
            k.act(zs[:], banks[3][:, 0:W], AF.Silu)
            for g in range(3):
                src3 = View(banks[g], banks[g].t[:, 0:W].rearrange("p (s d) -> p s d", s=NH))
                k.copy("act", cb[g][:, :, 3:131], src3)
            for g in range(3):
                pc = m.pw()
                for h in range(NH):
                    j = g * NH + h
                    for t in range(4):
                        k.matmul(pc[:, HS[h]], dgw[:, t * NG * NGRP + G * NG + j, :], cb[g][:, h, t:t + 128],
                                 start=(t == 0), stop=(t == 3))
                k.copy("act", cb[g][:, :, 0:3], cb[g][:, :, 128:131])
                k.act(sil[g][:], pc[:, 0:W], AF.Silu)
            yield
            pbg = m.pw()
            for kc in range(8):
                k.matmul(pbg[:, 0:2 * NH], xnT[:, kc, :], wbg[:, kc, G * 2 * NH:(G + 1) * 2 * NH], start=(kc == 0), stop=(kc == 7))
            k.act(sm["beta"][:], pbg[:, 0:NH], AF.Exp, scale=-1.0)
            k.ts("dve", sm["beta"][:], sm["beta"][:], 1.0, ALU.add)
            k.recip(sm["beta"][:], sm["beta"][:])
            k.tt("dve", sm["g"][:], pbg[:, NH:2 * NH], dtbG, ALU.add)
            k.act(sm["g"][:], sm["g"][:], AF.Exp)
            k.act(sm["g"][:], sm["g"][:], AF.Ln, bias=1.0)
            k.tt("dve", sm["g"][:], sm["g"][:], nexpAG, ALU.mult)
            pg = m.pw()
            k.matmul(pg[:, 0:NH], m.triI[:], sm["g"][:])
            k.matmul(pg[:, 8:8 + NH], m.ones[:], sm["g"][:])
            k.copy("dve", sm["gc"][:], pg[:, 0:NH])
            k.copy("dve", sm["glast"][:], pg[:, 8:8 + NH])
            k.ts("dve", sm["ngc"][:], sm["gc"][:], -1.0, ALU.mult)
            k.act(sm["egc"][:], sm["gc"][:], AF.Exp)
            k.act(sm["eglast"][:], sm["glast"][:], AF.Exp)
            k.tt("dve", sm["tmp8"][:], sm["glast"][:], sm["gc"][:], ALU.subtract)
            k.act(sm["ekd"][:], sm["tmp8"][:], AF.Exp)
            k.tt("dve", sm["sckb"][:], sm["beta"][:], sm["egc"][:], ALU.mult)
            yield
            for g in range(2):
                k.act(sq2[:], sil[g][:], AF.Square)
                ps = m.pw()
                k.matmul(ps[:, 0:W], m.onesb[:], sq2[:])
                k.act(ri[:], ps[:, 0:W], AF.Ln, bias=EPS)
                if g == 0:
                    k.act(ri[:], ri[:], AF.Exp, scale=-0.5, bias=qscale)
                    k.tt("dve", qT[:], sil[0][:], ri[:], ALU.mult)
                else:
                    k.act(ri[:], ri[:], AF.Exp, scale=-0.5)
                    k.tt("dve", kTf[:], sil[1][:], ri[:], ALU.mult)
                    k.copy("act", kT[:], kTf[:])
            pt = m.pw()
            for h in range(NH):
                k.transpose(pt[:, HS[h]], kTf[:, HS[h]], m.ident[:])
            k.copy("dve", ktm[:], pt[:, 0:W])
            pt = m.pw()
            for h in range(NH):
                k.transpose(pt[:, HS[h]], sil[2][:, HS[h]], m.ident[:])
            k.copy("dve", vtm[:], pt[:, 0:W])
            yield
            k.tt_bc(dg[:], ident4[:], sm["gc"][:, 0:NH], NH, ALU.mult)
            pR = m.pw()
            k.matmul(pR[:, 0:W], m.ones[:], dg[:])
            for h in range(NH):
                k.act(tmpF[:, HS[h]], pR[:, HS[h]], AF.Abs, bias=sm["ngc"][:, h:h + 1])
            k.act(Fm[:], tmpF[:], AF.Exp, scale=-1.0)
            k.tt("dve", FU[:], Fm[:], umask4[:], ALU.mult)
            pKK = m.pw()
            mm4(pKK, lambda h: kT[:, HS[h]], lambda h: kT[:, HS[h]])
            for h in range(NH):
                k.stt(Am[:, HS[h]], pKK[:, HS[h]], sm["beta"][:, h:h + 1], Fm[:, HS[h]], ALU.mult, ALU.mult)
            pQK = m.pw()
            mm4(pQK, lambda h: kT[:, HS[h]], lambda h: qT[:, HS[h]])
            k.tt("dve", attnT[:], pQK[:, 0:W], FU[:], ALU.mult)
            for i in range(7):
                k.tt("dve", L[i][:], Am[:], lvl4[i][:], ALU.mult)
            yield
            pY = m.pw()
            mm4(pY, lambda h: L[0][:, HS[h]], lambda h: m.identb[:])
            Pc, Pn, Xc, Xn = Pa, Pb, Xa, Xb
            k.stt(Pc[:], pY[:, 0:W], -1.0, ident4[:], ALU.mult, ALU.add)
            k.tt("dve", Xc[:], ident4[:], L[0][:], ALU.subtract)
            for i in range(1, 7):
                pY = m.pw()
                mm4(pY, lambda h: L[i][:, HS[h]], lambda h: Pc[:, HS[h]])
                k.copy("act", Y[:], pY[:, 0:W])
                pZ = m.pw()
                mm4(pZ, lambda h: Xc[:, HS[h]], lambda h: Y[:, HS[h]])
                if i < 6:
                    pZT = m.pw()
                    mm4(pZT, lambda h: Y[:, HS[h]], lambda h: Xc[:, HS[h]])
                k.stt(Pn[:], pZ[:, 0:W], -1.0, Pc[:], ALU.mult, ALU.add)
                Pc, Pn = Pn, Pc
                if i < 6:
                    k.stt(Xn[:], pZT[:, 0:W], -1.0, Xc[:], ALU.mult, ALU.add)
                    Xc, Xn = Xn, Xc
                yield
            yield
            k.tt_bc(vb[:], vtm[:], sm["beta"][:, 0:NH], NH, ALU.mult)
            k.tt_bc(kb[:], ktm[:], sm["sckb"][:, 0:NH], NH, ALU.mult)
            k.tt_bc(kd[:], ktm[:], sm["ekd"][:, 0:NH], NH, ALU.mult)
            pu = m.pw()
            mm4(pu, lambda h: Pc[:, HS[h]], lambda h: vb[:, HS[h]])
            k.copy("dve", u[:], pu[:, 0:W])
            pw_ = m.pw()
            mm4(pw_, lambda h: kb[:, HS[h]], lambda h: Pc[:, HS[h]])
            k.copy("dve", wT[:], pw_[:, 0:W])
            yield
            pws = m.pw()
            mm4(pws, lambda h: wT[:, HS[h]], lambda h: Sb[:, HS[h]])
            k.stt(vnew[:], pws[:, 0:W], -1.0, u[:], ALU.mult, ALU.add)
            po1 = m.pw()
            mm4(po1, lambda h: qT[:, HS[h]], lambda h: Sb[:, HS[h]])
            po2 = m.pw()
            mm4(po2, lambda h: attnT[:, HS[h]], lambda h: vnew[:, HS[h]])
            k.tt_bc(t1[:], po1[:, 0:W], sm["egc"][:, 0:NH], NH, ALU.mult)
            k.tt("dve", o_[:], po2[:, 0:W], t1[:], ALU.add)
            pS = m.pw()
            mm4(pS, lambda h: kd[:, HS[h]], lambda h: vnew[:, HS[h]])
            k.tt_bc(St[:], Sst[:], sm["eglast"][:, 0:NH], NH, ALU.mult)
            k.tt("dve", Sst[:], pS[:, 0:W], St[:], ALU.add)
            k.copy("act", Sb[:], Sst[:])
            yield
            for h in range(NH):
                k.act(junk[:, 0:128], o_[:, HS[h]], AF.Square, accum_out=sm["ss4"][:, h:h + 1])
            k.ts("dve", sm["ss4"][:], sm["ss4"][:], 1.0 / 128, ALU.mult, EPS, ALU.add)
            k.act(sm["ss4"][:], sm["ss4"][:], AF.Ln)
            k.act(sm["ss4"][:], sm["ss4"][:], AF.Exp, scale=-0.5)
            k.tt_bc(on[:], o_[:], sm["ss4"][:, 0:NH], NH, ALU.mult)
            pT = m.pw()
            for h in range(NH):
                k.transpose(pT[:, HS[h]], on[:, HS[h]], m.ident[:])
            og = ogt[(ci * NGRP + G) % 2]
            k.stt(og[:], pT[:, 0:W], cols[:, 8:9], zs[:], ALU.mult, ALU.mult)
            og3 = k.track(og.t.rearrange("p (a b) -> p a b", a=NH), "og3")
            og3.reads, og3.writes = og.reads, og.writes
            k.dma("sp", oview[:, G * NH:(G + 1) * NH, ci * 128:(ci + 1) * 128], og3[:])

        for G0 in range(0, NGRP, NSLOT):
            gens = [gbody(G0 + si, slots[si]) for si in range(NSLOT)]
            alive = list(gens)
            step = 0
            while alive:
                for gen_ in list(alive):
                    try:
                        next(gen_)
                    except StopIteration:
                        alive.remove(gen_)
                step += 1
                if G0 + NSLOT >= NGRP and step == 5 and ci + 1 < NCH:
                    prologue(ci + 1)
    return oview


GDN_HPG = 4


def build_fused(S):
    nc, k, m = new_prog()
    NPOS = S // 1024
    TOKO = NPOS * 512
    I = lambda n, shp, dt=F32: din(k, m, n, shp, dt)
    x = I("x", [S, D])
    Tg = dict(x=x, an=I("an0", [8, 128]), w_qkvz=I("w_qkvz", [D, 4096]), w_bg=I("w_bg", [D, 16]),
              conv=I("conv", [4, 3072]), alog=I("alog", [1, 8]), dtb=I("dtb", [1, 8]), ogain=I("ogain", [1, 128]))
    ogT = k.dram("ogT_scr", [8, 128, S], BF16)
    Tg["ogT"] = ogT
    h2 = k.dram("h2_scr", [S, D], F32)
    Tf0 = dict(x=x, aT=ogT, w_o=I("gdn_w_out", [D, D]), fnorm=I("fn0", [8, 128]), w_gu=I("w_gu0", [D, 2 * DFF]),
               w_d=I("w_d0", [DFF, D]), h=h2)
    qT = k.dram("qT_scr", [8, 128, S], BF16)
    kT = k.dram("kT_scr", [8, 128, S], BF16)
    v = k.dram("v_scr", [S, D], BF16)
    Tk = dict(h=h2, kvn=I("kvn", [8, 128]), an=I("an1", [8, 128]), kg=I("kg", [1, 128]), qg=I("qg", [1, 128]),
              w_kv=I("w_kv", [D, 2 * D]), w_q=I("w_q", [D, D]), qT=qT, kT=kT, v=v)
    oTs = k.dram("oT_scr", [8, 128, TOKO], BF16)
    Ta = dict(qT=qT, kT=kT, v=v, masks=I("masks", [128, 8, 512], BF16), oT=oTs)
    d_sel = I("sel", [128, 2])
    out = dout(k, m, "out", [TOKO, D])
    Tf1 = dict(x=None, aT=oTs, w_o=I("sb_w_out", [D, D]), fnorm=I("fn1", [8, 128]), w_gu=I("w_gu1", [D, 2 * DFF]),
               w_d=I("w_d1", [DFF, D]), h=out)
    sel = k.sbuf([128, 2], F32, "sel_sb")
    k.dma("sp", sel[:], d_sel[:])

    emit_gdn(k, m, S, Tg, GDN_HPG, 8 // GDN_HPG)
    barrier(k)
    emit_dense_ffn(k, m, S, Tf0)
    barrier(k)
    emit_kvq(k, m, S, Tk)
    barrier(k)
    emit_attn(k, m, S, 8, Ta, parity=True, sel=sel)
    barrier(k)

    def xload(ti, x_t, xtmp):
        j, r = ti // 4, ti % 4
        ra = (2 * j) * 512 + r * 128
        rb = (2 * j + 1) * 512 + r * 128
        k.dma("sp", xtmp[0][:], h2[ra:ra + 128, :])
        k.dma("sp", xtmp[1][:], h2[rb:rb + 128, :])
        k.ts("dve", xtmp[0][:], xtmp[0][:], sel[:, 0:1], ALU.mult)
        k.stt(x_t[:], xtmp[1][:], sel[:, 1:2], xtmp[0][:], ALU.mult, ALU.add)
    emit_dense_ffn(k, m, TOKO, Tf1, xload=xload)
    k.final_wait("sp", [out])
    k.emit()
    return nc


def host_parity_masks(p):
    c = host_masks()
    z = np.zeros_like(c)
    f = np.full_like(c, NEG)
    return _c(np.concatenate([z, c], axis=1) if p == 1 else np.concatenate([c, f], axis=1))


def kernel_fused(inp, x):
    B, S, _ = x.shape
    cores = list(range(2 * B))
    cf, cb = host_consts()
    w_in = inp["gdn_w_in"][0]
    cw = inp["gdn_conv_w"][0]

    def grp(a, width, hpg=GDN_HPG):
        nblk = a.shape[1] // (8 * width)
        parts = []
        for G in range(8 // hpg):
            for blk in range(nblk):
                parts.append(a[:, blk * 8 * width + G * hpg * width: blk * 8 * width + (G + 1) * hpg * width])
        return _c(np.concatenate(parts, axis=1))
    common = dict(cf=cf, cb=cb, an0=_c(inp["attn_norm"][0].reshape(8, 128)), w_qkvz=grp(w_in[:, 0:4096], 128),
                  w_bg=grp(w_in[:, 4096:4112], 1), conv=grp(cw, 128), alog=_c(inp["gdn_a_log"].reshape(1, 8)),
                  dtb=_c(inp["gdn_dt_bias"].reshape(1, 8)), ogain=_c(inp["gdn_o_gain"].reshape(1, 128)),
                  gdn_w_out=_c(inp["gdn_w_out"][0]), fn0=_c(inp["ffn_norm"][0].reshape(8, 128)),
                  w_gu0=_c(inp["ffn_w_gu"][0]), w_d0=_c(inp["ffn_w_down"][0]),
                  kvn=_c(inp["kv_norm"].reshape(8, 128)), an1=_c(inp["attn_norm"][1].reshape(8, 128)),
                  kg=_c(inp["k_gain"].reshape(1, 128)), qg=_c(inp["sb_q_gain"].reshape(1, 128)),
                  w_kv=_c(inp["w_kv"]), w_q=_c(inp["sb_w_q"][0]), sb_w_out=_c(inp["sb_w_out"][0]),
                  fn1=_c(inp["ffn_norm"][1].reshape(8, 128)), w_gu1=_c(inp["ffn_w_gu"][1]), w_d1=_c(inp["ffn_w_down"][1]))
    feeds = []
    for c in cores:
        b, p = c // 2, c % 2
        selv = np.empty((128, 2), np.float32)
        selv[:, 0] = 1.0 - p
        selv[:, 1] = float(p)
        feeds.append(dict(common, x=_c(x[b]), masks=host_parity_masks(p), sel=selv))
    res = run_bass_kernel_spmd(_prog("fused", build_fused, S), feeds, core_ids=cores).results
    out = np.empty((B, S, D), np.float32)
    for c in cores:
        b, p = c // 2, c % 2
        o = res[c]["out"]
        for j in range(S // 1024):
            out[b, (2 * j + p) * 512:(2 * j + p + 1) * 512] = o[j * 512:(j + 1) * 512]
    return out

from concourse.bass_utils import run_bass_kernel_spmd

_PROGS = {}


def _prog(name, fn, *args):
    key = (name,) + args
    if key not in _PROGS:
        _PROGS[key] = fn(*args)
    return _PROGS[key]


def _c(a):
    return np.ascontiguousarray(a)


def kernel_unfused(**inputs):
    inp = {n: np.asarray(v) for n, v in inputs.items()}
    x = inp["x"].astype(np.float32, copy=False)
    B, S, _ = x.shape
    NC = 8
    HALF = S // 2
    cores = list(range(NC))
    cf, cb = host_consts()
    cst = {"cf": cf, "cb": cb}

    w_in = inp["gdn_w_in"][0]
    cw = inp["gdn_conv_w"][0]
    feeds = []
    for c in cores:
        b, hg = c // 2, c % 2
        hs = slice(hg * 512, (hg + 1) * 512)
        h4 = slice(hg * 4, (hg + 1) * 4)
        feeds.append(dict(cst, x=_c(x[b]), an=_c(inp["attn_norm"][0].reshape(8, 128)),
                          w_qkvz=_c(np.concatenate([w_in[:, 0:1024][:, hs], w_in[:, 1024:2048][:, hs],
                                                    w_in[:, 2048:3072][:, hs], w_in[:, 3072:4096][:, hs]], axis=1)),
                          w_bg=_c(np.concatenate([w_in[:, 4096:4104][:, h4], w_in[:, 4104:4112][:, h4]], axis=1)),
                          conv=_c(np.concatenate([cw[:, 0:1024][:, hs], cw[:, 1024:2048][:, hs], cw[:, 2048:3072][:, hs]], axis=1)),
                          alog=_c(inp["gdn_a_log"][:, h4]), dtb=_c(inp["gdn_dt_bias"][:, h4]),
                          ogain=_c(inp["gdn_o_gain"].reshape(1, 128))))
    r1 = run_bass_kernel_spmd(_prog("gdn", build_gdn, S, 4), feeds, core_ids=cores).results
    ogT = [np.concatenate([r1[2 * b]["ogT"], r1[2 * b + 1]["ogT"]], axis=0) for b in range(B)]

    nc_ffn = _prog("ffn", build_dense_ffn, HALF)
    feeds = []
    for c in cores:
        b, hf = c // 2, c % 2
        ts = slice(hf * HALF, (hf + 1) * HALF)
        feeds.append(dict(cst, x=_c(x[b, ts]), aT=_c(ogT[b][:, :, ts]), w_o=_c(inp["gdn_w_out"][0]),
                          fnorm=_c(inp["ffn_norm"][0].reshape(8, 128)), w_gu=_c(inp["ffn_w_gu"][0]),
                          w_d=_c(inp["ffn_w_down"][0])))
    r2 = run_bass_kernel_spmd(nc_ffn, feeds, core_ids=cores).results
    h2 = [r["h"] for r in r2]

    feeds = []
    for c in cores:
        feeds.append(dict(cst, h=_c(h2[c]), kvn=_c(inp["kv_norm"].reshape(8, 128)),
                          an=_c(inp["attn_norm"][1].reshape(8, 128)), kg=_c(inp["k_gain"].reshape(1, 128)),
                          qg=_c(inp["sb_q_gain"].reshape(1, 128)), w_kv=_c(inp["w_kv"]), w_q=_c(inp["sb_w_q"][0])))
    r3 = run_bass_kernel_spmd(_prog("kvq", build_kvq, HALF), feeds, core_ids=cores).results

    mk = host_masks()
    feeds = []
    for c in cores:
        b, hg = c // 2, c % 2
        h4 = slice(hg * 4, (hg + 1) * 4)
        qT = np.concatenate([r3[2 * b]["qT"][h4], r3[2 * b + 1]["qT"][h4]], axis=2)
        kT = np.concatenate([r3[2 * b]["kT"][h4], r3[2 * b + 1]["kT"][h4]], axis=2)
        v = np.concatenate([r3[2 * b]["v"], r3[2 * b + 1]["v"]], axis=0)[:, hg * 512:(hg + 1) * 512]
        feeds.append(dict(cst, qT=_c(qT), kT=_c(kT), v=_c(v), masks=mk))
    r4 = run_bass_kernel_spmd(_prog("attn", build_attn, S, 4), feeds, core_ids=cores).results
    oT = [np.concatenate([r4[2 * b]["oT"], r4[2 * b + 1]["oT"]], axis=0) for b in range(B)]

    feeds = []
    for c in cores:
        b, hf = c // 2, c % 2
        ts = slice(hf * HALF, (hf + 1) * HALF)
        feeds.append(dict(cst, x=_c(h2[c]), aT=_c(oT[b][:, :, ts]), w_o=_c(inp["sb_w_out"][0]),
                          fnorm=_c(inp["ffn_norm"][1].reshape(8, 128)), w_gu=_c(inp["ffn_w_gu"][1]),
                          w_d=_c(inp["ffn_w_down"][1])))
    r5 = run_bass_kernel_spmd(_prog("ffn_b", build_dense_ffn, HALF), feeds, core_ids=cores).results
    out = np.empty((B, S, D), np.float32)
    for c in cores:
        b, hf = c // 2, c % 2
        out[b, hf * HALF:(hf + 1) * HALF] = r5[c]["h"]
    return out


def kernel(**inputs):
    inp = {n: np.asarray(v) for n, v in inputs.items()}
    x = inp["x"].astype(np.float32, copy=False)
    return kernel_fused(inp, x)
```

```python
from contextlib import ExitStack
import numpy as np
import concourse.bass as bass
import concourse.mybir as mybir

F32 = mybir.dt.float32
BF16 = mybir.dt.bfloat16
ALU = mybir.AluOpType
AF = mybir.ActivationFunctionType
AX = mybir.AxisListType

SAME_ENGINE_SYNC = True
N_DMA_SEMS = 6


class Buf:
    def __init__(self, t, name):
        self.t = t
        self.name = name
        self.reads = {}
        self.writes = {}

    def __getitem__(self, idx):
        return View(self, self.t[idx])


class View:
    def __init__(self, buf, ap):
        self.buf = buf
        self.ap = ap


def _ap(x):
    return x.ap if isinstance(x, View) else x


class Eng:
    def __init__(self, name):
        self.name = name
        self.ops = []
        self.trace = []
        self.count = 0
        self.waited = {}
        self.sem = None
        self.dma_sems = []
        self.dma_vals = []
        self.dma_k = 0


class K:
    def __init__(self, nc):
        self.nc = nc
        self.es = ExitStack()
        self.engs = {n: Eng(n) for n in ("pe", "act", "dve", "pool", "sp")}
        self.sems = {}
        for n, e in self.engs.items():
            e.sem = self.es.enter_context(nc.semaphore("s_" + n))
            self.sems[n] = e.sem
        for n in ("sp", "pool", "act"):
            e = self.engs[n]
            for i in range(N_DMA_SEMS):
                s = self.es.enter_context(nc.semaphore("d_%s%d" % (n, i)))
                key = "d_%s%d" % (n, i)
                self.sems[key] = s
                e.dma_sems.append(key)
                e.dma_vals.append(0)
        self.nbuf = 0

    def sbuf(self, shape, dtype, name=None):
        self.nbuf += 1
        name = name or "sb%d" % self.nbuf
        t = self.es.enter_context(self.nc.sbuf_tensor(name, list(shape), dtype))
        return Buf(t, name)

    def psum(self, shape, dtype, name=None):
        self.nbuf += 1
        name = name or "ps%d" % self.nbuf
        t = self.es.enter_context(self.nc.psum_tensor(name, list(shape), dtype))
        return Buf(t, name)

    def dram(self, name, shape, dtype, kind="Internal"):
        t = self.nc.dram_tensor(name, list(shape), dtype, kind=kind)
        return Buf(t.ap(), name)

    def track(self, ap, name):
        return Buf(ap, name)

    def _wait(self, e, key, val, war=False):
        if key == e.name and (e.name == "pe" or war or not SAME_ENGINE_SYNC):
            return
        if e.waited.get(key, 0) >= val:
            return
        e.waited[key] = val
        sem = self.sems[key]
        e.ops.append(("wait", key, val))
        e.trace.append(("w", key, val))

    def _deps(self, e, reads, writes, nowaw=False):
        for v in reads:
            if isinstance(v, View):
                for k, val in v.buf.writes.items():
                    self._wait(e, k, val)
        for v in writes:
            if isinstance(v, View):
                for k, val in v.buf.reads.items():
                    self._wait(e, k, val, war=True)
                if not nowaw:
                    for k, val in v.buf.writes.items():
                        self._wait(e, k, val)

    def _mark(self, key, val, reads, writes):
        for v in reads:
            if isinstance(v, View):
                if v.buf.reads.get(key, 0) < val:
                    v.buf.reads[key] = val
        for v in writes:
            if isinstance(v, View):
                if v.buf.writes.get(key, 0) < val:
                    v.buf.writes[key] = val

    def op(self, eng, fn, reads, writes, nowaw=False):
        e = self.engs[eng]
        self._deps(e, reads, writes, nowaw)
        e.count += 1
        sem = e.sem
        e.ops.append(("op", fn, e.count))
        e.trace.append(("i", e.name, 1))
        self._mark(e.name, e.count, reads, writes)

    def dma(self, eng, out, in_, nowaw=True, **kw):
        e = self.engs[eng]
        self._deps(e, [in_], [out], nowaw)
        i = e.dma_k % N_DMA_SEMS
        e.dma_k += 1
        key = e.dma_sems[i]
        if e.dma_vals[i] > 0:
            self._wait(e, key, e.dma_vals[i])
        e.dma_vals[i] += 16
        val = e.dma_vals[i]
        sem = self.sems[key]
        o, s = _ap(out), _ap(in_)
        e.ops.append(("dma", o, s, key, kw))
        e.trace.append(("i", key, 16))
        self._mark(key, val, [in_], [out])
        return (key, val)

    def allgather(self, out, in_, groups):
        e = self.engs["pool"]
        self._deps(e, [in_], [out], False)
        i = e.dma_k % N_DMA_SEMS
        e.dma_k += 1
        key = e.dma_sems[i]
        if e.dma_vals[i] > 0:
            self._wait(e, key, e.dma_vals[i])
        e.dma_vals[i] += 16
        val = e.dma_vals[i]
        o, s_ = _ap(out), _ap(in_)
        e.ops.append(("cc", o, s_, key, groups))
        e.trace.append(("i", key, 16))
        self._mark(key, val, [in_], [out])

    def final_wait(self, eng, bufs):
        e = self.engs[eng]
        for b in bufs:
            for k, val in b.writes.items():
                self._wait(e, k, val)

    def matmul(self, out, lhsT, rhs, start=True, stop=True, **kw):
        o, l, r = _ap(out), _ap(lhsT), _ap(rhs)
        self.op("pe", lambda h: h.matmul(o, l, r, start=start, stop=stop, **kw), [lhsT, rhs], [out])

    def transpose(self, out, in_, ident):
        o, i, d = _ap(out), _ap(in_), _ap(ident)
        self.op("pe", lambda h: h.transpose(o, i, d), [in_, ident], [out])

    def act(self, out, in_, func, bias=None, scale=None, accum_out=None, eng="act"):
        o, i = _ap(out), _ap(in_)
        kw = {}
        rd = [in_]
        wr = [out]
        if bias is not None:
            kw["bias"] = _ap(bias)
            rd.append(bias)
        if scale is not None:
            kw["scale"] = _ap(scale)
            rd.append(scale)
        if accum_out is not None:
            kw["accum_out"] = _ap(accum_out)
            wr.append(accum_out)
        self.op("act", lambda h: h.activation(o, i, func, **kw), rd, wr)

    def tt(self, eng, out, in0, in1, op):
        o, a, b = _ap(out), _ap(in0), _ap(in1)
        self.op(eng, lambda h: h.tensor_tensor(o, a, b, op), [in0, in1], [out])

    def ts(self, eng, out, in0, s1, op0, s2=None, op1=None, accum_out=None):
        o, a = _ap(out), _ap(in0)
        rd = [in0]
        wr = [out]
        if isinstance(s1, View):
            rd.append(s1)
        if isinstance(s2, View):
            rd.append(s2)
        kw = {}
        if op1 is not None:
            kw["op1"] = op1
        if accum_out is not None:
            kw["accum_out"] = _ap(accum_out)
            wr.append(accum_out)
        x1, x2 = _ap(s1), _ap(s2)
        self.op(eng, lambda h: h.tensor_scalar(o, a, x1, x2, op0, **kw), rd, wr)

    def tt_bc(self, out, in0, sc, ng, op):
        o3 = _ap(out).rearrange("p (s d) -> p s d", s=ng)
        a3 = _ap(in0).rearrange("p (s d) -> p s d", s=ng)
        b3 = _ap(sc).unsqueeze(2).broadcast_to([128, ng, 128])
        self.op("dve", lambda h: h.tensor_tensor(o3, a3, b3, op), [in0, sc], [out])

    def stt(self, out, in0, scalar, in1, op0, op1, eng="dve"):
        o, a, b = _ap(out), _ap(in0), _ap(in1)
        rd = [in0, in1]
        if isinstance(scalar, View):
            rd.append(scalar)
        s = _ap(scalar)
        self.op(eng, lambda h: h.scalar_tensor_tensor(o, a, s, b, op0, op1), rd, [out])

    def copy(self, eng, out, in_):
        o, i = _ap(out), _ap(in_)
        if eng == "act":
            self.op("act", lambda h: h.copy(o, i), [in_], [out])
        else:
            self.op(eng, lambda h: h.tensor_copy(o, i), [in_], [out])

    def memset(self, eng, out, val):
        o = _ap(out)
        self.op(eng, lambda h: h.memset(o, val), [], [out])

    def recip(self, out, in_):
        o, i = _ap(out), _ap(in_)
        self.op("dve", lambda h: h.reciprocal(o, i), [in_], [out])

    def affine_select(self, out, in_, pattern, cmp, fill, base, cm):
        o, i = _ap(out), _ap(in_)
        self.op("pool", lambda h: h.affine_select(o, i, pattern, cmp, fill, base=base, channel_multiplier=cm),
                [in_], [out])

    def check_deadlock(self):
        vals = {}
        pos = {n: 0 for n in self.engs}
        progress = True
        while progress:
            progress = False
            for n, e in self.engs.items():
                while pos[n] < len(e.trace):
                    kind, key, v = e.trace[pos[n]]
                    if kind == "w":
                        if vals.get(key, 0) >= v:
                            pos[n] += 1
                            progress = True
                        else:
                            break
                    else:
                        vals[key] = vals.get(key, 0) + v
                        pos[n] += 1
                        progress = True
        stuck = {n: (pos[n], len(e.trace), e.trace[pos[n]] if pos[n] < len(e.trace) else None)
                 for n, e in self.engs.items()}
        ok = all(pos[n] == len(e.trace) for n, e in self.engs.items())
        return ok, stuck, vals

    def emit(self):
        ok, stuck, vals = self.check_deadlock()
        if not ok:
            raise RuntimeError("DEADLOCK in sync graph: %s" % (stuck,))
        nc = self.nc
        import bisect
        sig = {n: set() for n in self.engs}
        for n, e in self.engs.items():
            for it in e.ops:
                if it[0] == "wait" and it[1] in sig:
                    sig[it[1]].add(it[2])
        sigl = {n: sorted(v) for n, v in sig.items()}
        sems = self.sems

        def run(n, h):
            e = self.engs[n]
            for it in e.ops:
                if it[0] == "wait":
                    key, val = it[1], it[2]
                    if key in sigl:
                        val = bisect.bisect_right(sigl[key], val)
                    h.wait_ge(sems[key], val)
                elif it[0] == "op":
                    ins = it[1](h)
                    if it[2] in sig[n]:
                        ins.then_inc(e.sem, 1)
                elif it[0] == "cc":
                    _, o, s_, key, groups = it
                    h.collective_compute("AllGather", ALU.bypass, replica_groups=groups, ins=[s_], outs=[o]).then_inc(sems[key], 16)
                else:
                    _, o, s_, key, kw = it
                    h.dma_start(out=o, in_=s_, **kw).then_inc(sems[key], 16)
        self.n_inc = {n: len(v) for n, v in sig.items()}
        with nc.Block() as block:
            @block.tensor
            def _(h):
                run("pe", h)

            @block.scalar
            def _(h):
                run("act", h)

            @block.vector
            def _(h):
                run("dve", h)

            @block.gpsimd
            def _(h):
                run("pool", h)

            @block.sync
            def _(h):
                run("sp", h)
        self.es.close()


D = 1024
H = 8
DH = 128
DFF = 2816
PROJ = 4112
EPS = 1e-6
NEG = -30000.0


class M:
    pass


def host_consts():
    r = np.arange(128)[:, None]
    c = np.arange(128)[None, :]
    mats = [r == c, np.ones((128, 128)), r <= c, r > c, c >= r]
    b = 1
    while b < 128:
        mats.append(((r // b) == (c // b) + 1) & ((r // b) % 2 == 1))
        b *= 2
    cf = np.stack([np.asarray(x, np.float32) for x in mats], axis=1)
    import ml_dtypes
    bmats = [r == c, np.ones((128, 128)), -(r >= c).astype(np.float32)]
    cb = np.stack([np.asarray(x, np.float32) for x in bmats], axis=1).astype(ml_dtypes.bfloat16)
    return np.ascontiguousarray(cf), np.ascontiguousarray(cb)


def setup_consts(k, m):
    cfd = k.dram("cf", [128, 12, 128], F32, kind="ExternalInput")
    cbd = k.dram("cb", [128, 3, 128], BF16, kind="ExternalInput")
    cf = k.sbuf([128, 12, 128], F32, "cfs")
    cb = k.sbuf([128, 3, 128], BF16, "cbs")
    k.dma("sp", cf[:], cfd[:])
    k.dma("sp", cb[:], cbd[:])
    m.cf, m.cb = cf, cb

    class V:
        def __init__(self, buf, j):
            self.buf, self.j = buf, j

        def __getitem__(self, idx):
            if idx == slice(None):
                return self.buf[:, self.j, :]
            a, b = idx
            return self.buf[a, self.j, b]
    m.ident = V(cf, 0)
    m.ones = V(cf, 1)
    m.triI = V(cf, 2)
    m.lmask = V(cf, 3)
    m.umask = V(cf, 4)
    m.lvl = [V(cf, 5 + i) for i in range(7)]
    m.identb = V(cb, 0)
    m.onesb = V(cb, 1)
    m.ntriS = V(cb, 2)


class Arena:
    def __init__(self, k, nbytes):
        self.k = k
        self.n32 = nbytes // 4
        self.base = k.es.enter_context(k.nc.sbuf_tensor("arena", [128, self.n32], F32))
        self.off = 0
        self.cnt = 0

    def reset(self):
        self.off = 0

    def alloc(self, shape, dtype, name=None):
        nel = int(np.prod(shape[1:]))
        nb = nel * (2 if dtype == BF16 else 4)
        n32 = (nb + 3) // 4
        n32 = (n32 + 7) // 8 * 8
        assert self.off + n32 <= self.n32, "arena overflow %d + %d > %d" % (self.off, n32, self.n32)
        ap = self.base[0:shape[0], self.off:self.off + n32]
        self.off += n32
        if dtype != F32:
            ap = ap.bitcast(dtype)
        ap = ap[:, 0:nel]
        if len(shape) == 3:
            ap = ap.rearrange("p (a b) -> p a b", a=shape[1])
        elif len(shape) == 4:
            ap = ap.rearrange("p (a b c) -> p a b c", a=shape[1], b=shape[2])
        self.cnt += 1
        return Buf(ap, name or "ar%d" % self.cnt)


def barrier(k):
    latest = {}
    for n, e in k.engs.items():
        if e.count:
            latest[n] = e.count
        for key, v in zip(e.dma_sems, e.dma_vals):
            if v:
                latest[key] = v
    for n, e in k.engs.items():
        for key, v in latest.items():
            if key == n:
                continue
            k._wait(e, key, v)


def new_prog():
    nc = bass.Bass("TRN2", target_bir_lowering=False)
    k = K(nc)
    m = M()
    m.ins = {}
    m.outs = {}
    setup_consts(k, m)
    m.pb = [k.psum([128, 512], F32, "pb%d" % i) for i in range(8)]
    m.pwi = 0

    def pw():
        b = m.pb[m.pwi % 8]
        m.pwi += 1
        return b
    m.pw = pw
    m.arena = Arena(k, 196 * 1024)
    return nc, k, m


def din(k, m, name, shape, dt=F32):
    b = k.dram(name, shape, dt, kind="ExternalInput")
    m.ins[name] = b
    return b


def dout(k, m, name, shape, dt=F32):
    b = k.dram(name, shape, dt, kind="ExternalOutput")
    m.outs[name] = b
    return b


def load_cols(k, m, rows):
    al = m.arena.alloc
    st = al([128, 128], F32)
    out = al([128, 128], F32)
    k.memset("dve", st[:], 0.0)
    r0 = 0
    for v, n in rows:
        k.dma("sp", st[r0:r0 + n, :], v, nowaw=False)
        r0 += n
    p = m.pw()
    k.transpose(p[:, 0:128], st[:], m.ident[:])
    k.copy("dve", out[:], p[:, 0:128])
    return out


def load_w(k, dst, src, kchunks):
    for kc in range(kchunks):
        k.dma("pool", dst[:, kc, :], src(kc), max_dma_last_dim=4096)


def rstd_of(k, junk, ss, src, n):
    k.act(junk[:, 0:n], src, AF.Square, accum_out=ss[:])
    k.ts("dve", ss[:], ss[:], 1.0 / n, ALU.mult, EPS, ALU.add)
    k.act(ss[:], ss[:], AF.Ln)
    k.act(ss[:], ss[:], AF.Exp, scale=-0.5)


def norm_T(k, m, xs, gains, outs):
    for half in range(2):
        p = m.pw()
        for c4 in range(4):
            c = half * 4 + c4
            k.transpose(p[:, c4 * 128:(c4 + 1) * 128], xs[:, c * 128:(c + 1) * 128], m.ident[:])
        for g, o in zip(gains, outs):
            o3 = _ap(o[:, half * 4:half * 4 + 4, :])
            a3 = _ap(p[:]).rearrange("p (s d) -> p s d", s=4)
            b3 = _ap(g)[:, half * 4:half * 4 + 4].unsqueeze(2).broadcast_to([128, 4, 128])
            k.op("dve", lambda h, o3=o3, a3=a3, b3=b3: h.tensor_tensor(o3, a3, b3, ALU.mult), [p[:], g], [o[:]])


D = 1024
H = 8
DFF = 2816
EPS = 1e-6


def build_dense_ffn(TOK):
    nc, k, m = new_prog()
    T = dict(x=din(k, m, "x", [TOK, D]), aT=din(k, m, "aT", [8, 128, TOK], BF16), w_o=din(k, m, "w_o", [D, D]),
             fnorm=din(k, m, "fnorm", [8, 128]), w_gu=din(k, m, "w_gu", [D, 2 * DFF]), w_d=din(k, m, "w_d", [DFF, D]),
             h=dout(k, m, "h", [TOK, D]))
    emit_dense_ffn(k, m, TOK, T)
    k.final_wait("sp", [T["h"]])
    k.emit()
    return nc


def emit_dense_ffn(k, m, TOK, T, xload=None):
    m.arena.reset()
    al = m.arena.alloc
    d_x, d_aT, d_wo, d_fn, d_wgu, d_wd, d_h = T["x"], T["aT"], T["w_o"], T["fnorm"], T["w_gu"], T["w_d"], T["h"]
    cols = load_cols(k, m, [(d_fn[:], 8)])
    w_o = al([128, 8, D], BF16, "w_o")
    w_gu = al([128, 8, 2 * DFF], BF16, "w_gu")
    w_d = al([128, 22, D], BF16, "w_d")
    load_w(k, w_o, lambda kc: d_wo[kc * 128:(kc + 1) * 128, :], 8)
    load_w(k, w_gu, lambda kc: d_wgu[kc * 128:(kc + 1) * 128, :], 8)
    load_w(k, w_d, lambda kc: d_wd[kc * 128:(kc + 1) * 128, :], 22)
    junk = al([128, 1024], BF16)
    ss = al([128, 1], F32)
    xt = [al([128, D], F32, "xt%d" % i) for i in range(1 if xload is not None else 2)]
    at = [al([128, 8, 128], BF16, "at%d" % i) for i in range(2)]
    h1 = [al([128, D], F32, "h1_%d" % i) for i in range(2)]
    h2 = [al([128, D], F32, "h2_%d" % i) for i in range(1 if xload is not None else 2)]
    xtmp = [al([128, D], F32, "xtmp%d" % i) for i in range(2)] if xload is not None else None
    xs = al([128, D], F32, "xs")
    hnTs = [al([128, 8, 128], BF16, "hnT%d" % i) for i in range(2)]
    hidF = al([128, 22 * 128], BF16, "hidT")
    hidT = Buf(hidF.t.rearrange("p (a b) -> p a b", a=22), "hidT3")
    hidT.reads, hidT.writes = hidF.reads, hidF.writes
    sg = [al([128, 256], F32, "sg%d" % i) for i in range(2)]
    aTv = k.track(d_aT.t.rearrange("h p t -> p h t"), "aTv")
    NT = TOK // 128

    def prologue(ti):
        x_t, a_t, h1_, h2_ = xt[ti % len(xt)], at[ti % 2], h1[ti % 2], h2[ti % len(h2)]
        if xload is None:
            k.dma("sp", x_t[:], d_x[ti * 128:(ti + 1) * 128, :])
        else:
            xload(ti, x_t, xtmp)
        k.dma("sp", a_t[:], aTv[:, :, ti * 128:(ti + 1) * 128])
        for half in range(2):
            py = m.pw()
            for h in range(8):
                k.matmul(py[:], a_t[:, h, :], w_o[:, h, half * 512:(half + 1) * 512], start=(h == 0), stop=(h == 7))
            k.tt("dve", h1_[:, half * 512:(half + 1) * 512], py[:], x_t[:, half * 512:(half + 1) * 512], ALU.add)
        rstd_of(k, junk, ss, h1_[:], D)
        k.ts("dve", xs[:], h1_[:], ss[:], ALU.mult)
        norm_T(k, m, xs, [cols[:, 0:8]], [hnTs[ti % 2]])

    for ti in range(NT):
        if ti == 0:
            prologue(0)
        hnT = hnTs[ti % 2]
        h1_, h2_ = h1[ti % 2], h2[ti % len(h2)]
        for j2 in range(11):
            p = m.pw()
            for q in range(4):
                j = j2 * 2 + (q % 2)
                col = (0 if q < 2 else DFF) + j * 128
                for kc in range(8):
                    k.matmul(p[:, q * 128:(q + 1) * 128], w_gu[:, kc, col:col + 128], hnT[:, kc, :],
                             start=(kc == 0), stop=(kc == 7))
            s_ = sg[j2 % 2]
            k.act(s_[:], p[:, 0:256], AF.Silu)
            k.tt("dve", hidF[:, j2 * 256:(j2 + 1) * 256], p[:, 256:512], s_[:], ALU.mult)
            if j2 == 5 and ti + 1 < NT:
                prologue(ti + 1)
        for half in range(2):
            py = m.pw()
            for j in range(22):
                k.matmul(py[:], hidT[:, j, :], w_d[:, j, half * 512:(half + 1) * 512], start=(j == 0), stop=(j == 21))
            k.tt("dve", h2_[:, half * 512:(half + 1) * 512], py[:], h1_[:, half * 512:(half + 1) * 512], ALU.add)
        k.dma("sp", d_h[ti * 128:(ti + 1) * 128, :], h2_[:])


def build_kvq(TOK):
    nc, k, m = new_prog()
    T = dict(h=din(k, m, "h", [TOK, D]), kvn=din(k, m, "kvn", [8, 128]), an=din(k, m, "an", [8, 128]),
             kg=din(k, m, "kg", [1, 128]), qg=din(k, m, "qg", [1, 128]), w_kv=din(k, m, "w_kv", [D, 2 * D]),
             w_q=din(k, m, "w_q", [D, D]), qT=dout(k, m, "qT", [8, 128, TOK], BF16),
             kT=dout(k, m, "kT", [8, 128, TOK], BF16), v=dout(k, m, "v", [TOK, D], BF16))
    outs = emit_kvq(k, m, TOK, T)
    k.final_wait("sp", outs)
    k.emit()
    return nc


def emit_kvq(k, m, TOK, T):
    m.arena.reset()
    al = m.arena.alloc
    d_h, d_kvn, d_an, d_kg, d_qg, d_wkv, d_wq = T["h"], T["kvn"], T["an"], T["kg"], T["qg"], T["w_kv"], T["w_q"]
    d_qT, d_kT, d_v = T["qT"], T["kT"], T["v"]
    cols = load_cols(k, m, [(d_kvn[:], 8), (d_an[:], 8), (d_kg[:], 1), (d_qg[:], 1)])
    w_kv = al([128, 8, 2 * D], BF16, "w_kv")
    w_q = al([128, 8, D], BF16, "w_q")
    load_w(k, w_kv, lambda kc: d_wkv[kc * 128:(kc + 1) * 128, :], 8)
    load_w(k, w_q, lambda kc: d_wq[kc * 128:(kc + 1) * 128, :], 8)
    junk = al([128, 1024], BF16)
    ss = al([128, 1], F32)
    ht = [al([128, D], F32, "ht%d" % i) for i in range(2)]
    xs = al([128, D], F32, "xs")
    xkTs = [al([128, 8, 128], BF16, "xkT%d" % i) for i in range(2)]
    xqTs = [al([128, 8, 128], BF16, "xqT%d" % i) for i in range(2)]
    vt = [al([128, D], BF16, "vt%d" % i) for i in range(2)]
    ss8 = {n: al([128, 8], F32, "ss8" + n) for n in "kq"}
    nrm = {n: al([128, D], F32, "nrm" + n) for n in "kq"}
    oT = {n: [al([128, 8 * 128], BF16, "oT%s%d" % (n, i)) for i in range(2)] for n in "kq"}
    qTv = k.track(d_qT.t.rearrange("h p t -> p h t"), "qTv")
    kTv = k.track(d_kT.t.rearrange("h p t -> p h t"), "kTv")
    qscale = 128.0 ** -0.5
    NT = TOK // 128
    junk2 = al([128, 1024], BF16)

    def prologue(tj):
        h_t = ht[tj % 2]
        k.dma("sp", h_t[:], d_h[tj * 128:(tj + 1) * 128, :])
        rstd_of(k, junk2, ss, h_t[:], D)
        k.ts("dve", xs[:], h_t[:], ss[:], ALU.mult)
        norm_T(k, m, xs, [cols[:, 0:8], cols[:, 8:16]], [xkTs[tj % 2], xqTs[tj % 2]])

    for ti in range(NT):
        if ti == 0:
            prologue(0)
        xkT, xqT = xkTs[ti % 2], xqTs[ti % 2]
        v_t = vt[ti % 2]
        for n4 in (2, 3):
            py = m.pw()
            for kc in range(8):
                k.matmul(py[:], xkT[:, kc, :], w_kv[:, kc, n4 * 512:(n4 + 1) * 512], start=(kc == 0), stop=(kc == 7))
            k.copy("act", v_t[:, (n4 - 2) * 512:(n4 - 1) * 512], py[:])
        k.dma("sp", d_v[ti * 128:(ti + 1) * 128, :], v_t[:])
        if ti + 1 < NT:
            prologue(ti + 1)
        for which, xT, w, gcol, dview in (("k", xkT, w_kv, 16, kTv), ("q", xqT, w_q, 17, qTv)):
            pys = []
            for n4 in range(2):
                py = m.pw()
                for kc in range(8):
                    k.matmul(py[:], xT[:, kc, :], w[:, kc, n4 * 512:(n4 + 1) * 512], start=(kc == 0), stop=(kc == 7))
                pys.append(py)
                for h4 in range(4):
                    hh = n4 * 4 + h4
                    k.act(junk[:, 0:128], py[:, h4 * 128:(h4 + 1) * 128], AF.Square, accum_out=ss8[which][:, hh:hh + 1])
            s8 = ss8[which]
            k.ts("dve", s8[:], s8[:], 1.0 / 128, ALU.mult, EPS, ALU.add)
            k.act(s8[:], s8[:], AF.Ln)
            k.act(s8[:], s8[:], AF.Exp, scale=-0.5)
            nr = nrm[which]
            for n4 in range(2):
                k.tt_bc(nr[:, n4 * 512:(n4 + 1) * 512], pys[n4][:], s8[:, n4 * 4:n4 * 4 + 4], 4, ALU.mult)
            o_ = oT[which][ti % 2]
            for n4 in range(2):
                p = m.pw()
                for h4 in range(4):
                    hh = n4 * 4 + h4
                    k.transpose(p[:, h4 * 128:(h4 + 1) * 128], nr[:, hh * 128:(hh + 1) * 128], m.ident[:])
                if which == "k":
                    k.ts("dve", o_[:, n4 * 512:(n4 + 1) * 512], p[:], cols[:, gcol:gcol + 1], ALU.mult)
                else:
                    k.ts("dve", o_[:, n4 * 512:(n4 + 1) * 512], p[:], cols[:, gcol:gcol + 1], ALU.mult, qscale, ALU.mult)
            o3 = k.track(o_.t.rearrange("p (a b) -> p a b", a=8), "o3")
            o3.reads, o3.writes = o_.reads, o_.writes
            k.dma("sp", dview[:, :, ti * 128:(ti + 1) * 128], o3[:])
    return [d_v, qTv, kTv]


def host_masks():
    import ml_dtypes
    s = np.arange(128)[:, None, None]
    jj = np.arange(4)[None, :, None]
    t = np.arange(512)[None, None, :]
    mk = np.where(jj * 128 + s < t, 0.0, NEG).astype(np.float32)
    return np.ascontiguousarray(mk.astype(ml_dtypes.bfloat16))


NEG = -30000.0


def build_attn(S, NH):
    nc, k, m = new_prog()
    T = dict(qT=din(k, m, "qT", [NH, 128, S], BF16), kT=din(k, m, "kT", [NH, 128, S], BF16),
             v=din(k, m, "v", [S, NH * 128], BF16), masks=din(k, m, "masks", [128, 4, 512], BF16),
             oT=dout(k, m, "oT", [NH, 128, S], BF16))
    emit_attn(k, m, S, NH, T)
    k.final_wait("sp", [T["oT"]])
    k.emit()
    return nc


def emit_attn(k, m, S, NH, T, parity=False, sel=None):
    m.arena.reset()
    al = m.arena.alloc
    d_q, d_k, d_v, d_mk, d_o = T["qT"], T["kT"], T["v"], T["masks"], T["oT"]
    NB = S // 128
    NSB = S // 512
    NM = 8 if parity else 4
    masks = al([128, NM, 512], BF16, "masks_sb")
    k.dma("sp", masks[:], d_mk[:])
    qs = [al([128, S], BF16, "qs%d" % i) for i in range(2)]
    ks = [al([128, S], BF16, "ks%d" % i) for i in range(2)]
    vs = [al([128, NB, 128], BF16, "vs%d" % i) for i in range(2)]
    e_ = [al([128, 512], F32, "e%d" % i) for i in range(3)]
    sp_ = [al([128, 512], BF16, "sp%d" % i) for i in range(3)]
    att_ = [al([128, 512], BF16, "att%d" % i) for i in range(2)]
    fac_ = [al([128, 4], F32, "fac%d" % i) for i in range(2)]
    accs = [al([128, 512], F32, "acc%d" % i) for i in range(2)]
    tmp = al([128, 512], F32, "acctmp")
    ot_ = [al([128, 512], BF16, "ot%d" % i) for i in range(2)]
    qsel = [al([128, 512], BF16, "qsel%d" % i) for i in range(2)]
    qtmp = al([128, 512], BF16, "qtmp")
    vview = k.track(d_v.t.rearrange("(blk p) c -> p blk c", p=128), "vview")
    pds = [Buf(m.pb[7].t[:, j * 64:j * 64 + 64], "pd%d" % j) for j in range(8)]
    ring = m.pb[0:7]
    rs = {"i": 0}

    def pw7():
        b_ = ring[rs["i"] % 7]
        rs["i"] += 1
        return b_
    old_pw = m.pw
    m.pw = pw7
    tiles = []
    for h in range(NH):
        for i in range(NSB // 2 if parity else NSB):
            nkb = 4 * (2 * i + 2) if parity else 4 * (i + 1)
            for kb in range(nkb):
                tiles.append((h, i, kb, nkb))
    state = {"cur": None}

    def stage_a(t):
        h, i, kb, nkb = tiles[t]
        q_h, k_h, v_h = qs[h % 2], ks[h % 2], vs[h % 2]
        if i == 0 and kb == 0:
            k.dma("sp", q_h[:], d_q[h, :, :])
            k.dma("sp", k_h[:], d_k[h, :, :])
            step = 16 if NB >= 16 else NB
            for b0 in range(0, NB, step):
                k.dma("sp", v_h[:, b0:b0 + step, :], vview[:, b0:b0 + step, h * 128:(h + 1) * 128])
        if parity:
            qb = qsel[(h * (NSB // 2) + i) % 2]
            if kb == 0:
                k.ts("dve", qtmp[:], q_h[:, (2 * i) * 512:(2 * i + 1) * 512], sel[:, 0:1], ALU.mult)
                k.stt(qb[:], q_h[:, (2 * i + 1) * 512:(2 * i + 2) * 512], sel[:, 1:2], qtmp[:], ALU.mult, ALU.add)
            qv = qb[:]
        else:
            qv = q_h[:, i * 512:(i + 1) * 512]
        jj = kb - (nkb - NM)
        diag = jj >= 0
        kv_ = k_h[:, kb * 128:(kb + 1) * 128]
        e, sp = e_[t % 3], sp_[t % 3]
        pz = m.pw()
        k.matmul(pz[:], kv_, qv, start=True, stop=not diag)
        if diag:
            k.matmul(pz[:], m.identb[:], masks[:, jj, :], start=False, stop=True)
        k.act(e[:], pz[:], AF.Exp)
        k.act(sp[:], e[:], AF.Ln, bias=1.0)
        pd = pds[t % len(pds)]
        return (kv_, qv, diag, jj, sp, v_h, pd, pz)

    def stage_pd(t, a):
        kb = tiles[t][2]
        sp, pd = a[4], a[6]
        if kb > 0:
            for sub in range(4):
                k.matmul(pd[:, sub:sub + 1], sp[:, sub * 128:(sub + 1) * 128], m.onesb[:, 0:1])

    def stage_b(t, a, mid):
        h, i, kb, nkb = tiles[t]
        kv_, qv, diag, jj, sp, v_h, pd, pz = a
        att, fac = att_[t % 2], fac_[t % 2]
        cur = state["cur"]
        nxt = accs[0] if cur is not accs[0] else accs[1]
        pl = pz
        k.matmul(pl[:], m.ntriS[:], sp[:], start=False, stop=True)
        mid()
        if kb > 0:
            k.act(fac[:], pd[:, 0:4], AF.Exp, scale=-1.0)
            k.tt_bc(tmp[:], cur[:], fac[:, 0:4], 4, ALU.mult)
        k.act(att[:], pl[:], AF.Exp)
        pP = m.pw()
        for sub in range(4):
            k.matmul(pP[:, sub * 128:(sub + 1) * 128], att[:, sub * 128:(sub + 1) * 128], v_h[:, kb, :])
        if kb == 0:
            k.copy("dve", nxt[:], pP[:])
        else:
            k.tt("dve", nxt[:], pP[:], tmp[:], ALU.add)
        state["cur"] = nxt
        if kb == nkb - 1:
            pT = m.pw()
            for sub in range(4):
                k.transpose(pT[:, sub * 128:(sub + 1) * 128], nxt[:, sub * 128:(sub + 1) * 128], m.ident[:])
            o_ = ot_[i % 2]
            k.copy("dve", o_[:], pT[:])
            k.dma("sp", d_o[h, :, i * 512:(i + 1) * 512], o_[:])

    LA = 2
    nt = len(tiles)
    A = {}
    for t0 in range(min(LA, nt)):
        A[t0] = stage_a(t0)
    stage_pd(0, A[0])
    for t in range(nt):
        if t + LA < nt:
            A[t + LA] = stage_a(t + LA)
        if t + 1 < nt:
            stage_b(t, A[t], lambda: stage_pd(t + 1, A[t + 1]))
        else:
            stage_b(t, A[t], lambda: None)
        del A[t]
    m.pw = old_pw


def build_gdn(S, NH=4, NGRP=1):
    nc, k, m = new_prog()
    W = NH * 128
    T = dict(x=din(k, m, "x", [S, D]), an=din(k, m, "an", [8, 128]), w_qkvz=din(k, m, "w_qkvz", [D, NGRP * 4 * W]),
             w_bg=din(k, m, "w_bg", [D, NGRP * 2 * NH]), conv=din(k, m, "conv", [4, NGRP * 3 * W]),
             alog=din(k, m, "alog", [1, NGRP * NH]), dtb=din(k, m, "dtb", [1, NGRP * NH]),
             ogain=din(k, m, "ogain", [1, 128]), ogT=dout(k, m, "ogT", [NGRP * NH, 128, S], BF16))
    ov = emit_gdn(k, m, S, T, NH, NGRP)
    print("gdn arena words used", m.arena.off, "of", m.arena.n32)
    k.final_wait("sp", [ov])
    k.emit()
    return nc


def emit_gdn(k, m, S, T, NH=4, NGRP=1):
    m.arena.reset()
    al = m.arena.alloc
    W = NH * 128
    d_x, d_an, d_w, d_wbg, d_conv, d_alog, d_dtb, d_og, d_o = (T["x"], T["an"], T["w_qkvz"], T["w_bg"], T["conv"],
                                                                  T["alog"], T["dtb"], T["ogain"], T["ogT"])
    NG = 3 * NH
    cols = load_cols(k, m, [(d_an[:], 8), (d_og[:], 1)])
    convc = load_cols(k, m, [(k.track(d_conv.t.rearrange("t (j p) -> (t j) p", p=128), "cwv")[:], 4 * NG * NGRP)])
    w = al([128, 8, NGRP * 4 * W], BF16, "w")
    wbg = al([128, 8, NGRP * 2 * NH], BF16, "wbg")
    load_w(k, w, lambda kc: d_w[kc * 128:(kc + 1) * 128, :], 8)
    load_w(k, wbg, lambda kc: d_wbg[kc * 128:(kc + 1) * 128, :], 8)
    alog = al([128, NGRP * NH], F32, "alog")
    dtb = al([128, NGRP * NH], F32, "dtb")
    k.dma("sp", alog[:], k.track(d_alog.t[0, :].partition_broadcast(128), "alv")[:])
    k.dma("sp", dtb[:], k.track(d_dtb.t[0, :].partition_broadcast(128), "dtv")[:])
    nexpA = al([128, NGRP * NH], F32, "nexpA")
    k.act(nexpA[:], alog[:], AF.Exp)
    k.ts("dve", nexpA[:], nexpA[:], -1.0, ALU.mult)
    ident4 = al([128, W], F32, "ident4")
    umask4 = al([128, W], F32, "umask4")
    lvl4 = [al([128, W], BF16, "lvl4_%d" % i) for i in range(7)]
    for h in range(NH):
        sl = slice(h * 128, (h + 1) * 128)
        k.copy("dve", ident4[:, sl], m.ident[:])
        k.copy("dve", umask4[:, sl], m.umask[:])
        for i in range(7):
            k.copy("dve", lvl4[i][:, sl], m.lvl[i][:])
    junk = al([128, 1024], BF16)
    junk2 = al([128, 1024], BF16)
    ss = al([128, 1], F32)
    xt = [al([128, D], F32, "xt%d" % i) for i in range(2)]
    xs = al([128, D], F32, "xs")
    xnTs = [al([128, 8, 128], BF16, "xnT%d" % i) for i in range(2)]
    cbs = [[al([128, NH, 131], BF16, "cb%d_%d" % (G, g)) for g in range(3)] for G in range(NGRP)]
    dgw = al([128, 4 * NG * NGRP, 128], BF16, "dgw")
    for c_ in range(4 * NG * NGRP):
        k.ts("dve", dgw[:, c_, :], m.identb[:], convc[:, c_:c_ + 1], ALU.mult)
    for G in range(NGRP):
        for g in range(3):
            k.memset("dve", cbs[G][g][:], 0.0)
    F2 = lambda n, dt=F32: al([128, W], dt, n)
    Ssts = [F2("S%d" % G) for G in range(NGRP)]
    Sbs = [F2("Sb%d" % G, BF16) for G in range(NGRP)]
    for G in range(NGRP):
        k.memset("dve", Ssts[G][:], 0.0)
        k.memset("dve", Sbs[G][:], 0.0)
    NSLOT = 2 if (NGRP >= 2 and NH <= 2) else 1

    def mk_slot(si):
        F = lambda n, dt=F32: al([128, W], dt, "%s_s%d" % (n, si))
        sl = {}
        sl["sil"] = [F("sil%d" % g_) for g_ in range(3)]
        for n in ("zs", "ri", "kTf", "vtm", "ktm", "dg", "tmpF", "Fm", "FU", "Am", "u", "t1", "o_", "on", "St"):
            sl[n] = F(n)
        sl["sq2"] = F("sq2", BF16)
        for n in ("qT", "kT", "Pa", "Pb", "Xa", "Xb", "Y", "attnT", "vb", "kb", "kd", "wT", "vnew"):
            sl[n] = F(n, BF16)
        sl["L"] = [F("L%d" % i, BF16) for i in range(7)]
        sl["ogt"] = [F("ogt%d" % i, BF16) for i in range(2)]
        sl["sm"] = {n: al([128, NH], F32, "%s_s%d" % (n, si)) for n in
                    ("beta", "g", "gc", "ngc", "glast", "egc", "ekd", "eglast", "sckb", "tmp8", "ss4")}
        return sl
    slots = [mk_slot(si) for si in range(NSLOT)]
    oview = k.track(d_o.t.rearrange("h p t -> p h t"), "oview")
    qscale = float(np.log(128.0 ** -0.5))
    HS = [slice(h * 128, (h + 1) * 128) for h in range(NH)]

    def mm4(p, lhs_of, rhs_of):
        for h in range(NH):
            k.matmul(p[:, HS[h]], lhs_of(h), rhs_of(h))

    NCH = S // 128

    def prologue(cj):
        x_t = xt[cj % 2]
        k.dma("sp", x_t[:], d_x[cj * 128:(cj + 1) * 128, :])
        rstd_of(k, junk2, ss, x_t[:], D)
        k.ts("dve", xs[:], x_t[:], ss[:], ALU.mult)
        norm_T(k, m, xs, [cols[:, 0:8]], [xnTs[cj % 2]])

    for ci in range(NCH):
        if ci == 0:
            prologue(0)
        xnT = xnTs[ci % 2]

        def gbody(G, sl):
            cb, Sst, Sb = cbs[G], Ssts[G], Sbs[G]
            dtbG, nexpAG = dtb[:, G * NH:(G + 1) * NH], nexpA[:, G * NH:(G + 1) * NH]
            sil, sm, L, ogt = sl["sil"], sl["sm"], sl["L"], sl["ogt"]
            zs, sq2, ri, kTf, vtm, ktm, dg, tmpF, Fm, FU, Am = (sl[n] for n in ("zs", "sq2", "ri", "kTf", "vtm", "ktm", "dg", "tmpF", "Fm", "FU", "Am"))
            u, t1, o_, on, St = (sl[n] for n in ("u", "t1", "o_", "on", "St"))
            qT, kT, Pa, Pb, Xa, Xb, Y, attnT, vb, kb, kd, wT, vnew = (sl[n] for n in ("qT", "kT", "Pa", "Pb", "Xa", "Xb", "Y", "attnT", "vb", "kb", "kd", "wT", "vnew"))
            yield
            banks = []
            for g in range(4):
                p = m.pw()
                for h in range(NH):
                    fc = g * NH + h
                    for kc in range(8):
                        k.matmul(p[:, HS[h]], w[:, kc, G * 4 * W + fc * 128:G * 4 * W + (fc + 1) * 128], xnT[:, kc, :], start=(kc == 0), stop=(kc == 7))
                banks.append(p)
            k.act(zs[:], banks[3][:, 0:W], AF.Silu)
            for g in range(3):
                src3 = View(banks[g], banks[g].t[:, 0:W].rearrange("p (s d) -> p s d", s=NH))
                k.copy("act", cb[g][:, :, 3:131], src3)
            for g in range(3):
                pc = m.pw()
                for h in range(NH):
                    j = g * NH + h
                    for t in range(4):
                        k.matmul(pc[:, HS[h]], dgw[:, t * NG * NGRP + G * NG + j, :], cb[g][:, h, t:t + 128],
                                 start=(t == 0), stop=(t == 3))
                k.copy("act", cb[g][:, :, 0:3], cb[g][:, :, 128:131])
                k.act(sil[g][:], pc[:, 0:W], AF.Silu)
            yield
            pbg = m.pw()
            for kc in range(8):
                k.matmul(pbg[:, 0:2 * NH], xnT[:, kc, :], wbg[:, kc, G * 2 * NH:(G + 1) * 2 * NH], start=(kc == 0), stop=(kc == 7))
            k.act(sm["beta"][:], pbg[:, 0:NH], AF.Exp, scale=-1.0)
            k.ts("dve", sm["beta"][:], sm["beta"][:], 1.0, ALU.add)
            k.recip(sm["beta"][:], sm["beta"][:])
            k.tt("dve", sm["g"][:], pbg[:, NH:2 * NH], dtbG, ALU.add)
            k.act(sm["g"][:], sm["g"][:], AF.Exp)
            k.act(sm["g"][:], sm["g"][:], AF.Ln, bias=1.0)
            k.tt("dve", sm["g"][:], sm["g"][:], nexpAG, ALU.mult)
            pg = m.pw()
            k.matmul(pg[:, 0:NH], m.triI[:], sm["g"][:])
            k.matmul(pg[:, 8:8 + NH], m.ones[:], sm["g"][:])
            k.copy("dve", sm["gc"][:], pg[:, 0:NH])
            k.copy("dve", sm["glast"][:], pg[:, 8:8 + NH])
            k.ts("dve", sm["ngc"][:], sm["gc"][:], -1.0, ALU.mult)
            k.act(sm["egc"][:], sm["gc"][:], AF.Exp)
            k.act(sm["eglast"][:], sm["glast"][:], AF.Exp)
            k.tt("dve", sm["tmp8"][:], sm["glast"][:], sm["gc"][:], ALU.subtract)
            k.act(sm["ekd"][:], sm["tmp8"][:], AF.Exp)
            k.tt("dve", sm["sckb"][:], sm["beta"][:], sm["egc"][:], ALU.mult)
            yield
            for g in range(2):
                k.act(sq2[:], sil[g][:], AF.Square)
                ps = m.pw()
                k.matmul(ps[:, 0:W], m.onesb[:], sq2[:])
                k.act(ri[:], ps[:, 0:W], AF.Ln, bias=EPS)
                if g == 0:
                    k.act(ri[:], ri[:], AF.Exp, scale=-0.5, bias=qscale)
                    k.tt("dve", qT[:], sil[0][:], ri[:], ALU.mult)
                else:
                    k.act(ri[:], ri[:], AF.Exp, scale=-0.5)
                    k.tt("dve", kTf[:], sil[1][:], ri[:], ALU.mult)
                    k.copy("act", kT[:], kTf[:])
            pt = m.pw()
            for h in range(NH):
                k.transpose(pt[:, HS[h]], kTf[:, HS[h]], m.ident[:])
            k.copy("dve", ktm[:], pt[:, 0:W])
            pt = m.pw()
            for h in range(NH):
                k.transpose(pt[:, HS[h]], sil[2][:, HS[h]], m.ident[:])
            k.copy("dve", vtm[:], pt[:, 0:W])
            yield
            k.tt_bc(dg[:], ident4[:], sm["gc"][:, 0:NH], NH, ALU.mult)
            pR = m.pw()
            k.matmul(pR[:, 0:W], m.ones[:], dg[:])
            for h in range(NH):
                k.act(tmpF[:, HS[h]], pR[:, HS[h]], AF.Abs, bias=sm["ngc"][:, h:h + 1])
            k.act(Fm[:], tmpF[:], AF.Exp, scale=-1.0)
            k.tt("dve", FU[:], Fm[:], umask4[:], ALU.mult)
            pKK = m.pw()
            mm4(pKK, lambda h: kT[:, HS[h]], lambda h: kT[:, HS[h]])
            for h in range(NH):
                k.stt(Am[:, HS[h]], pKK[:, HS[h]], sm["beta"][:, h:h + 1], Fm[:, HS[h]], ALU.mult, ALU.mult)
            pQK = m.pw()
            mm4(pQK, lambda h: kT[:, HS[h]], lambda h: qT[:, HS[h]])
            k.tt("dve", attnT[:], pQK[:, 0:W], FU[:], ALU.mult)
            for i in range(7):
                k.tt("dve", L[i][:], Am[:], lvl4[i][:], ALU.mult)
            yield
            pY = m.pw()
            mm4(pY, lambda h: L[0][:, HS[h]], lambda h: m.identb[:])
            Pc, Pn, Xc, Xn = Pa, Pb, Xa, Xb
            k.stt(Pc[:], pY[:, 0:W], -1.0, ident4[:], ALU.mult, ALU.add)
            k.tt("dve", Xc[:], ident4[:], L[0][:], ALU.subtract)
            for i in range(1, 7):
                pY = m.pw()
                mm4(pY, lambda h: L[i][:, HS[h]], lambda h: Pc[:, HS[h]])
                k.copy("act", Y[:], pY[:, 0:W])
                pZ = m.pw()
                mm4(pZ, lambda h: Xc[:, HS[h]], lambda h: Y[:, HS[h]])
                if i < 6:
                    pZT = m.pw()
                    mm4(pZT, lambda h: Y[:, HS[h]], lambda h: Xc[:, HS[h]])
                k.stt(Pn[:], pZ[:, 0:W], -1.0, Pc[:], ALU.mult, ALU.add)
                Pc, Pn = Pn, Pc
                if i < 6:
                    k.stt(Xn[:], pZT[:, 0:W], -1.0, Xc[:], ALU.mult, ALU.add)
                    Xc, Xn = Xn, Xc
                yield
            yield
            k.tt_bc(vb[:], vtm[:], sm["beta"][:, 0:NH], NH, ALU.mult)
            k.tt_bc(kb[:], ktm[:], sm["sckb"][:, 0:NH], NH, ALU.mult)
            k.tt_bc(kd[:], ktm[:], sm["ekd"][:, 0:NH], NH, ALU.mult)
            pu = m.pw()
            mm4(pu, lambda h: Pc[:, HS[h]], lambda h: vb[:, HS[h]])
            k.copy("dve", u[:], pu[:, 0:W])
            pw_ = m.pw()
            mm4(pw_, lambda h: kb[:, HS[h]], lambda h: Pc[:, HS[h]])
            k.copy("dve", wT[:], pw_[:, 0:W])
            yield
            pws = m.pw()
            mm4(pws, lambda h: wT[:, HS[h]], lambda h: Sb[:, HS[h]])
            k.stt(vnew[:], pws[:, 0:W], -1.0, u[:], ALU.mult, ALU.add)
            po1 = m.pw()
            mm4(po1, lambda h: qT[:, HS[h]], lambda h: Sb[:, HS[h]])
            po2 = m.pw()
            mm4(po2, lambda h: attnT[:, HS[h]], lambda h: vnew[:, HS[h]])
            k.tt_bc(t1[:], po1[:, 0:W], sm["egc"][:, 0:NH], NH, ALU.mult)
            k.tt("dve", o_[:], po2[:, 0:W], t1[:], ALU.add)
            pS = m.pw()
            mm4(pS, lambda h: kd[:, HS[h]], lambda h: vnew[:, HS[h]])
            k.tt_bc(St[:], Sst[:], sm["eglast"][:, 0:NH], NH, ALU.mult)
            k.tt("dve", Sst[:], pS[:, 0:W], St[:], ALU.add)
            k.copy("act", Sb[:], Sst[:])
            yield
            for h in range(NH):
                k.act(junk[:, 0:128], o_[:, HS[h]], AF.Square, accum_out=sm["ss4"][:, h:h + 1])
            k.ts("dve", sm["ss4"][:], sm["ss4"][:], 1.0 / 128, ALU.mult, EPS, ALU.add)
            k.act(sm["ss4"][:], sm["ss4"][:], AF.Ln)
            k.act(sm["ss4"][:], sm["ss4"][:], AF.Exp, scale=-0.5)
            k.tt_bc(on[:], o_[:], sm["ss4"][:, 0:NH], NH, ALU.mult)
            pT = m.pw()
            for h in range(NH):
                k.transpose(pT[:, HS[h]], on[:, HS[h]], m.ident[:])
            og = ogt[(ci * NGRP + G) % 2]
            k.stt(og[:], pT[:, 0:W], cols[:, 8:9], zs[:], ALU.mult, ALU.mult)
            og3 = k.track(og.t.rearrange("p (a b) -> p a b", a=NH), "og3")
            og3.reads, og3.writes = og.reads, og.writes
            k.dma("sp", oview[:, G * NH:(G + 1) * NH, ci * 128:(ci + 1) * 128], og3[:])

        for G0 in range(0, NGRP, NSLOT):
            gens = [gbody(G0 + si, slots[si]) for si in range(NSLOT)]
            alive = list(gens)
            step = 0
            while alive:
                for gen_ in list(alive):
                    try:
                        next(gen_)
                    except StopIteration:
                        alive.remove(gen_)
                step += 1
                if G0 + NSLOT >= NGRP and step == 5 and ci + 1 < NCH:
                    prologue(ci + 1)
    return oview


GDN_HPG = 4


def build_fused(S):
    nc, k, m = new_prog()
    NPOS = S // 1024
    TOKO = NPOS * 512
    I = lambda n, shp, dt=F32: din(k, m, n, shp, dt)
    x = I("x", [S, D])
    Tg = dict(x=x, an=I("an0", [8, 128]), w_qkvz=I("w_qkvz", [D, 4096]), w_bg=I("w_bg", [D, 16]),
              conv=I("conv", [4, 3072]), alog=I("alog", [1, 8]), dtb=I("dtb", [1, 8]), ogain=I("ogain", [1, 128]))
    ogT = k.dram("ogT_scr", [8, 128, S], BF16)
    Tg["ogT"] = ogT
    h2 = k.dram("h2_scr", [S, D], F32)
    Tf0 = dict(x=x, aT=ogT, w_o=I("gdn_w_out", [D, D]), fnorm=I("fn0", [8, 128]), w_gu=I("w_gu0", [D, 2 * DFF]),
               w_d=I("w_d0", [DFF, D]), h=h2)
    qT = k.dram("qT_scr", [8, 128, S], BF16)
    kT = k.dram("kT_scr", [8, 128, S], BF16)
    v = k.dram("v_scr", [S, D], BF16)
    Tk = dict(h=h2, kvn=I("kvn", [8, 128]), an=I("an1", [8, 128]), kg=I("kg", [1, 128]), qg=I("qg", [1, 128]),
              w_kv=I("w_kv", [D, 2 * D]), w_q=I("w_q", [D, D]), qT=qT, kT=kT, v=v)
    oTs = k.dram("oT_scr", [8, 128, TOKO], BF16)
    Ta = dict(qT=qT, kT=kT, v=v, masks=I("masks", [128, 8, 512], BF16), oT=oTs)
    d_sel = I("sel", [128, 2])
    out = dout(k, m, "out", [TOKO, D])
    Tf1 = dict(x=None, aT=oTs, w_o=I("sb_w_out", [D, D]), fnorm=I("fn1", [8, 128]), w_gu=I("w_gu1", [D, 2 * DFF]),
               w_d=I("w_d1", [DFF, D]), h=out)
    sel = k.sbuf([128, 2], F32, "sel_sb")
    k.dma("sp", sel[:], d_sel[:])

    emit_gdn(k, m, S, Tg, GDN_HPG, 8 // GDN_HPG)
    barrier(k)
    emit_dense_ffn(k, m, S, Tf0)
    barrier(k)
    emit_kvq(k, m, S, Tk)
    barrier(k)
    emit_attn(k, m, S, 8, Ta, parity=True, sel=sel)
    barrier(k)

    def xload(ti, x_t, xtmp):
        j, r = ti // 4, ti % 4
        ra = (2 * j) * 512 + r * 128
        rb = (2 * j + 1) * 512 + r * 128
        k.dma("sp", xtmp[0][:], h2[ra:ra + 128, :])
        k.dma("sp", xtmp[1][:], h2[rb:rb + 128, :])
        k.ts("dve", xtmp[0][:], xtmp[0][:], sel[:, 0:1], ALU.mult)
        k.stt(x_t[:], xtmp[1][:], sel[:, 1:2], xtmp[0][:], ALU.mult, ALU.add)
    emit_dense_ffn(k, m, TOKO, Tf1, xload=xload)
    k.final_wait("sp", [out])
    k.emit()
    return nc


def host_parity_masks(p):
    c = host_masks()
    z = np.zeros_like(c)
    f = np.full_like(c, NEG)
    return _c(np.concatenate([z, c], axis=1) if p == 1 else np.concatenate([c, f], axis=1))


def kernel_fused(inp, x):
    B, S, _ = x.shape
    cores = list(range(2 * B))
    cf, cb = host_consts()
    w_in = inp["gdn_w_in"][0]
    cw = inp["gdn_conv_w"][0]

    def grp(a, width, hpg=GDN_HPG):
        nblk = a.shape[1] // (8 * width)
        parts = []
        for G in range(8 // hpg):
            for blk in range(nblk):
                parts.append(a[:, blk * 8 * width + G * hpg * width: blk * 8 * width + (G + 1) * hpg * width])
        return _c(np.concatenate(parts, axis=1))
    common = dict(cf=cf, cb=cb, an0=_c(inp["attn_norm"][0].reshape(8, 128)), w_qkvz=grp(w_in[:, 0:4096], 128),
                  w_bg=grp(w_in[:, 4096:4112], 1), conv=grp(cw, 128), alog=_c(inp["gdn_a_log"].reshape(1, 8)),
                  dtb=_c(inp["gdn_dt_bias"].reshape(1, 8)), ogain=_c(inp["gdn_o_gain"].reshape(1, 128)),
                  gdn_w_out=_c(inp["gdn_w_out"][0]), fn0=_c(inp["ffn_norm"][0].reshape(8, 128)),
                  w_gu0=_c(inp["ffn_w_gu"][0]), w_d0=_c(inp["ffn_w_down"][0]),
                  kvn=_c(inp["kv_norm"].reshape(8, 128)), an1=_c(inp["attn_norm"][1].reshape(8, 128)),
                  kg=_c(inp["k_gain"].reshape(1, 128)), qg=_c(inp["sb_q_gain"].reshape(1, 128)),
                  w_kv=_c(inp["w_kv"]), w_q=_c(inp["sb_w_q"][0]), sb_w_out=_c(inp["sb_w_out"][0]),
                  fn1=_c(inp["ffn_norm"][1].reshape(8, 128)), w_gu1=_c(inp["ffn_w_gu"][1]), w_d1=_c(inp["ffn_w_down"][1]))
    feeds = []
    for c in cores:
        b, p = c // 2, c % 2
        selv = np.empty((128, 2), np.float32)
        selv[:, 0] = 1.0 - p
        selv[:, 1] = float(p)
        feeds.append(dict(common, x=_c(x[b]), masks=host_parity_masks(p), sel=selv))
    res = run_bass_kernel_spmd(_prog("fused", build_fused, S), feeds, core_ids=cores).results
    out = np.empty((B, S, D), np.float32)
    for c in cores:
        b, p = c // 2, c % 2
        o = res[c]["out"]
        for j in range(S // 1024):
            out[b, (2 * j + p) * 512:(2 * j + p + 1) * 512] = o[j * 512:(j + 1) * 512]
    return out

from concourse.bass_utils import run_bass_kernel_spmd

_PROGS = {}


def _prog(name, fn, *args):
    key = (name,) + args
    if key not in _PROGS:
        _PROGS[key] = fn(*args)
    return _PROGS[key]


def _c(a):
    return np.ascontiguousarray(a)


def kernel_unfused(**inputs):
    inp = {n: np.asarray(v) for n, v in inputs.items()}
    x = inp["x"].astype(np.float32, copy=False)
    B, S, _ = x.shape
    NC = 8
    HALF = S // 2
    cores = list(range(NC))
    cf, cb = host_consts()
    cst = {"cf": cf, "cb": cb}

    w_in = inp["gdn_w_in"][0]
    cw = inp["gdn_conv_w"][0]
    feeds = []
    for c in cores:
        b, hg = c // 2, c % 2
        hs = slice(hg * 512, (hg + 1) * 512)
        h4 = slice(hg * 4, (hg + 1) * 4)
        feeds.append(dict(cst, x=_c(x[b]), an=_c(inp["attn_norm"][0].reshape(8, 128)),
                          w_qkvz=_c(np.concatenate([w_in[:, 0:1024][:, hs], w_in[:, 1024:2048][:, hs],
                                                    w_in[:, 2048:3072][:, hs], w_in[:, 3072:4096][:, hs]], axis=1)),
                          w_bg=_c(np.concatenate([w_in[:, 4096:4104][:, h4], w_in[:, 4104:4112][:, h4]], axis=1)),
                          conv=_c(np.concatenate([cw[:, 0:1024][:, hs], cw[:, 1024:2048][:, hs], cw[:, 2048:3072][:, hs]], axis=1)),
                          alog=_c(inp["gdn_a_log"][:, h4]), dtb=_c(inp["gdn_dt_bias"][:, h4]),
                          ogain=_c(inp["gdn_o_gain"].reshape(1, 128))))
    r1 = run_bass_kernel_spmd(_prog("gdn", build_gdn, S, 4), feeds, core_ids=cores).results
    ogT = [np.concatenate([r1[2 * b]["ogT"], r1[2 * b + 1]["ogT"]], axis=0) for b in range(B)]

    nc_ffn = _prog("ffn", build_dense_ffn, HALF)
    feeds = []
    for c in cores:
        b, hf = c // 2, c % 2
        ts = slice(hf * HALF, (hf + 1) * HALF)
        feeds.append(dict(cst, x=_c(x[b, ts]), aT=_c(ogT[b][:, :, ts]), w_o=_c(inp["gdn_w_out"][0]),
                          fnorm=_c(inp["ffn_norm"][0].reshape(8, 128)), w_gu=_c(inp["ffn_w_gu"][0]),
                          w_d=_c(inp["ffn_w_down"][0])))
    r2 = run_bass_kernel_spmd(nc_ffn, feeds, core_ids=cores).results
    h2 = [r["h"] for r in r2]

    feeds = []
    for c in cores:
        feeds.append(dict(cst, h=_c(h2[c]), kvn=_c(inp["kv_norm"].reshape(8, 128)),
                          an=_c(inp["attn_norm"][1].reshape(8, 128)), kg=_c(inp["k_gain"].reshape(1, 128)),
                          qg=_c(inp["sb_q_gain"].reshape(1, 128)), w_kv=_c(inp["w_kv"]), w_q=_c(inp["sb_w_q"][0])))
    r3 = run_bass_kernel_spmd(_prog("kvq", build_kvq, HALF), feeds, core_ids=cores).results

    mk = host_masks()
    feeds = []
    for c in cores:
        b, hg = c // 2, c % 2
        h4 = slice(hg * 4, (hg + 1) * 4)
        qT = np.concatenate([r3[2 * b]["qT"][h4], r3[2 * b + 1]["qT"][h4]], axis=2)
        kT = np.concatenate([r3[2 * b]["kT"][h4], r3[2 * b + 1]["kT"][h4]], axis=2)
        v = np.concatenate([r3[2 * b]["v"], r3[2 * b + 1]["v"]], axis=0)[:, hg * 512:(hg + 1) * 512]
        feeds.append(dict(cst, qT=_c(qT), kT=_c(kT), v=_c(v), masks=mk))
    r4 = run_bass_kernel_spmd(_prog("attn", build_attn, S, 4), feeds, core_ids=cores).results
    oT = [np.concatenate([r4[2 * b]["oT"], r4[2 * b + 1]["oT"]], axis=0) for b in range(B)]

    feeds = []
    for c in cores:
        b, hf = c // 2, c % 2
        ts = slice(hf * HALF, (hf + 1) * HALF)
        feeds.append(dict(cst, x=_c(h2[c]), aT=_c(oT[b][:, :, ts]), w_o=_c(inp["sb_w_out"][0]),
                          fnorm=_c(inp["ffn_norm"][1].reshape(8, 128)), w_gu=_c(inp["ffn_w_gu"][1]),
                          w_d=_c(inp["ffn_w_down"][1])))
    r5 = run_bass_kernel_spmd(_prog("ffn_b", build_dense_ffn, HALF), feeds, core_ids=cores).results
    out = np.empty((B, S, D), np.float32)
    for c in cores:
        b, hf = c // 2, c % 2
        out[b, hf * HALF:(hf + 1) * HALF] = r5[c]["h"]
    return out


def kernel(**inputs):
    inp = {n: np.asarray(v) for n, v in inputs.items()}
    x = inp["x"].astype(np.float32, copy=False)
    return kernel_fused(inp, x)
```
